# Optimizing a Trainium2 kernel written in Bass

```python
import jax, jax.numpy as jnp
from jax import lax
import numpy as np

D_MODEL = 2048
BATCH = 2
SEQ = 8192
DEPTH = 4

HEAD_DIM = 64
N_Q_HEADS = 16
N_KV_HEADS = 4
ATTN_WIDTH = N_Q_HEADS * HEAD_DIM
KV_WIDTH = N_KV_HEADS * HEAD_DIM
WINDOW = 128
ATTN_BLOCK = 128
ROPE_THETA = 10000.0

GLA_HEADS = 4
GLA_V_WIDTH = D_MODEL - ATTN_WIDTH
GLA_K_WIDTH = GLA_V_WIDTH // 2
GLA_DK = GLA_K_WIDTH // GLA_HEADS
GLA_DV = GLA_V_WIDTH // GLA_HEADS
GLA_GATE_RANK = 16
GLA_TAU = 16.0
GLA_CHUNK = 64

MIX_SPLITS = (ATTN_WIDTH, KV_WIDTH, KV_WIDTH, GLA_K_WIDTH, GLA_K_WIDTH, GLA_V_WIDTH, GLA_V_WIDTH, GLA_GATE_RANK)
MIX_IN_WIDTH = ATTN_WIDTH + 2 * KV_WIDTH + 2 * GLA_K_WIDTH + 2 * GLA_V_WIDTH + GLA_GATE_RANK

D_FF = 5504
NORM_EPS = 1e-6
N_SANDWICH_NORMS = 6

kernel_name = 'hybrid_swa_sink_gla_macaron_sandwich'


def rmsnorm(x, gain):
    xf = x.astype(jnp.float32)
    y = xf * lax.rsqrt(jnp.mean(xf * xf, axis=-1, keepdims=True) + NORM_EPS)
    return (y * gain.astype(jnp.float32)).astype(x.dtype)


def rope(x, positions):
    d = x.shape[-1]
    inv_freq = ROPE_THETA ** (-jnp.arange(0, d, 2, dtype=jnp.float32) / d)
    ang = positions.astype(jnp.float32)[..., None] * inv_freq
    cos = jnp.cos(ang)[:, :, None, :]
    sin = jnp.sin(ang)[:, :, None, :]
    xf = x.astype(jnp.float32)
    x1, x2 = xf[..., : d // 2], xf[..., d // 2:]
    return jnp.concatenate([x1 * cos - x2 * sin, x2 * cos + x1 * sin], axis=-1).astype(x.dtype)


def sliding_window_attention(q, k, v, sinks):
    B, S = q.shape[0], q.shape[1]
    nb = S // ATTN_BLOCK
    G = N_Q_HEADS // N_KV_HEADS
    qb = q.reshape(B, nb, ATTN_BLOCK, N_KV_HEADS, G, HEAD_DIM)

    def with_prev(t):
        tb = t.reshape(B, nb, ATTN_BLOCK, N_KV_HEADS, HEAD_DIM)
        prev = jnp.pad(tb[:, :-1], ((0, 0), (1, 0), (0, 0), (0, 0), (0, 0)))
        return jnp.concatenate([prev, tb], axis=2)

    kb, vb = with_prev(k), with_prev(v)
    scores = jnp.einsum('bnqhgd,bnkhd->bnhgqk', qb, kb,
                        preferred_element_type=jnp.float32) * (HEAD_DIM ** -0.5)
    qi = jnp.arange(ATTN_BLOCK)[:, None]
    kj = jnp.arange(2 * ATTN_BLOCK)[None, :]
    diff = qi + ATTN_BLOCK - kj
    band = (diff >= 0) & (diff < WINDOW)
    key_valid = (jnp.arange(nb)[:, None] > 0) | (kj >= ATTN_BLOCK)
    mask = band[None] & key_valid[:, None, :]
    scores = jnp.where(mask[None, :, None, None], scores, -jnp.inf)
    sink = sinks.astype(jnp.float32).reshape(N_KV_HEADS, G)[None, None, :, :, None]
    m = jnp.maximum(scores.max(axis=-1), sink)
    p = jnp.exp(scores - m[..., None])
    denom = p.sum(axis=-1) + jnp.exp(sink - m)
    p = p / denom[..., None]
    out = jnp.einsum('bnhgqk,bnkhd->bnqhgd', p, vb.astype(jnp.float32))
    return out.reshape(B, S, ATTN_WIDTH).astype(q.dtype)


def gated_linear_attention(q, k, v, log_a):
    B, S = q.shape[0], q.shape[1]
    N, C = S // GLA_CHUNK, GLA_CHUNK
    q = q.astype(jnp.float32).reshape(B, N, C, GLA_HEADS, GLA_DK) * (GLA_DK ** -0.5)
    k = k.astype(jnp.float32).reshape(B, N, C, GLA_HEADS, GLA_DK)
    v = v.astype(jnp.float32).reshape(B, N, C, GLA_HEADS, GLA_DV)
    b = jnp.cumsum(log_a.astype(jnp.float32).reshape(B, N, C, GLA_HEADS, GLA_DK), axis=2)
    b_last = b[:, :, -1:]
    q_dec = q * jnp.exp(b)
    k_dec = k * jnp.exp(-b)
    causal = jnp.tril(jnp.ones((C, C), dtype=bool))
    intra = jnp.einsum('bnthk,bnshk->bnhts', q_dec, k_dec)
    intra = jnp.where(causal, intra, 0.0)
    o_intra = jnp.einsum('bnhts,bnshv->bnthv', intra, v)
    dS = jnp.einsum('bnshk,bnshv->bnhkv', k * jnp.exp(b_last - b), v)
    chunk_decay = jnp.exp(b_last[:, :, 0])

    def step(state, inp):
        dec, ds = inp
        return dec[..., None] * state + ds, state

    init = jnp.zeros((B, GLA_HEADS, GLA_DK, GLA_DV), jnp.float32)
    _, s_before = lax.scan(step, init, (jnp.moveaxis(chunk_decay, 1, 0), jnp.moveaxis(dS, 1, 0)))
    s_before = jnp.moveaxis(s_before, 0, 1)
    o_inter = jnp.einsum('bnthk,bnhkv->bnthv', q_dec, s_before)
    return (o_intra + o_inter).reshape(B, S, GLA_HEADS, GLA_DV)


def hybrid_mixer(h, positions, w_in, sinks, gate_w2, gate_b, gla_norm_gain, w_out):
    B, S = h.shape[0], h.shape[1]
    split_at = tuple(int(i) for i in np.cumsum(MIX_SPLITS)[:-1])
    q_a, k_a, v_a, q_g, k_g, v_g, r_g, g_lr = jnp.split(h @ w_in, split_at, axis=-1)
    q_a = rope(q_a.reshape(B, S, N_Q_HEADS, HEAD_DIM), positions)
    k_a = rope(k_a.reshape(B, S, N_KV_HEADS, HEAD_DIM), positions)
    v_a = v_a.reshape(B, S, N_KV_HEADS, HEAD_DIM)
    attn_out = sliding_window_attention(q_a, k_a, v_a, sinks)
    log_a = jax.nn.log_sigmoid((g_lr @ gate_w2 + gate_b).astype(jnp.float32)) / GLA_TAU
    o = gated_linear_attention(q_g.reshape(B, S, GLA_HEADS, GLA_DK),
                               k_g.reshape(B, S, GLA_HEADS, GLA_DK),
                               v_g.reshape(B, S, GLA_HEADS, GLA_DV),
                               log_a.reshape(B, S, GLA_HEADS, GLA_DK))
    o = rmsnorm(o, gla_norm_gain).reshape(B, S, GLA_V_WIDTH)
    gla_out = (o * jax.nn.silu(r_g.astype(jnp.float32))).astype(h.dtype)
    return jnp.concatenate([attn_out.astype(h.dtype), gla_out], axis=-1) @ w_out


def swiglu(h, w_in, w_out):
    gate, up = jnp.split(h @ w_in, 2, axis=-1)
    return (jax.nn.silu(gate) * up) @ w_out


def setup_inputs(seed: int = 0) -> dict:
    key = jax.random.key(seed)
    ks = jax.random.split(key, 11)
    f32 = jnp.float32
    x = jax.random.normal(ks[0], (BATCH, SEQ, D_MODEL), f32)
    positions = jnp.broadcast_to(jnp.arange(SEQ, dtype=jnp.int32), (BATCH, SEQ))
    norm_gains = 1.0 + 0.05 * jax.random.normal(ks[1], (DEPTH, N_SANDWICH_NORMS, D_MODEL), f32)
    ffn_w_in = jax.random.normal(ks[2], (DEPTH, 2, D_MODEL, 2 * D_FF), f32) * D_MODEL ** -0.5
    ffn_w_out = jax.random.normal(ks[3], (DEPTH, 2, D_FF, D_MODEL), f32) * D_FF ** -0.5
    w_mix_in = jax.random.normal(ks[4], (DEPTH, D_MODEL, MIX_IN_WIDTH), f32) * D_MODEL ** -0.5
    attn_sinks = jax.random.normal(ks[5], (DEPTH, N_Q_HEADS), f32)
    gla_gate_w2 = jax.random.normal(ks[6], (DEPTH, GLA_GATE_RANK, GLA_K_WIDTH), f32) * GLA_GATE_RANK ** -0.5
    gla_gate_b = 0.1 * jax.random.normal(ks[7], (DEPTH, GLA_K_WIDTH), f32)
    gla_norm_gain = 1.0 + 0.05 * jax.random.normal(ks[8], (DEPTH, GLA_DV), f32)
    w_mix_out = jax.random.normal(ks[9], (DEPTH, D_MODEL, D_MODEL), f32) * D_MODEL ** -0.5
    return {'x': x, 'positions': positions, 'norm_gains': norm_gains, 'ffn_w_in': ffn_w_in,
            'ffn_w_out': ffn_w_out, 'w_mix_in': w_mix_in, 'attn_sinks': attn_sinks,
            'gla_gate_w2': gla_gate_w2, 'gla_gate_b': gla_gate_b, 'gla_norm_gain': gla_norm_gain,
            'w_mix_out': w_mix_out}


def reference(x, positions, norm_gains, ffn_w_in, ffn_w_out, w_mix_in, attn_sinks,
              gla_gate_w2, gla_gate_b, gla_norm_gain, w_mix_out):
    for l in range(DEPTH):
        g = norm_gains[l]
        h = swiglu(rmsnorm(x, g[0]), ffn_w_in[l, 0], ffn_w_out[l, 0])
        x = x + 0.5 * rmsnorm(h, g[1])
        h = hybrid_mixer(rmsnorm(x, g[2]), positions, w_mix_in[l], attn_sinks[l],
                         gla_gate_w2[l], gla_gate_b[l], gla_norm_gain[l], w_mix_out[l])
        x = x + rmsnorm(h, g[3])
        h = swiglu(rmsnorm(x, g[4]), ffn_w_in[l, 1], ffn_w_out[l, 1])
        x = x + 0.5 * rmsnorm(h, g[5])
    return x
```

```python
import numpy as np
from contextlib import ExitStack
import concourse.bass as bass
import concourse.mybir as mybir
from concourse.bass_utils import run_bass_kernel_spmd

F32 = mybir.dt.float32
BF16 = mybir.dt.bfloat16
I32 = mybir.dt.int32
AF = mybir.ActivationFunctionType
ALU = mybir.AluOpType
AX = mybir.AxisListType

ENGS = ("pe", "act", "dve", "pool", "sp")


class Res:
    __slots__ = ("name", "ws", "rs", "prs")

    def __init__(self, name):
        self.name = name
        self.ws = []
        self.rs = []
        self.prs = []


class Op:
    __slots__ = ("eng", "fn", "deps", "marked", "val", "dma", "key", "idx", "inc")

    def __init__(self, eng, fn, dma, key, idx, inc=16):
        self.inc = inc
        self.eng = eng
        self.fn = fn
        self.deps = []
        self.marked = False
        self.val = None
        self.dma = dma
        self.key = key
        self.idx = idx


class KB:
    def __init__(self, nc):
        self.nc = nc
        self.ops = {e: [] for e in ENGS}
        self.stack = ExitStack()
        self.nres = 0

    def sbuf(self, name, shape, dtype):
        return self.stack.enter_context(self.nc.sbuf_tensor(name, list(shape), dtype))

    def psum(self, name, shape, dtype):
        return self.stack.enter_context(self.nc.psum_tensor(name, list(shape), dtype))

    def res(self, name=None):
        self.nres += 1
        return Res(name or f"r{self.nres}")

    def add(self, eng, fn, reads=(), writes=(), dma=False, key=None, inc=16, par=False):
        ops = self.ops[eng]
        if dma and key is None:
            key = "dma_" + writes[0].name
        op = Op(eng, fn, dma, key, len(ops), inc)
        deps = []
        for r in reads:
            for o in r.ws:
                deps.append((o, "raw"))
        for w in writes:
            if w.rs:
                for o in w.rs:
                    deps.append((o, "war"))
            elif par:
                for o in w.prs:
                    deps.append((o, "war"))
            else:
                for o in w.ws:
                    deps.append((o, "waw"))
        for d, kind in deps:
            if d is op:
                continue
            if d.eng == eng and not d.dma and not dma:
                if eng == "pe":
                    continue
                if kind != "raw" or (op.idx - d.idx) > 2:
                    continue
            if d.eng == eng and d.dma and dma and False:
                continue
            d.marked = True
            op.deps.append(d)
        for r in reads:
            r.rs.append(op)
        for w in writes:
            if w.rs:
                w.prs = w.rs
                w.rs = []
                w.ws = [op]
            elif par:
                w.ws.append(op)
            else:
                w.ws = [op]
        ops.append(op)
        return op

    def emit(self):
        nc = self.nc
        keys = {}
        for e in ENGS:
            cnt = 0
            for op in self.ops[e]:
                if op.dma:
                    keys[op.key] = keys.get(op.key, 0) + op.inc
                    op.val = keys[op.key]
                elif op.marked:
                    cnt += 1
                    op.val = cnt
            self.nmarks = getattr(self, "nmarks", {})
            self.nmarks[e] = cnt
        EPOCH = 20000
        nep = {e: self.nmarks[e] // EPOCH + 1 for e in ENGS}
        semnames = [f"{e}{i}" for e in ENGS for i in range(nep[e])] + sorted(keys.keys())
        sems = {}
        for n in semnames:
            sems[n] = self.stack.enter_context(nc.semaphore("s_" + n))
        self.nsems = len(semnames)

        def run(engname, engobj):
            seen = {}
            for op in self.ops[engname]:
                need = {}
                for d in op.deps:
                    k = d.key if d.dma else d.eng
                    if d.val > need.get(k, 0):
                        need[k] = d.val
                for k, v in need.items():
                    if seen.get(k, 0) >= v:
                        continue
                    seen[k] = v
                    if k in ENGS:
                        ep, vv = (v - 1) // EPOCH, (v - 1) % EPOCH + 1
                        engobj.wait_ge(sems[f"{k}{ep}"], vv)
                    else:
                        engobj.wait_ge(sems[k], v)
                ins = op.fn(engobj)
                if op.dma:
                    ins.then_inc(sems[op.key], op.inc)
                elif op.marked:
                    ins.then_inc(sems[f"{engname}{(op.val - 1) // EPOCH}"], 1)

        with nc.Block() as block:
            @block.tensor
            def _(e):
                run("pe", e)

            @block.scalar
            def _(e):
                run("act", e)

            @block.vector
            def _(e):
                run("dve", e)

            @block.gpsimd
            def _(e):
                run("pool", e)

            @block.sync
            def _(e):
                run("sp", e)

    def close(self):
        self.stack.close()


NPROJ = 41
P_QA, P_KA, P_VA, P_QG, P_KG, P_VG, P_RG, P_GLR = 0, 8, 12, 16, 20, 24, 32, 40
O_QR, O_KT, O_V2, O_TMPB, O_GKV, O_PN, O_PT = 0, 16384, 25088, 34816, 36864, 40960, 43008
KTW = 2176
O_TMP2, O_AM, O_SQP, O_SB = 40960, 43008, 43264, 43776
S_DEC, S_DTOT, S_ATT = 0, 128, 136
TWO_PI_HI = 6.28125
TWO_PI_LO = 6.283185307179586 - 6.28125
MAGIC = 12582912.0


def _proj_cols(m):
    if m < 8:
        return [(m * 128, 128)]
    if m < 12:
        g = m - 8
        return [(1024 + g * 64, 64)] * 2
    if m < 16:
        g = m - 12
        return [(1280 + g * 64, 64)] * 2
    if m < 20:
        return [(1536 + (m - 16) * 128, 128)]
    if m < 24:
        return [(2048 + (m - 20) * 128, 128)]
    if m < 32:
        return [(2560 + (m - 24) * 128, 128)]
    if m < 40:
        return [(3584 + (m - 32) * 128, 128)]
    return [(4608, 16)]


class MixerMixin:
    def mixer_init(self):
        kb, nc = self.kb, self.nc
        dscr = lambda name, shape, dt=F32: nc.dram_tensor(name, list(shape), dt).ap()
        self.PROJ = dscr("PROJ", [NPROJ, 128, T])
        self.r_PROJ = kb.res("PROJ")
        self.COSD = dscr("COSD", [128, T]); self.SIND = dscr("SIND", [128, T])
        self.r_ROPE = kb.res("ROPE")
        self.KVsrc = dscr("KVsrc", [128, 1024], BF16); self.KVdst = dscr("KVdst", [512, 1024], BF16)
        self.Ssrc = dscr("Ssrc", [128, 1028]); self.Sdst = dscr("Sdst", [512, 1028])
        self.r_KVsrc, self.r_KVdst, self.r_Ssrc, self.r_Sdst = [kb.res(n) for n in ("KVsrc", "KVdst", "Ssrc", "Sdst")]
        self.CORE = kb.sbuf("CORE", [128, 520], F32); self.r_CORE = kb.res("CORE")
        self.SINK = kb.sbuf("SINK", [128, 16], F32); self.r_SINK = kb.res("SINK")
        self.SINKP = kb.sbuf("SINKP", [128, 16], F32); self.r_SINKP = kb.res("SINKP")
        self.GBN = kb.sbuf("GBN", [128, 6 * self.depth], F32); self.r_GBN = kb.res("GBN")
        self.SMALL = kb.sbuf("SMALL", [128, 256], F32); self.r_SMALL = kb.res("SMALL")
        self.r_att = [kb.res("att0"), kb.res("att1")]
        self.r_att2 = [kb.res("att20"), kb.res("att21")]
        kb.add("sp", lambda e: e.dma_start(out=self.CORE[:], in_=self.core_d), writes=[self.r_CORE], dma=True)
        kb.add("sp", lambda e: e.dma_start(out=self.GBN[:, 0:4 * self.depth], in_=self.gb_d), writes=[self.r_GBN], dma=True, key="gbn")
        kb.add("sp", lambda e: e.dma_start(out=self.GBN[:, 4 * self.depth:6 * self.depth], in_=self.gn_d), writes=[self.r_GBN], dma=True, key="gbn", par=True)

    def bigf(self, lo, n):
        return self.BIGF[:, lo:lo + n]

    def bigres(self, lo, n):
        out = []
        for blk in range(lo // 2048, (lo + n - 1) // 2048 + 1):
            for h in range(2):
                a = blk * 2048 + h * 1024
                if a < lo + n and a + 1024 > lo:
                    out.append(self.r_BIG[blk][h])
        return out

    def Fres(self, i):
        return [self.r_F[i][0], self.r_F[i][1]]

    def phase_rope_tables(self):
        kb = self.kb
        A, B_, C_, Pi = self.F[0], self.F[1], self.F[2], self.F[3]
        rA, rB, rC, rP = self.Fres(0), self.Fres(1), self.Fres(2), self.Fres(3)
        invf = self.cst(self.C_INVF, 1); sign = self.cst(self.C_SIGN, 1)
        kb.add("sp", lambda e: e.dma_start(out=Pi[:].bitcast(I32), in_=self.pos.partition_broadcast(128)), writes=rP, dma=True, key="posld")
        kb.add("dve", lambda e: e.tensor_copy(out=A[:], in_=Pi[:].bitcast(I32)), reads=rP, writes=rA)
        kb.add("dve", lambda e: e.tensor_scalar(out=A[:], in0=A[:], scalar1=invf, scalar2=0.0, op0=ALU.mult, op1=ALU.add), reads=rA + [self.r_CST], writes=rA)

        def reduce_sin(src, rsrc, dst, rdst, tmp, rtmp):
            kb.add("dve", lambda e: e.tensor_scalar(out=tmp[:], in0=src[:], scalar1=float(1.0 / (2 * np.pi)), scalar2=MAGIC, op0=ALU.mult, op1=ALU.add), reads=rsrc, writes=rtmp)
            kb.add("dve", lambda e: e.tensor_scalar(out=tmp[:], in0=tmp[:], scalar1=-MAGIC, scalar2=1.0, op0=ALU.add, op1=ALU.mult), reads=rtmp, writes=rtmp)
            kb.add("dve", lambda e: e.scalar_tensor_tensor(out=dst[:], in0=tmp[:], scalar=-TWO_PI_HI, in1=src[:], op0=ALU.mult, op1=ALU.add), reads=rtmp + rsrc, writes=rdst)
            kb.add("dve", lambda e: e.scalar_tensor_tensor(out=dst[:], in0=tmp[:], scalar=-TWO_PI_LO, in1=dst[:], op0=ALU.mult, op1=ALU.add), reads=rtmp + rdst, writes=rdst)
            kb.add("dve", lambda e: e.tensor_scalar(out=dst[:], in0=dst[:], scalar1=3.1415925, scalar2=-3.1415925, op0=ALU.min, op1=ALU.max), reads=rdst, writes=rdst)

        reduce_sin(A, rA, B_, rB, C_, rC)
        kb.add("act", lambda e: e.activation(out=C_[:], in_=B_[:], func=AF.Sin), reads=rB, writes=rC)
        kb.add("dve", lambda e: e.tensor_scalar(out=C_[:], in0=C_[:], scalar1=sign, scalar2=0.0, op0=ALU.mult, op1=ALU.add), reads=rC + [self.r_CST], writes=rC)
        kb.add("sp", lambda e: e.dma_start(out=self.SIND, in_=C_[:]), reads=rC, writes=[self.r_ROPE], dma=True, key="rope")
        kb.add("dve", lambda e: e.tensor_scalar(out=A[:], in0=B_[:], scalar1=float(np.pi / 2), scalar2=0.0, op0=ALU.add, op1=ALU.add), reads=rB, writes=rA)
        reduce_sin(A, rA, B_, rB, Pi, rP)
        kb.add("act", lambda e: e.activation(out=A[:], in_=B_[:], func=AF.Sin), reads=rB, writes=rA)
        kb.add("sp", lambda e: e.dma_start(out=self.COSD, in_=A[:]), reads=rA, writes=[self.r_ROPE], dma=True, key="rope", par=True)

    def phase_inproj(self, l):
        kb = self.kb
        rstd = self.F[0]
        unit = 0

        def ld(m):
            runs = _proj_cols(m)
            s = self.wslot
            self.wslot = (s + 1) % 4
            Wt = self.W[s]
            c = 0
            for ri, (c0, n) in enumerate(runs):
                kb.add("pool", lambda e, c=c, c0=c0, n=n: e.dma_start(out=Wt[:, :, c:c + n], in_=self.w_mi[l][:, c0:c0 + n].rearrange("(kc p) m -> p kc m", p=128)),
                       writes=[self.r_W[s]], dma=True, par=(ri > 0))
                c += n
            return Wt, self.r_W[s], c
        import os
        mlist = [int(v) for v in os.environ.get("MLIST", "").split(",") if v] or list(range(NPROJ))
        nxtw = ld(mlist[0])
        for mi, m in enumerate(mlist):
            Wt, rW, ncol = nxtw
            if mi + 1 < len(mlist):
                nxtw = ld(mlist[mi + 1])
            for h in range(2):
                base = (unit % 4) * 2
                unit += 1
                sl = slice(h * 1024, (h + 1) * 1024)
                for n in range(2):
                    bank = base + n
                    tsl = slice(h * 1024 + n * 512, h * 1024 + (n + 1) * 512)
                    for k in range(NB):
                        kb.add("pe", lambda e, Wt=Wt, k=k, tsl=tsl, bank=bank, ncol=ncol: e.matmul(self.PS[0:ncol, bank * 512:(bank + 1) * 512], Wt[:, k, 0:ncol], self.HT[:, k, tsl], start=(k == 0), stop=(k == NB - 1)),
                               reads=[rW, self.r_HT[k][h]], writes=[self.r_PS[bank]])
                st, rst = self.next_stg()
                kb.add("dve", lambda e, st=st, base=base, sl=sl, ncol=ncol: e.tensor_tensor(out=st[0:ncol, :], in0=self.PS[0:ncol, base * 512:(base + 2) * 512], in1=rstd[0:ncol, sl], op=ALU.mult),
                       reads=[self.r_PS[base], self.r_PS[base + 1], self.r_F[0][h]], writes=[rst])
                kb.add("sp", lambda e, st=st, m=m, sl=sl, ncol=ncol: e.dma_start(out=self.PROJ[m][0:ncol, sl], in_=st[0:ncol, :]), reads=[rst], writes=[self.r_PROJ], dma=True, par=True)

    def rope_block(self, m, dst_lo, COS, SIN, rT):
        kb = self.kb
        X, Tm = self.F[2], self.F[3]
        rX, rTm = self.Fres(2), self.Fres(3)
        kb.add("sp", lambda e: e.dma_start(out=X[:], in_=self.PROJ[m]), reads=[self.r_PROJ], writes=rX, dma=True, key="ropeX")
        for n in range(4):
            kb.add("pe", lambda e, n=n: e.matmul(self.PS[:, n * 512:(n + 1) * 512], self.cst(self.C_PERM, 128), X[:, n * 512:(n + 1) * 512], start=True, stop=True),
                   reads=rX + [self.r_CST], writes=[self.r_PS[n]])
        kb.add("dve", lambda e: e.tensor_tensor(out=Tm[:], in0=self.PS[:, 0:2048], in1=SIN[:], op=ALU.mult), reads=[self.r_PS[i] for i in range(4)] + rT, writes=rTm)
        kb.add("pool", lambda e: e.tensor_tensor(out=X[:], in0=X[:], in1=COS[:], op=ALU.mult), reads=rX + rT, writes=rX)
        kb.add("dve", lambda e: e.tensor_tensor(out=self.bigf(dst_lo, T), in0=X[:], in1=Tm[:], op=ALU.add), reads=rX + rTm, writes=self.bigres(dst_lo, T))

    def phase_attn(self, l):
        kb = self.kb
        COS, SIN = self.F[0], self.F[1]
        rT = self.Fres(0) + self.Fres(1)
        kb.add("sp", lambda e: e.dma_start(out=COS[:], in_=self.COSD), reads=[self.r_ROPE], writes=self.Fres(0), dma=True, key="ldcos")
        kb.add("sp", lambda e: e.dma_start(out=SIN[:], in_=self.SIND), reads=[self.r_ROPE], writes=self.Fres(1), dma=True, key="ldsin")
        kb.add("sp", lambda e: e.dma_start(out=self.SINK[:], in_=self.sinks_d[l:l + 1, :].partition_broadcast(128)), writes=[self.r_SINK], dma=True)
        kb.add("dve", lambda e: e.tensor_copy(out=self.SINKP[:].rearrange("p (g b a) -> p g b a", b=2, a=2), in_=self.SINK[:].rearrange("p (g a b) -> p g b a", a=2, b=2)), reads=[self.r_SINK], writes=[self.r_SINKP])
        for g in range(4):
            self.rope_block(P_KA + g, O_KT + g * KTW + 128, COS, SIN, rT)
        import os
        ATT = int(os.environ.get("ATT", "9"))
        if ATT < 2:
            return
        X = self.F[2]; rX = self.Fres(2)
        for g in range(4):
            kb.add("sp", lambda e, g=g: e.dma_start(out=X[:], in_=self.PROJ[P_VA + g]), reads=[self.r_PROJ], writes=rX, dma=True, key="ropeX")
            kb.add("act", lambda e: e.activation(out=self.bigf(O_TMPB, T), in_=X[:], func=AF.Copy), reads=rX, writes=self.bigres(O_TMPB, T))
            for half in range(2):
                bank = 4 + half
                pst = self.PS[:, bank * 512:(bank + 1) * 512].bitcast(BF16)
                for tt in range(8):
                    t = half * 8 + tt
                    kb.add("pe", lambda e, t=t, tt=tt, pst=pst: e.transpose(out=pst[:, tt * 128:(tt + 1) * 128], in_=self.bigf(O_TMPB + t * 128, 128), identity=self.IDB[:]),
                           reads=self.bigres(O_TMPB + t * 128, 128) + [self.r_cb], writes=[self.r_PS[bank]])
                lo = O_V2 + (1 + half * 8) * 512 + g * 128
                dst = self.BIGF[:, lo:lo + 8 * 512].rearrange("p (t c) -> p t c", c=512)[:, :, 0:128]
                kb.add("act", lambda e, dst=dst, pst=pst: e.activation(out=dst, in_=pst.rearrange("p (t c) -> p t c", c=128), func=AF.Copy),
                       reads=[self.r_PS[bank]], writes=self.bigres(O_V2 + (1 + half * 8) * 512, 8 * 512))
        if ATT < 3:
            return
        for b in range(8):
            self.rope_block(P_QA + b, O_QR + b * T, COS, SIN, rT)
        if ATT < 4:
            return
        for g in range(4):
            kb.add("sp", lambda e, g=g: e.dma_start(out=self.KVsrc[:, g * 128:(g + 1) * 128], in_=self.bigf(O_KT + g * KTW + T, 128)),
                   reads=self.bigres(O_KT + g * KTW + T, 128), writes=[self.r_KVsrc], dma=True, par=(g > 0))
        kb.add("sp", lambda e: e.dma_start(out=self.KVsrc[:, 512:1024], in_=self.bigf(O_V2 + 16 * 512, 512)), reads=self.bigres(O_V2 + 16 * 512, 512), writes=[self.r_KVsrc], dma=True, par=True)
        kb.add("pool", lambda e: e.collective_compute("AllGather", ALU.bypass, replica_groups=[[0, 1, 2, 3], [4, 5, 6, 7]], ins=[self.KVsrc], outs=[self.KVdst]),
               reads=[self.r_KVsrc], writes=[self.r_KVdst], dma=True, key="cc1", inc=1)
        G = self.bigf(O_GKV, 4096).rearrange("p (r c) -> p r c", c=1024)
        rG = self.bigres(O_GKV, 4096)
        kb.add("sp", lambda e: e.dma_start(out=G, in_=self.KVdst.rearrange("(r p) c -> p r c", p=128)), reads=[self.r_KVdst], writes=rG, dma=True, key="gkv")
        if ATT < 5:
            return
        sel = lambda r: self.CORE[:, 512 + r:513 + r]
        for g in range(4):
            dK = self.bigf(O_KT + g * KTW, 128)
            rK = self.bigres(O_KT + g * KTW, 128)
            for r in range(4):
                src = G[:, r, g * 128:(g + 1) * 128]
                if r == 0:
                    kb.add("dve", lambda e, dK=dK, src=src, r=r: e.tensor_scalar(out=dK, in0=src, scalar1=sel(r), scalar2=0.0, op0=ALU.mult, op1=ALU.add), reads=rG + [self.r_CORE], writes=rK)
                else:
                    kb.add("dve", lambda e, dK=dK, src=src, r=r: e.scalar_tensor_tensor(out=dK, in0=src, scalar=sel(r), in1=dK, op0=ALU.mult, op1=ALU.add), reads=rG + rK + [self.r_CORE], writes=rK)
        dV = self.bigf(O_V2, 512); rV = self.bigres(O_V2, 512)
        for r in range(4):
            src = G[:, r, 512:1024]
            if r == 0:
                kb.add("dve", lambda e, src=src, r=r: e.tensor_scalar(out=dV, in0=src, scalar1=sel(r), scalar2=0.0, op0=ALU.mult, op1=ALU.add), reads=rG + [self.r_CORE], writes=rV)
            else:
                kb.add("dve", lambda e, src=src, r=r: e.scalar_tensor_tensor(out=dV, in0=src, scalar=sel(r), in1=dV, op0=ALU.mult, op1=ALU.add), reads=rG + rV + [self.r_CORE], writes=rV)
        if ATT < 6:
            return
        unit = 0
        NUN = int(os.environ.get("UNITS", "64"))
        for g in range(4):
            for t in range(16):
                if unit >= NUN:
                    break
                u = unit % 2
                unit += 1
                bS, bT, bO = u * 4, u * 4 + 2, u * 4 + 3
                S = self.PS[:, bS * 512:(bS + 2) * 512]
                rS = [self.r_PS[bS], self.r_PS[bS + 1]]
                for j in range(4):
                    blk, hf = 2 * g + j // 2, j % 2
                    psl = slice(hf * 64, hf * 64 + 64)
                    qlo = O_QR + blk * T + t * 128
                    klo = O_KT + g * KTW + t * 128
                    pj = (j % 2) * 2 + j // 2
                    kb.add("pe", lambda e, pj=pj, psl=psl, qlo=qlo, klo=klo, S=S: e.matmul(S[:, pj * 256:(pj + 1) * 256], self.BIGF[psl, qlo:qlo + 128], self.BIGF[psl, klo:klo + 256], start=True, stop=True),
                           reads=self.bigres(qlo, 128) + self.bigres(klo, 256), writes=[rS[pj // 2]])
                CORE_ = int(os.environ.get("CORE", "9"))
                if CORE_ < 2:
                    continue
                sm, rsm = self.next_stg()
                mask = self.CORE[:, 256:512] if t == 0 else self.CORE[:, 0:256]
                st0 = S_ATT + u * 32
                sml = lambda a, st0=st0: self.SMALL[:, st0 + a * 4:st0 + a * 4 + 4]
                rsml = self.r_att[u]
                for j in range(4):
                    kb.add("dve", lambda e, j=j, sm=sm, S=S, mask=mask: e.scalar_tensor_tensor(out=sm[:, j * 256:(j + 1) * 256], in0=S[:, j * 256:(j + 1) * 256], scalar=0.125, in1=mask, op0=ALU.mult, op1=ALU.add),
                           reads=rS + [self.r_CORE], writes=[rsm])
                sm3 = sm.rearrange("p (j k) -> p j k", k=256)
                kb.add("dve", lambda e, sm3=sm3, sml=sml: e.tensor_reduce(out=sml(0), in_=sm3, axis=AX.X, op=ALU.max), reads=[rsm], writes=[rsml])
                kb.add("dve", lambda e, sml=sml, g=g: e.tensor_tensor(out=sml(0), in0=sml(0), in1=self.SINKP[:, 4 * g:4 * g + 4], op=ALU.max), reads=[rsml, self.r_SINKP], writes=[rsml])
                kb.add("dve", lambda e, sml=sml: e.tensor_scalar(out=sml(1), in0=sml(0), scalar1=-1.0, scalar2=0.0, op0=ALU.mult, op1=ALU.add), reads=[rsml], writes=[rsml])
                if CORE_ < 3:
                    continue
                kb.add("dve", lambda e, sml=sml: e.memset(sml(2), 0.0), writes=[self.r_att2[u]])
                for j in range(4):
                    kb.add("act", lambda e, j=j, sm=sm, sml=sml: e.activation(out=sm[:, j * 256:(j + 1) * 256], in_=sm[:, j * 256:(j + 1) * 256], func=AF.Exp, bias=sml(1)[:, j:j + 1], accum_out=sml(2)[:, j:j + 1]),
                           reads=[rsm, rsml], writes=[rsm, self.r_att2[u]])
                kb.add("dve", lambda e, sml=sml, g=g: e.tensor_tensor(out=sml(3), in0=sml(1), in1=self.SINKP[:, 4 * g:4 * g + 4], op=ALU.add), reads=[rsml, self.r_SINKP], writes=[rsml])
                kb.add("act", lambda e, sml=sml: e.activation(out=sml(3), in_=sml(3), func=AF.Exp), reads=[rsml], writes=[rsml])
                kb.add("dve", lambda e, sml=sml: e.tensor_tensor(out=sml(4), in0=sml(2), in1=sml(3), op=ALU.add), reads=[rsml, self.r_att2[u]], writes=[rsml])
                kb.add("dve", lambda e, sml=sml: e.reciprocal(out=sml(4), in_=sml(4)), reads=[rsml], writes=[rsml])
                if CORE_ < 4:
                    continue
                pn = self.bigf(O_PN + u * 1024, 1024); rpn = self.bigres(O_PN + u * 1024, 1024)
                for j in range(4):
                    kb.add("dve", lambda e, j=j, pn=pn, sm=sm, sml=sml: e.tensor_scalar(out=pn[:, j * 256:(j + 1) * 256], in0=sm[:, j * 256:(j + 1) * 256], scalar1=sml(4)[:, j:j + 1], scalar2=0.0, op0=ALU.mult, op1=ALU.add),
                           reads=[rsm, rsml], writes=rpn)
                if CORE_ < 5:
                    continue
                pst = self.PS[:, bT * 512:(bT + 1) * 512].bitcast(BF16)
                for jc in range(8):
                    kb.add("pe", lambda e, jc=jc, pst=pst, pn=pn: e.transpose(out=pst[:, jc * 128:(jc + 1) * 128], in_=pn[:, jc * 128:(jc + 1) * 128], identity=self.IDB[:]),
                           reads=rpn + [self.r_cb], writes=[self.r_PS[bT]])
                pT = self.bigf(O_PT + u * 1024, 1024); rpT = self.bigres(O_PT + u * 1024, 1024)
                kb.add("act", lambda e, pT=pT, pst=pst: e.activation(out=pT, in_=pst, func=AF.Copy), reads=[self.r_PS[bT]], writes=rpT)
                if CORE_ < 6:
                    continue
                O = self.PS[:, bO * 512:(bO + 1) * 512]
                for j in range(4):
                    for c in range(2):
                        vlo = O_V2 + (t + c) * 512 + g * 128
                        kb.add("pe", lambda e, j=j, c=c, vlo=vlo, O=O, pT=pT: e.matmul(O[:, j * 128:(j + 1) * 128], self.bigf(vlo, 128), pT[:, (2 * j + c) * 128:(2 * j + c + 1) * 128], start=(c == 0), stop=(c == 1)),
                               reads=self.bigres(vlo, 128) + rpT, writes=[self.r_PS[bO]])
                if CORE_ < 7:
                    continue
                for pj in range(4):
                    j = (pj % 2) * 2 + pj // 2
                    blk, hf = 2 * g + j // 2, j % 2
                    psl = slice(hf * 64, hf * 64 + 64)
                    kb.add("act", lambda e, j=pj, blk=blk, psl=psl, O=O, t=t: e.activation(out=self.HT[psl, blk, t * 128:(t + 1) * 128], in_=O[psl, j * 128:(j + 1) * 128], func=AF.Copy),
                           reads=[self.r_PS[bO]], writes=[self.r_HT[blk][t // 8]])

    def phase_gla_prep(self, l):
        kb = self.kb
        Fc, Fe, Fq, Fs = self.F[0], self.F[1], self.F[2], self.F[3]
        rc, re, rq, rs_ = self.Fres(0), self.Fres(1), self.Fres(2), self.Fres(3)
        Wg, rWg = self.W[0], self.r_W[0]
        Ww, rWw = self.W[1], self.r_W[1]
        glr = Wg[:].rearrange("p a b -> p (a b)")
        w2 = Ww[:].rearrange("p a b -> p (a b)")
        kb.add("pool", lambda e: e.dma_start(out=glr[0:16, :], in_=self.PROJ[P_GLR][0:16, :]), reads=[self.r_PROJ], writes=[rWg], dma=True)
        kb.add("pool", lambda e: e.dma_start(out=w2[0:16, 0:512], in_=self.w2_d[l]), writes=[rWw], dma=True)
        kb.add("dve", lambda e: e.memset(Fs[:, 1024:2048], 0.0), writes=[self.r_F[3][1]])
        dec = lambda h: self.SMALL[:, S_DEC + h * 32:S_DEC + (h + 1) * 32]
        for h in range(4):
            base = h * 5 * T
            QD, KD, KRT, VT = base, base + T, base + 2 * T, base + 3 * T
            for n in range(4):
                kb.add("pe", lambda e, n=n, h=h: e.matmul(self.PS[:, n * 512:(n + 1) * 512], w2[0:16, h * 128:(h + 1) * 128], glr[0:16, n * 512:(n + 1) * 512], start=True, stop=True),
                       reads=[rWg, rWw], writes=[self.r_PS[n]])
            gb = self.GBN[:, l * 4 + h:l * 4 + h + 1]
            kb.add("act", lambda e, gb=gb: e.activation(out=Fe[:], in_=self.PS[:, 0:2048], func=AF.Identity, bias=gb, scale=1.0), reads=[self.r_PS[i] for i in range(4)] + [self.r_GBN], writes=re)
            kb.add("act", lambda e: e.activation(out=Fe[:], in_=Fe[:], func=AF.Exp, scale=-1.0), reads=re, writes=re)
            kb.add("act", lambda e: e.activation(out=Fe[:], in_=Fe[:], func=AF.Ln, bias=1.0), reads=re, writes=re)
            for n in range(32):
                kb.add("dve", lambda e, n=n: e.tensor_tensor_scan(out=Fc[:, n * 64:(n + 1) * 64], data0=self.cst(self.C_ONES, 64), data1=Fe[:, n * 64:(n + 1) * 64], initial=0.0, op0=ALU.mult, op1=ALU.add),
                       reads=re + [self.r_CST], writes=[self.r_F[0][n // 16]])
            c3 = Fc[:].rearrange("p (n t) -> p n t", t=64)
            kb.add("act", lambda e, h=h: e.activation(out=dec(h), in_=c3[:, :, 63], func=AF.Exp, scale=-1.0 / 16), reads=rc, writes=[self.r_SMALL])
            kb.add("dve", lambda e, h=h: e.tensor_reduce(out=self.SMALL[:, S_DTOT + h:S_DTOT + h + 1], in_=c3[:, :, 63], axis=AX.X, op=ALU.add), reads=rc, writes=[self.r_SMALL])
            kb.add("act", lambda e, h=h: e.activation(out=self.SMALL[:, S_DTOT + h:S_DTOT + h + 1], in_=self.SMALL[:, S_DTOT + h:S_DTOT + h + 1], func=AF.Exp, scale=-1.0 / 16), reads=[self.r_SMALL], writes=[self.r_SMALL])
            kb.add("sp", lambda e, h=h: e.dma_start(out=Fq[:], in_=self.PROJ[P_QG + h]), reads=[self.r_PROJ], writes=rq, dma=True, key="glaQ")
            kb.add("act", lambda e: e.activation(out=Fe[:], in_=Fc[:], func=AF.Exp, scale=-1.0 / 16), reads=rc, writes=re)
            kb.add("dve", lambda e, QD=QD: e.scalar_tensor_tensor(out=self.bigf(QD, T), in0=Fq[:], scalar=float(128 ** -0.5), in1=Fe[:], op0=ALU.mult, op1=ALU.mult), reads=rq + re, writes=self.bigres(QD, T))
            kb.add("sp", lambda e, h=h: e.dma_start(out=Fq[:], in_=self.PROJ[P_KG + h]), reads=[self.r_PROJ], writes=rq, dma=True, key="glaQ")
            kb.add("act", lambda e: e.activation(out=Fe[:], in_=Fc[:], func=AF.Exp, scale=1.0 / 16), reads=rc, writes=re)
            kb.add("dve", lambda e, KD=KD: e.tensor_tensor(out=self.bigf(KD, T), in0=Fq[:], in1=Fe[:], op=ALU.mult), reads=rq + re, writes=self.bigres(KD, T))
            kb.add("dve", lambda e: e.tensor_tensor(out=c3, in0=c3, in1=c3[:, :, 63:64].to_broadcast([128, 32, 64]), op=ALU.subtract), reads=rc, writes=rc)
            kb.add("act", lambda e: e.activation(out=Fe[:], in_=Fc[:], func=AF.Exp, scale=1.0 / 16), reads=rc, writes=re)
            kb.add("dve", lambda e: e.tensor_tensor(out=self.bigf(O_TMP2, T), in0=Fq[:], in1=Fe[:], op=ALU.mult), reads=rq + re, writes=self.bigres(O_TMP2, T))
            for half in range(2):
                bank = 4 + half
                pst = self.PS[:, bank * 512:(bank + 1) * 512].bitcast(BF16)
                for tt in range(8):
                    t = half * 8 + tt
                    kb.add("pe", lambda e, t=t, tt=tt, pst=pst: e.transpose(out=pst[:, tt * 128:(tt + 1) * 128], in_=self.bigf(O_TMP2 + t * 128, 128), identity=self.IDB[:]),
                           reads=self.bigres(O_TMP2 + t * 128, 128) + [self.r_cb], writes=[self.r_PS[bank]])
                kb.add("act", lambda e, pst=pst, KRT=KRT, half=half: e.activation(out=self.bigf(KRT + half * 1024, 1024), in_=pst, func=AF.Copy), reads=[self.r_PS[bank]], writes=self.bigres(KRT + half * 1024, 1024))
            for e2 in range(2):
                kb.add("sp", lambda e, h=h, e2=e2: e.dma_start(out=Fq[:], in_=self.PROJ[P_VG + 2 * h + e2]), reads=[self.r_PROJ], writes=rq, dma=True, key="glaQ")
                kb.add("act", lambda e: e.activation(out=self.bigf(O_TMP2, T), in_=Fq[:], func=AF.Copy), reads=rq, writes=self.bigres(O_TMP2, T))
                for half in range(2):
                    bank = 6 + half
                    pst = self.PS[:, bank * 512:(bank + 1) * 512].bitcast(BF16)
                    for tt in range(8):
                        t = half * 8 + tt
                        kb.add("pe", lambda e, t=t, tt=tt, pst=pst: e.transpose(out=pst[:, tt * 128:(tt + 1) * 128], in_=self.bigf(O_TMP2 + t * 128, 128), identity=self.IDB[:]),
                               reads=self.bigres(O_TMP2 + t * 128, 128) + [self.r_cb], writes=[self.r_PS[bank]])
                    lo = VT + half * 8 * 256 + e2 * 128
                    dst = self.BIGF[:, lo:lo + 8 * 256].rearrange("p (t c) -> p t c", c=256)[:, :, 0:128]
                    kb.add("act", lambda e, dst=dst, pst=pst: e.activation(out=dst, in_=pst.rearrange("p (t c) -> p t c", c=128), func=AF.Copy),
                           reads=[self.r_PS[bank]], writes=self.bigres(VT + half * 8 * 256, 8 * 256))
            Sm = Fs[:, 1024 + h * 256:1024 + (h + 1) * 256]
            for n in range(32):
                tile, par = n // 2, n % 2
                psl = slice(par * 64, par * 64 + 64)
                bank = 2 + (n % 2)
                dS = self.PS[:, bank * 512:bank * 512 + 256]
                kb.add("pe", lambda e, tile=tile, psl=psl, dS=dS, KRT=KRT, VT=VT: e.matmul(dS, self.BIGF[psl, KRT + tile * 128:KRT + tile * 128 + 128], self.BIGF[psl, VT + tile * 256:VT + tile * 256 + 256], start=True, stop=True),
                       reads=self.bigres(KRT + tile * 128, 128) + self.bigres(VT + tile * 256, 256), writes=[self.r_PS[bank]])
                kb.add("dve", lambda e, Sm=Sm, dS=dS, n=n, h=h: e.scalar_tensor_tensor(out=Sm, in0=Sm, scalar=dec(h)[:, n:n + 1], in1=dS, op0=ALU.mult, op1=ALU.add),
                       reads=[self.r_PS[bank], self.r_SMALL, self.r_F[3][1]], writes=[self.r_F[3][1]])
        kb.add("sp", lambda e: e.dma_start(out=self.Ssrc[:, 0:1024], in_=Fs[:, 1024:2048]), reads=[self.r_F[3][1]], writes=[self.r_Ssrc], dma=True)
        kb.add("sp", lambda e: e.dma_start(out=self.Ssrc[:, 1024:1028], in_=self.SMALL[:, S_DTOT:S_DTOT + 4]), reads=[self.r_SMALL], writes=[self.r_Ssrc], dma=True, par=True)
        kb.add("pool", lambda e: e.collective_compute("AllGather", ALU.bypass, replica_groups=[[0, 1, 2, 3], [4, 5, 6, 7]], ins=[self.Ssrc], outs=[self.Sdst]),
               reads=[self.r_Ssrc], writes=[self.r_Sdst], dma=True, key="cc2", inc=1)

    def phase_gla_out(self, l):
        kb = self.kb
        F0, F1, F2, F3 = self.F
        GA = F0[:].rearrange("p (r c) -> p r c", c=1024)
        GB2 = F1[:, 0:1024]
        GD = F1[:, 1024:1036]
        rG = self.Fres(0) + self.Fres(1)
        kb.add("sp", lambda e: e.dma_start(out=GA, in_=self.Sdst[0:256, 0:1024].rearrange("(r p) c -> p r c", p=128)), reads=[self.r_Sdst], writes=self.Fres(0), dma=True, key="gS0")
        kb.add("sp", lambda e: e.dma_start(out=GB2, in_=self.Sdst[256:384, 0:1024]), reads=[self.r_Sdst], writes=[self.r_F[1][0]], dma=True, key="gS1")
        kb.add("sp", lambda e: e.dma_start(out=GD.rearrange("p (r c) -> p r c", c=4), in_=self.Sdst[0:384, 1024:1028].rearrange("(r p) c -> p r c", p=128)), reads=[self.r_Sdst], writes=[self.r_F[1][1]], dma=True, key="gS2")
        Sin = F3[:, 1024:2048]
        rSin = [self.r_F[3][1]]
        Tt = F3[:, 0:1024]
        rTt = [self.r_F[3][0]]
        lt = lambda r: self.CORE[:, 516 + r:517 + r]
        kb.add("dve", lambda e: e.memset(Sin, 0.0), writes=rSin)
        for r in range(3):
            Sr = GA[:, r, :] if r < 2 else GB2
            for h in range(4):
                hs = slice(h * 256, (h + 1) * 256)
                kb.add("dve", lambda e, r=r, h=h, hs=hs, Sr=Sr: e.scalar_tensor_tensor(out=Tt[:, hs], in0=Sin[:, hs], scalar=GD[:, r * 4 + h:r * 4 + h + 1], in1=Sr[:, hs], op0=ALU.mult, op1=ALU.add),
                       reads=rG + rSin, writes=rTt)
            kb.add("dve", lambda e: e.tensor_tensor(out=Tt, in0=Tt, in1=Sin, op=ALU.subtract), reads=rTt + rSin, writes=rTt)
            kb.add("dve", lambda e, r=r: e.scalar_tensor_tensor(out=Sin, in0=Tt, scalar=lt(r), in1=Sin, op0=ALU.mult, op1=ALU.add), reads=rTt + rSin + [self.r_CORE], writes=rSin)
        dec = lambda h: self.SMALL[:, S_DEC + h * 32:S_DEC + (h + 1) * 32]
        gmask = self.cst(self.C_GMASK, 128)
        for h in range(4):
            base = h * 5 * T
            QD, KD, KRT, VT = base, base + T, base + 2 * T, base + 3 * T
            Sm = Sin[:, h * 256:(h + 1) * 256]
            Sb = self.bigf(O_SB + h * 256, 256); rSb = self.bigres(O_SB + h * 256, 256)
            O_h = [F0, F1]
            rO = [self.Fres(0), self.Fres(1)]
            kb.add("act", lambda e, Sb=Sb, Sm=Sm: e.activation(out=Sb, in_=Sm, func=AF.Copy), reads=rSin, writes=rSb)
            for m in range(16):
                u = m % 2
                bI, bOo, bD = 0 + u, 2 + u, 4 + u
                tok = slice(m * 128, (m + 1) * 128)
                I_ps = self.PS[:, bI * 512:bI * 512 + 128]
                kb.add("pe", lambda e, I_ps=I_ps, m=m, KD=KD, QD=QD: e.matmul(I_ps, self.bigf(KD + m * 128, 128), self.bigf(QD + m * 128, 128), start=True, stop=True),
                       reads=self.bigres(KD + m * 128, 128) + self.bigres(QD + m * 128, 128), writes=[self.r_PS[bI]])
                AM = self.bigf(O_AM + u * 128, 128); rAM = self.bigres(O_AM + u * 128, 128)
                kb.add("dve", lambda e, AM=AM, I_ps=I_ps: e.tensor_tensor(out=AM, in0=I_ps, in1=gmask, op=ALU.mult), reads=[self.r_PS[bI], self.r_CST], writes=rAM)
                Ops = self.PS[:, bOo * 512:bOo * 512 + 256]
                for par in range(2):
                    n = 2 * m + par
                    psl = slice(par * 64, par * 64 + 64)
                    for e2 in range(2):
                        oc = Ops[:, e2 * 128 + par * 64:e2 * 128 + (par + 1) * 64]
                        kb.add("pe", lambda e, oc=oc, e2=e2, par=par, AM=AM, m=m, VT=VT: e.matmul(oc, self.bigf(VT + m * 256 + e2 * 128, 128), AM[:, par * 64:(par + 1) * 64], start=True, stop=False),
                               reads=self.bigres(VT + m * 256, 256) + rAM, writes=[self.r_PS[bOo]])
                        kb.add("pe", lambda e, oc=oc, e2=e2, par=par, Sb=Sb, m=m, QD=QD: e.matmul(oc, Sb[:, e2 * 128:(e2 + 1) * 128], self.bigf(QD + m * 128 + par * 64, 64), start=False, stop=True),
                               reads=rSb + self.bigres(QD + m * 128, 128), writes=[self.r_PS[bOo]])
                    bD = 4 + par
                    dS = self.PS[:, bD * 512:bD * 512 + 256]
                    kb.add("pe", lambda e, psl=psl, dS=dS, m=m, KRT=KRT, VT=VT: e.matmul(dS, self.BIGF[psl, KRT + m * 128:KRT + m * 128 + 128], self.BIGF[psl, VT + m * 256:VT + m * 256 + 256], start=True, stop=True),
                           reads=self.bigres(KRT + m * 128, 128) + self.bigres(VT + m * 256, 256), writes=[self.r_PS[bD]])
                    kb.add("dve", lambda e, Sm=Sm, dS=dS, n=n, h=h: e.scalar_tensor_tensor(out=Sm, in0=Sm, scalar=dec(h)[:, n:n + 1], in1=dS, op0=ALU.mult, op1=ALU.add),
                           reads=[self.r_PS[bD], self.r_SMALL] + rSin, writes=rSin)
                    kb.add("act", lambda e, Sb=Sb, Sm=Sm: e.activation(out=Sb, in_=Sm, func=AF.Copy), reads=rSin, writes=rSb)
                SQ = self.bigf(O_SQP + u * 256, 256); rSQ = self.bigres(O_SQP + u * 256, 256)
                for e2 in range(2):
                    kb.add("act", lambda e, e2=e2, Ops=Ops, tok=tok: e.activation(out=O_h[e2][:, tok], in_=Ops[:, e2 * 128:(e2 + 1) * 128], func=AF.Copy), reads=[self.r_PS[bOo]], writes=[rO[e2][m // 8]])
                kb.add("act", lambda e, SQ=SQ, Ops=Ops: e.activation(out=SQ, in_=Ops, func=AF.Square), reads=[self.r_PS[bOo]], writes=rSQ)
                bSt = 6 + u
                St = self.PS[:, bSt * 512:bSt * 512 + 128]
                for e2 in range(2):
                    kb.add("pe", lambda e, e2=e2, St=St, SQ=SQ: e.matmul(St, self.ONESB[:], SQ[:, e2 * 128:(e2 + 1) * 128], start=(e2 == 0), stop=(e2 == 1)), reads=rSQ + [self.r_cb], writes=[self.r_PS[bSt]])
                kb.add("dve", lambda e, St=St, tok=tok: e.tensor_scalar(out=F2[:, tok], in0=St, scalar1=1.0 / 256, scalar2=EPS, op0=ALU.mult, op1=ALU.add), reads=[self.r_PS[bSt]], writes=[self.r_F[2][m // 8]])
            kb.add("act", lambda e: e.activation(out=F2[:], in_=F2[:], func=AF.Sqrt), reads=self.Fres(2), writes=self.Fres(2))
            kb.add("dve", lambda e: e.reciprocal(out=F2[:], in_=F2[:]), reads=self.Fres(2), writes=self.Fres(2))
            for e2 in range(2):
                blk = 8 + 2 * h + e2
                gn = self.GBN[:, 4 * self.depth + l * 2 + e2:4 * self.depth + l * 2 + e2 + 1]
                for hf in range(2):
                    sl = slice(hf * 1024, (hf + 1) * 1024)
                    R = F3[:, 0:1024]
                    kb.add("sp", lambda e, blk=blk, sl=sl, h=h, e2=e2: e.dma_start(out=R, in_=self.PROJ[P_RG + 2 * h + e2][:, sl]), reads=[self.r_PROJ], writes=rTt, dma=True, key="glaR")
                    kb.add("act", lambda e: e.activation(out=R, in_=R, func=AF.Silu), reads=rTt, writes=rTt)
                    kb.add("dve", lambda e, e2=e2, sl=sl, gn=gn: e.scalar_tensor_tensor(out=O_h[e2][:, sl], in0=O_h[e2][:, sl], scalar=gn, in1=F2[:, sl], op0=ALU.mult, op1=ALU.mult),
                           reads=[rO[e2][hf], self.r_F[2][hf], self.r_GBN], writes=[rO[e2][hf]])
                    kb.add("dve", lambda e, e2=e2, sl=sl, blk=blk: e.tensor_tensor(out=self.HT[:, blk, sl], in0=O_h[e2][:, sl], in1=R, op=ALU.mult),
                           reads=[rO[e2][hf]] + rTt, writes=[self.r_HT[blk][hf]])

    def mixer(self, l, x_src, r_xs, x_dst, r_xd, nxt):
        self.phase_inproj(l)
        self.phase_attn(l)
        self.phase_gla_prep(l)
        self.phase_gla_out(l)
        ht = lambda k, h: (self.HT[:, k, :], self.r_HT[k][h])
        self.phase_C(self.w_mo[l], NB, ht, None, True, False, self.OUT, self.r_OUT, 1.0)
        self.phase_E(l, 3, x_src, r_xs, x_dst, r_xd, nxt)


D = 2048
T = 2048
NB = 16
DFF = 5504
NFB = 43
FA = 22
EPS = 1e-6
MIXW = 4624


class Prog(MixerMixin):
    def __init__(self, depth=4, dbg=(), with_ffn=True):
        self.dbg = dbg
        DEPTH = depth
        self.depth = depth
        nc = bass.Bass("TRN2", target_bir_lowering=False)
        self.nc = nc
        self.kb = KB(nc)
        kb = self.kb

        def din(name, shape, dt=F32):
            return nc.dram_tensor(name, list(shape), dt, kind="ExternalInput").ap()

        def dscr(name, shape, dt=F32):
            return nc.dram_tensor(name, list(shape), dt).ap()

        self.xT = din("xT", [NB, 128, T])
        self.yT = nc.dram_tensor("yT", [NB, 128, T], F32, kind="ExternalOutput").ap()
        self.pos = din("pos", [1, T], I32)
        self.gcol_d = din("gcol", [128, DEPTH * 6 * NB])
        if with_ffn:
            self.w_in = din("ffn_w_in", [DEPTH, 2, D, 2 * DFF])
            self.w_out = din("ffn_w_out", [DEPTH, 2, DFF, D])
        self.w_mi = din("w_mix_in", [DEPTH, D, MIXW])
        self.w_mo = din("w_mix_out", [DEPTH, D, D])
        self.sinks_d = din("attn_sinks", [DEPTH, 16])
        self.w2_d = din("gla_gate_w2", [DEPTH, 16, 512])
        self.gb_d = din("gb_col", [128, DEPTH * 4])
        self.gn_d = din("gn_col", [128, DEPTH * 2])
        self.cst_d = din("cst", [128, 512])
        self.ident_d = din("ident", [128, 128])
        self.core_d = din("corec", [128, 520])
        self.XA = dscr("XA", [NB, 128, T])
        self.XB = dscr("XB", [NB, 128, T])
        self.PART = dscr("PART", [NB, 128, T])
        self.OUT = dscr("OUT", [NB, 128, T])
        self.r_XA, self.r_XB, self.r_PART, self.r_OUT = [kb.res(n) for n in ("XA", "XB", "PART", "OUT")]
        self.r_xin = kb.res("xin")
        self.r_y = kb.res("yT")
        self.HT = kb.sbuf("HT", [128, NB, T], BF16)
        self.r_HT = [[kb.res(f"HT{j}_{h}") for h in range(2)] for j in range(NB)]
        self.BIGF = kb.sbuf("BIG", [128, FA * T], BF16)
        self.BIG = self.BIGF[:].rearrange("p (a b) -> p a b", b=T)
        self.r_BIG = [[kb.res(f"BIG{j}_{h}") for h in range(2)] for j in range(FA)]
        self.W = [kb.sbuf(f"W{i}", [128, NB, 128], BF16) for i in range(4)]
        self.r_W = [kb.res(f"W{i}") for i in range(4)]
        self.F = [kb.sbuf(f"F{i}", [128, T], F32) for i in range(4)]
        self.r_F = [[kb.res(f"F{i}_{h}") for h in range(2)] for i in range(4)]
        self.CST = kb.sbuf("CST", [128, 512], F32)
        self.r_CST = kb.res("CST")
        self.GCOL = kb.sbuf("GCOL", [128, DEPTH * 6 * NB], F32)
        self.r_GCOL = kb.res("GCOL")
        self.ONESB = kb.sbuf("ONESB", [128, 128], BF16)
        self.IDB = kb.sbuf("IDB", [128, 128], BF16)
        self.r_cb = kb.res("cb")
        self.PS = kb.psum("PS", [128, 8 * 512], F32)
        self.r_PS = [kb.res(f"PS{i}") for i in range(8)]
        self.wslot = 0
        self.stg = 0
        self.load_consts()
        self.mixer_init()

    C_ONES = 0
    C_PERM = 128
    C_GMASK = 256
    C_INVF = 384
    C_SIGN = 385
    def cst(self, c0, n):
        return self.CST[:, c0:c0 + n]

    def load_consts(self):
        kb = self.kb
        kb.add("sp", lambda e: e.dma_start(out=self.CST[:], in_=self.cst_d), writes=[self.r_CST], dma=True)
        kb.add("sp", lambda e: e.dma_start(out=self.GCOL[:], in_=self.gcol_d), writes=[self.r_GCOL], dma=True)
        kb.add("act", lambda e: e.activation(out=self.ONESB[:], in_=self.cst(self.C_ONES, 128), func=AF.Copy), reads=[self.r_CST], writes=[self.r_cb])
        kb.add("pool", lambda e: e.dma_start(out=self.IDB[:], in_=self.ident_d), writes=[self.r_cb], dma=True, key="identld")

    def gcol(self, l, i, j):
        c = (l * 6 + i) * NB + j
        return self.GCOL[:, c:c + 1]

    def stats_to_rstd(self, fi, c):
        kb = self.kb
        Fb = self.F[fi]
        rF = self.r_F[fi]
        for n in range(4):
            h = n // 2
            kb.add("pe", lambda e, n=n: e.matmul(self.PS[:, n * 512:(n + 1) * 512], self.cst(self.C_ONES, 128), Fb[:, n * 512:(n + 1) * 512], start=True, stop=True),
                   reads=[rF[h], self.r_CST], writes=[self.r_PS[n]])
        for h in range(2):
            sl = slice(h * 1024, (h + 1) * 1024)
            kb.add("dve", lambda e, sl=sl: e.tensor_scalar(out=Fb[:, sl], in0=self.PS[:, sl], scalar1=1.0 / (D * c * c), scalar2=EPS / (c * c), op0=ALU.mult, op1=ALU.add),
                   reads=[self.r_PS[2 * h], self.r_PS[2 * h + 1]], writes=[rF[h]])
            kb.add("act", lambda e, sl=sl: e.activation(out=Fb[:, sl], in_=Fb[:, sl], func=AF.Sqrt), reads=[rF[h]], writes=[rF[h]])
            kb.add("dve", lambda e, sl=sl: e.reciprocal(out=Fb[:, sl], in_=Fb[:, sl]), reads=[rF[h]], writes=[rF[h]])

    def next_stg(self):
        s = self.stg
        self.stg = (s + 1) % 4
        fi, h = 2 + s // 2, s % 2
        return self.F[fi][:, h * 1024:(h + 1) * 1024], self.r_F[fi][h]

    def phase_prenorm_from(self, x_ap, r_x, l, gi):
        kb = self.kb
        acc = self.F[0]
        kb.add("dve", lambda e: e.memset(acc[:], 0.0), writes=self.r_F[0])
        for j in range(NB):
            for h in range(2):
                sl = slice(h * 1024, (h + 1) * 1024)
                st, rst = self.next_stg()
                kb.add("sp", lambda e, st=st, j=j, sl=sl: e.dma_start(out=st, in_=x_ap[j][:, sl]), reads=[r_x], writes=[rst], dma=True)
                kb.add("act", lambda e, st=st, j=j, sl=sl: e.activation(out=self.HT[:, j, sl], in_=st, func=AF.Copy, scale=self.gcol(l, gi, j)),
                       reads=[rst, self.r_GCOL], writes=[self.r_HT[j][h]])
                kb.add("pool", lambda e, st=st: e.tensor_tensor(out=st, in0=st, in1=st, op=ALU.mult), reads=[rst], writes=[rst])
                kb.add("pool", lambda e, st=st, sl=sl: e.tensor_tensor(out=acc[:, sl], in0=acc[:, sl], in1=st, op=ALU.add), reads=[rst, self.r_F[0][h]], writes=[self.r_F[0][h]])
        self.stats_to_rstd(0, 1.0)

    def load_w(self, src_ap, nk):
        kb = self.kb
        s = self.wslot
        self.wslot = (s + 1) % 4
        Wt = self.W[s]
        kb.add("pool", lambda e: e.dma_start(out=Wt[:, 0:nk, :], in_=src_ap.rearrange("(kc p) m -> p kc m", p=128)), writes=[self.r_W[s]], dma=True)
        return Wt, self.r_W[s]

    def phase_ffn_B(self, l, i, f0, nf):
        kb = self.kb
        rstd = self.F[0]
        unit = getattr(self, "_bunit", 0)
        def ldB(b):
            fb = f0 + b
            return (self.load_w(self.w_in[l, i][:, fb * 128:(fb + 1) * 128], NB),
                    self.load_w(self.w_in[l, i][:, DFF + fb * 128:DFF + (fb + 1) * 128], NB))
        nxtw = ldB(0)
        for b in range(nf):
            (Wg, rWg), (Wu, rWu) = nxtw
            if b + 1 < nf:
                nxtw = ldB(b + 1)
            for h in range(2):
                base = (unit % 2) * 4
                unit += 1
                for (Wt, rW, boff) in ((Wg, rWg, 0), (Wu, rWu, 2)):
                    for n in range(2):
                        bank = base + boff + n
                        tsl = slice(h * 1024 + n * 512, h * 1024 + (n + 1) * 512)
                        for k in range(NB):
                            kb.add("pe", lambda e, Wt=Wt, k=k, tsl=tsl, bank=bank: e.matmul(self.PS[:, bank * 512:(bank + 1) * 512], Wt[:, k, :], self.HT[:, k, tsl], start=(k == 0), stop=(k == NB - 1)),
                                   reads=[rW, self.r_HT[k][h]], writes=[self.r_PS[bank]])
                sl = slice(h * 1024, (h + 1) * 1024)
                g_ps = self.PS[:, base * 512:(base + 2) * 512]
                u_ps = self.PS[:, (base + 2) * 512:(base + 4) * 512]
                s1, r1 = self.next_stg()
                s2, r2 = self.next_stg()
                kb.add("dve", lambda e, s1=s1, g_ps=g_ps, sl=sl: e.tensor_tensor(out=s1, in0=g_ps, in1=rstd[:, sl], op=ALU.mult),
                       reads=[self.r_PS[base], self.r_PS[base + 1], self.r_F[0][h]], writes=[r1])
                kb.add("act", lambda e, s1=s1: e.activation(out=s1, in_=s1, func=AF.Silu), reads=[r1], writes=[r1])
                kb.add("dve", lambda e, s2=s2, u_ps=u_ps, sl=sl: e.tensor_tensor(out=s2, in0=u_ps, in1=rstd[:, sl], op=ALU.mult),
                       reads=[self.r_PS[base + 2], self.r_PS[base + 3], self.r_F[0][h]], writes=[r2])
                kb.add("dve", lambda e, s1=s1, s2=s2, b=b, sl=sl: e.tensor_tensor(out=self.BIG[:, b, sl], in0=s1, in1=s2, op=ALU.mult),
                       reads=[r1, r2], writes=[self.r_BIG[b][h]])
        self._bunit = unit

    def phase_C(self, w_ap, nk, xsrc_blocks, r_src, last, part_in, out_ap, r_out, post_c):
        kb = self.kb
        acc = self.F[1]
        if last:
            kb.add("dve", lambda e: e.memset(acc[:], 0.0), writes=self.r_F[1])
        unit = getattr(self, "_cunit", 0)
        def ldC(j):
            a = self.load_w(w_ap[0:min(nk, NB) * 128, j * 128:(j + 1) * 128], min(nk, NB))
            b_ = self.load_w(w_ap[NB * 128:nk * 128, j * 128:(j + 1) * 128], nk - NB) if nk > NB else (None, None)
            return a, b_
        nxtw = ldC(0)
        for j in range(NB):
            (Wa, rWa), (Wb, rWb) = nxtw
            if j + 1 < NB:
                nxtw = ldC(j + 1)
            for h in range(2):
                base = (unit % 4) * 2
                unit += 1
                sl = slice(h * 1024, (h + 1) * 1024)
                st, rst = self.next_stg()
                if last and part_in:
                    kb.add("sp", lambda e, st=st, j=j, sl=sl: e.dma_start(out=st, in_=self.PART[j][:, sl]), reads=[self.r_PART], writes=[rst], dma=True)
                for n in range(2):
                    bank = base + n
                    tsl = slice(h * 1024 + n * 512, h * 1024 + (n + 1) * 512)
                    for k in range(nk):
                        Wt, rW, kk = (Wa, rWa, k) if k < NB else (Wb, rWb, k - NB)
                        src, rs = xsrc_blocks(k, h)
                        kb.add("pe", lambda e, Wt=Wt, kk=kk, src=src, tsl=tsl, bank=bank, k=k: e.matmul(self.PS[:, bank * 512:(bank + 1) * 512], Wt[:, kk, :], src[:, tsl], start=(k == 0), stop=(k == nk - 1)),
                               reads=[rW, rs], writes=[self.r_PS[bank]])
                ps = self.PS[:, base * 512:(base + 2) * 512]
                rps = [self.r_PS[base], self.r_PS[base + 1]]
                if not last:
                    kb.add("act", lambda e, st=st, ps=ps: e.activation(out=st, in_=ps, func=AF.Copy), reads=rps, writes=[rst])
                    kb.add("sp", lambda e, st=st, j=j, sl=sl: e.dma_start(out=self.PART[j][:, sl], in_=st), reads=[rst], writes=[self.r_PART], dma=True, par=True)
                else:
                    if part_in:
                        kb.add("dve", lambda e, st=st, ps=ps: e.tensor_tensor(out=st, in0=ps, in1=st, op=ALU.add), reads=rps + [rst], writes=[rst])
                    else:
                        kb.add("act", lambda e, st=st, ps=ps: e.activation(out=st, in_=ps, func=AF.Copy), reads=rps, writes=[rst])
                    kb.add("sp", lambda e, st=st, j=j, sl=sl: e.dma_start(out=out_ap[j][:, sl], in_=st), reads=[rst], writes=[r_out], dma=True, par=True)
                    s2, r2 = self.next_stg()
                    kb.add("act", lambda e, st=st, s2=s2: e.activation(out=s2, in_=st, func=AF.Square), reads=[rst], writes=[r2])
                    kb.add("dve", lambda e, s2=s2, sl=sl: e.tensor_tensor(out=acc[:, sl], in0=acc[:, sl], in1=s2, op=ALU.add), reads=[r2, self.r_F[1][h]], writes=[self.r_F[1][h]])
        self._cunit = unit
        if last:
            self.stats_to_rstd(1, post_c)

    def phase_E(self, l, gi_post, x_src, r_xs, x_dst, r_xd, nxt):
        kb = self.kb
        rstd = self.F[1]
        acc = self.F[0]
        if nxt is not None:
            kb.add("dve", lambda e: e.memset(acc[:], 0.0), writes=self.r_F[0])
        for j in range(NB):
            for h in range(2):
                sl = slice(h * 1024, (h + 1) * 1024)
                s1, r1 = self.next_stg()
                s2, r2 = self.next_stg()
                kb.add("sp", lambda e, s1=s1, j=j, sl=sl: e.dma_start(out=s1, in_=self.OUT[j][:, sl]), reads=[self.r_OUT], writes=[r1], dma=True)
                kb.add("sp", lambda e, s2=s2, j=j, sl=sl: e.dma_start(out=s2, in_=x_src[j][:, sl]), reads=[r_xs], writes=[r2], dma=True)
                kb.add("dve", lambda e, s1=s1, sl=sl: e.tensor_tensor(out=s1, in0=s1, in1=rstd[:, sl], op=ALU.mult), reads=[r1, self.r_F[1][h]], writes=[r1])
                kb.add("dve", lambda e, s1=s1, s2=s2, j=j: e.scalar_tensor_tensor(out=s2, in0=s1, scalar=self.gcol(l, gi_post, j), in1=s2, op0=ALU.mult, op1=ALU.add),
                       reads=[r1, r2, self.r_GCOL], writes=[r2])
                kb.add("sp", lambda e, s2=s2, j=j, sl=sl: e.dma_start(out=x_dst[j][:, sl], in_=s2), reads=[r2], writes=[r_xd], dma=True, par=True)
                if nxt is not None:
                    nl, ng = nxt
                    kb.add("act", lambda e, s2=s2, j=j, sl=sl, nl=nl, ng=ng: e.activation(out=self.HT[:, j, sl], in_=s2, func=AF.Copy, scale=self.gcol(nl, ng, j)),
                           reads=[r2, self.r_GCOL], writes=[self.r_HT[j][h]])
                    kb.add("pool", lambda e, s1=s1, s2=s2: e.tensor_tensor(out=s1, in0=s2, in1=s2, op=ALU.mult), reads=[r2], writes=[r1])
                    kb.add("pool", lambda e, s1=s1, sl=sl: e.tensor_tensor(out=acc[:, sl], in0=acc[:, sl], in1=s1, op=ALU.add), reads=[r1, self.r_F[0][h]], writes=[self.r_F[0][h]])
        if nxt is not None:
            self.stats_to_rstd(0, 1.0)

    def ffn(self, l, i, x_src, r_xs, x_dst, r_xd, nxt):
        big = lambda k, h: (self.BIG[:, k, :], self.r_BIG[k][h])
        self.phase_ffn_B(l, i, 0, FA)
        self.phase_C(self.w_out[l, i][0:FA * 128, :], FA, big, None, False, False, None, None, None)
        self.phase_ffn_B(l, i, FA, NFB - FA)
        self.phase_C(self.w_out[l, i][FA * 128:DFF, :], NFB - FA, big, None, True, True, self.OUT, self.r_OUT, 0.5)
        self.phase_E(l, 1 + 4 * i, x_src, r_xs, x_dst, r_xd, nxt)

    def finish(self):
        kb = self.kb
        kb.add("sp", lambda e: e.nop(), reads=[self.r_y])
        kb.emit()
        kb.close()
        return self.nc


def make_consts():
    c = np.zeros((128, 512), np.float32)
    c[:, 0:128] = 1.0
    P = np.zeros((128, 128), np.float32)
    for m in range(128):
        k = (m // 64) * 64 + ((m % 64) + 32) % 64
        P[k, m] = 1.0
    c[:, 128:256] = P
    s = np.arange(128)[:, None]; t = np.arange(128)[None, :]
    c[:, 256:384] = ((s // 64 == t // 64) & (s <= t)).astype(np.float32)
    inv_freq = (10000.0 ** (-np.arange(0, 64, 2, dtype=np.float32) / 64)).astype(np.float32)
    d = np.arange(128) % 64
    c[:, 384] = inv_freq[d % 32]
    c[:, 385] = np.where(d < 32, -1.0, 1.0)
    return c

def make_core_consts(rank):
    c = np.zeros((128, 520), np.float32)
    qi = np.arange(128)[:, None]; kj = np.arange(256)[None, :]
    diff = qi + 128 - kj
    band = (diff >= 0) & (diff < 128)
    c[:, 0:256] = np.where(band, 0.0, -30000.0)
    first = band & (kj >= 128) if rank == 0 else band
    c[:, 256:512] = np.where(first, 0.0, -30000.0)
    if rank > 0:
        c[:, 512 + rank - 1] = 1.0
    for r in range(4):
        if r < rank:
            c[:, 516 + r] = 1.0
    return c


def build_full(depth=4):
    p = Prog(depth=depth)
    p.phase_rope_tables()
    p.phase_prenorm_from(p.xT, p.r_xin, 0, 0)
    locs = [(p.XA, p.r_XA), (p.XB, p.r_XB)]
    src = (p.xT, p.r_xin)
    k = 0
    nsub = 3 * depth
    for l in range(depth):
        for sub in range(3):
            dst = (p.yT, p.r_y) if k == nsub - 1 else locs[k % 2]
            if sub == 0:
                p.ffn(l, 0, src[0], src[1], dst[0], dst[1], (l, 2))
            elif sub == 1:
                p.mixer(l, src[0], src[1], dst[0], dst[1], (l, 4))
            else:
                p.ffn(l, 1, src[0], src[1], dst[0], dst[1], (l + 1, 0) if l + 1 < depth else None)
            src = dst
            k += 1
    return p.finish()


def kernel(x, positions, norm_gains, ffn_w_in, ffn_w_out, w_mix_in, attn_sinks, gla_gate_w2, gla_gate_b, gla_norm_gain, w_mix_out):
    x = np.asarray(x, dtype=np.float32)
    positions = np.asarray(positions, dtype=np.int32)
    depth = int(np.asarray(norm_gains).shape[0])
    B, S, _ = x.shape
    f32 = lambda a: np.ascontiguousarray(np.asarray(a, dtype=np.float32))
    shared = dict(
        gcol=np.ascontiguousarray(f32(norm_gains).reshape(depth * 6 * NB, 128).T),
        ffn_w_in=f32(ffn_w_in), ffn_w_out=f32(ffn_w_out), w_mix_in=f32(w_mix_in), w_mix_out=f32(w_mix_out),
        attn_sinks=f32(attn_sinks), gla_gate_w2=f32(gla_gate_w2),
        gb_col=np.ascontiguousarray(f32(gla_gate_b).reshape(depth * 4, 128).T),
        gn_col=np.ascontiguousarray(f32(gla_norm_gain).reshape(depth * 2, 128).T),
        cst=make_consts(), ident=np.eye(128, dtype=np.float32))
    ins = []
    for core in range(8):
        b, r = core // 4, core % 4
        d = dict(shared)
        d["xT"] = np.ascontiguousarray(x[b, r * T:(r + 1) * T].T).reshape(NB, 128, T)
        d["pos"] = np.ascontiguousarray(positions[b, r * T:(r + 1) * T][None])
        d["corec"] = make_core_consts(r)
        ins.append(d)
    nc = build_full(depth)
    res = run_bass_kernel_spmd(nc, ins, core_ids=list(range(8)))
    y = np.empty((B, S, D), np.float32)
    for core in range(8):
        b, r = core // 4, core % 4
        y[b, r * T:(r + 1) * T] = res.results[core]["yT"].reshape(D, T).T
    return y
```

```python
import numpy as np
from contextlib import ExitStack
import concourse.bass as bass
import concourse.mybir as mybir
from concourse.bass_utils import run_bass_kernel_spmd

F32 = mybir.dt.float32
BF16 = mybir.dt.bfloat16
I32 = mybir.dt.int32
AF = mybir.ActivationFunctionType
ALU = mybir.AluOpType
AX = mybir.AxisListType

ENGS = ("pe", "act", "dve", "pool", "sp")


class Res:
    __slots__ = ("name", "ws", "rs", "prs")

    def __init__(self, name):
        self.name = name
        self.ws = []
        self.rs = []
        self.prs = []


class Op:
    __slots__ = ("eng", "fn", "deps", "marked", "val", "dma", "key", "idx", "inc")

    def __init__(self, eng, fn, dma, key, idx, inc=16):
        self.inc = inc
        self.eng = eng
        self.fn = fn
        self.deps = []
        self.marked = False
        self.val = None
        self.dma = dma
        self.key = key
        self.idx = idx


class KB:
    def __init__(self, nc):
        self.nc = nc
        self.ops = {e: [] for e in ENGS}
        self.stack = ExitStack()
        self.nres = 0

    def sbuf(self, name, shape, dtype):
        return self.stack.enter_context(self.nc.sbuf_tensor(name, list(shape), dtype))

    def psum(self, name, shape, dtype):
        return self.stack.enter_context(self.nc.psum_tensor(name, list(shape), dtype))

    def res(self, name=None):
        self.nres += 1
        return Res(name or f"r{self.nres}")

    def add(self, eng, fn, reads=(), writes=(), dma=False, key=None, inc=16, par=False):
        ops = self.ops[eng]
        if dma and key is None:
            key = "dma_" + writes[0].name
        op = Op(eng, fn, dma, key, len(ops), inc)
        deps = []
        for r in reads:
            for o in r.ws:
                deps.append((o, "raw"))
        for w in writes:
            if w.rs:
                for o in w.rs:
                    deps.append((o, "war"))
            elif par:
                for o in w.prs:
                    deps.append((o, "war"))
            else:
                for o in w.ws:
                    deps.append((o, "waw"))
        for d, kind in deps:
            if d is op:
                continue
            if d.eng == eng and not d.dma and not dma:
                if eng == "pe":
                    continue
                if kind != "raw" or (op.idx - d.idx) > 2:
                    continue
            if d.eng == eng and d.dma and dma and False:
                continue
            d.marked = True
            op.deps.append(d)
        for r in reads:
            r.rs.append(op)
        for w in writes:
            if w.rs:
                w.prs = w.rs
                w.rs = []
                w.ws = [op]
            elif par:
                w.ws.append(op)
            else:
                w.ws = [op]
        ops.append(op)
        return op

    def emit(self):
        nc = self.nc
        keys = {}
        for e in ENGS:
            cnt = 0
            for op in self.ops[e]:
                if op.dma:
                    keys[op.key] = keys.get(op.key, 0) + op.inc
                    op.val = keys[op.key]
                elif op.marked:
                    cnt += 1
                    op.val = cnt
            self.nmarks = getattr(self, "nmarks", {})
            self.nmarks[e] = cnt
        EPOCH = 20000
        nep = {e: self.nmarks[e] // EPOCH + 1 for e in ENGS}
        semnames = [f"{e}{i}" for e in ENGS for i in range(nep[e])] + sorted(keys.keys())
        sems = {}
        for n in semnames:
            sems[n] = self.stack.enter_context(nc.semaphore("s_" + n))
        self.nsems = len(semnames)

        def run(engname, engobj):
            seen = {}
            for op in self.ops[engname]:
                need = {}
                for d in op.deps:
                    k = d.key if d.dma else d.eng
                    if d.val > need.get(k, 0):
                        need[k] = d.val
                for k, v in need.items():
                    if seen.get(k, 0) >= v:
                        continue
                    seen[k] = v
                    if k in ENGS:
                        ep, vv = (v - 1) // EPOCH, (v - 1) % EPOCH + 1
                        engobj.wait_ge(sems[f"{k}{ep}"], vv)
                    else:
                        engobj.wait_ge(sems[k], v)
                ins = op.fn(engobj)
                if op.dma:
                    ins.then_inc(sems[op.key], op.inc)
                elif op.marked:
                    ins.then_inc(sems[f"{engname}{(op.val - 1) // EPOCH}"], 1)

        with nc.Block() as block:
            @block.tensor
            def _(e):
                run("pe", e)

            @block.scalar
            def _(e):
                run("act", e)

            @block.vector
            def _(e):
                run("dve", e)

            @block.gpsimd
            def _(e):
                run("pool", e)

            @block.sync
            def _(e):
                run("sp", e)

    def close(self):
        self.stack.close()


NPROJ = 41
P_QA, P_KA, P_VA, P_QG, P_KG, P_VG, P_RG, P_GLR = 0, 8, 12, 16, 20, 24, 32, 40
O_QR, O_KT, O_V2, O_TMPB, O_GKV, O_PN, O_PT = 0, 16384, 25088, 34816, 36864, 40960, 43008
KTW = 2176
O_TMP2, O_AM, O_SQP, O_SB = 40960, 43008, 43264, 43776
S_DEC, S_DTOT, S_ATT = 0, 128, 136
TWO_PI_HI = 6.28125
TWO_PI_LO = 6.283185307179586 - 6.28125
MAGIC = 12582912.0


def _proj_cols(m):
    if m < 8:
        return [(m * 128, 128)]
    if m < 12:
        g = m - 8
        return [(1024 + g * 64, 64)] * 2
    if m < 16:
        g = m - 12
        return [(1280 + g * 64, 64)] * 2
    if m < 20:
        return [(1536 + (m - 16) * 128, 128)]
    if m < 24:
        return [(2048 + (m - 20) * 128, 128)]
    if m < 32:
        return [(2560 + (m - 24) * 128, 128)]
    if m < 40:
        return [(3584 + (m - 32) * 128, 128)]
    return [(4608, 16)]


class MixerMixin:
    def mixer_init(self):
        kb, nc = self.kb, self.nc
        dscr = lambda name, shape, dt=F32: nc.dram_tensor(name, list(shape), dt).ap()
        self.PROJ = dscr("PROJ", [NPROJ, 128, T])
        self.r_PROJ = kb.res("PROJ")
        self.COSD = dscr("COSD", [128, T]); self.SIND = dscr("SIND", [128, T])
        self.r_ROPE = kb.res("ROPE")
        self.KVsrc = dscr("KVsrc", [128, 1024], BF16); self.KVdst = dscr("KVdst", [512, 1024], BF16)
        self.Ssrc = dscr("Ssrc", [128, 1028]); self.Sdst = dscr("Sdst", [512, 1028])
        self.r_KVsrc, self.r_KVdst, self.r_Ssrc, self.r_Sdst = [kb.res(n) for n in ("KVsrc", "KVdst", "Ssrc", "Sdst")]
        self.CORE = kb.sbuf("CORE", [128, 520], F32); self.r_CORE = kb.res("CORE")
        self.SINK = kb.sbuf("SINK", [128, 16], F32); self.r_SINK = kb.res("SINK")
        self.SINKP = kb.sbuf("SINKP", [128, 16], F32); self.r_SINKP = kb.res("SINKP")
        self.GBN = kb.sbuf("GBN", [128, 6 * self.depth], F32); self.r_GBN = kb.res("GBN")
        self.SMALL = kb.sbuf("SMALL", [128, 256], F32); self.r_SMALL = kb.res("SMALL")
        self.r_att = [kb.res("att0"), kb.res("att1")]
        self.r_att2 = [kb.res("att20"), kb.res("att21")]
        self.r_Sm = [kb.res(f"Sm{h}") for h in range(4)]
        kb.add("sp", lambda e: e.dma_start(out=self.CORE[:], in_=self.core_d), writes=[self.r_CORE], dma=True)
        kb.add("sp", lambda e: e.dma_start(out=self.GBN[:, 0:4 * self.depth], in_=self.gb_d), writes=[self.r_GBN], dma=True, key="gbn")
        kb.add("sp", lambda e: e.dma_start(out=self.GBN[:, 4 * self.depth:6 * self.depth], in_=self.gn_d), writes=[self.r_GBN], dma=True, key="gbn", par=True)

    def bigf(self, lo, n):
        return self.BIGF[:, lo:lo + n]

    def bigres(self, lo, n):
        out = []
        for blk in range(lo // 2048, (lo + n - 1) // 2048 + 1):
            for h in range(2):
                a = blk * 2048 + h * 1024
                if a < lo + n and a + 1024 > lo:
                    out.append(self.r_BIG[blk][h])
        return out

    def Fres(self, i):
        return [self.r_F[i][0], self.r_F[i][1]]

    def phase_rope_tables(self):
        kb = self.kb
        A, B_, C_, Pi = self.F[0], self.F[1], self.F[2], self.F[3]
        rA, rB, rC, rP = self.Fres(0), self.Fres(1), self.Fres(2), self.Fres(3)
        invf = self.cst(self.C_INVF, 1); sign = self.cst(self.C_SIGN, 1)
        kb.add("sp", lambda e: e.dma_start(out=Pi[:].bitcast(I32), in_=self.pos.partition_broadcast(128)), writes=rP, dma=True, key="posld")
        kb.add("dve", lambda e: e.tensor_copy(out=A[:], in_=Pi[:].bitcast(I32)), reads=rP, writes=rA)
        kb.add("dve", lambda e: e.tensor_scalar(out=A[:], in0=A[:], scalar1=invf, scalar2=0.0, op0=ALU.mult, op1=ALU.add), reads=rA + [self.r_CST], writes=rA)

        def reduce_sin(src, rsrc, dst, rdst, tmp, rtmp):
            kb.add("dve", lambda e: e.tensor_scalar(out=tmp[:], in0=src[:], scalar1=float(1.0 / (2 * np.pi)), scalar2=MAGIC, op0=ALU.mult, op1=ALU.add), reads=rsrc, writes=rtmp)
            kb.add("dve", lambda e: e.tensor_scalar(out=tmp[:], in0=tmp[:], scalar1=-MAGIC, scalar2=1.0, op0=ALU.add, op1=ALU.mult), reads=rtmp, writes=rtmp)
            kb.add("dve", lambda e: e.scalar_tensor_tensor(out=dst[:], in0=tmp[:], scalar=-TWO_PI_HI, in1=src[:], op0=ALU.mult, op1=ALU.add), reads=rtmp + rsrc, writes=rdst)
            kb.add("dve", lambda e: e.scalar_tensor_tensor(out=dst[:], in0=tmp[:], scalar=-TWO_PI_LO, in1=dst[:], op0=ALU.mult, op1=ALU.add), reads=rtmp + rdst, writes=rdst)
            kb.add("dve", lambda e: e.tensor_scalar(out=dst[:], in0=dst[:], scalar1=3.1415925, scalar2=-3.1415925, op0=ALU.min, op1=ALU.max), reads=rdst, writes=rdst)

        reduce_sin(A, rA, B_, rB, C_, rC)
        kb.add("act", lambda e: e.activation(out=C_[:], in_=B_[:], func=AF.Sin), reads=rB, writes=rC)
        kb.add("dve", lambda e: e.tensor_scalar(out=C_[:], in0=C_[:], scalar1=sign, scalar2=0.0, op0=ALU.mult, op1=ALU.add), reads=rC + [self.r_CST], writes=rC)
        kb.add("sp", lambda e: e.dma_start(out=self.SIND, in_=C_[:]), reads=rC, writes=[self.r_ROPE], dma=True, key="rope")
        kb.add("dve", lambda e: e.tensor_scalar(out=A[:], in0=B_[:], scalar1=float(np.pi / 2), scalar2=0.0, op0=ALU.add, op1=ALU.add), reads=rB, writes=rA)
        reduce_sin(A, rA, B_, rB, Pi, rP)
        kb.add("act", lambda e: e.activation(out=A[:], in_=B_[:], func=AF.Sin), reads=rB, writes=rA)
        kb.add("sp", lambda e: e.dma_start(out=self.COSD, in_=A[:]), reads=rA, writes=[self.r_ROPE], dma=True, key="rope", par=True)

    def phase_inproj(self, l):
        kb = self.kb
        rstd = self.F[0]
        unit = 0

        def ld(m):
            runs = _proj_cols(m)
            s = self.wslot
            self.wslot = (s + 1) % 4
            Wt = self.W[s]
            c = 0
            for ri, (c0, n) in enumerate(runs):
                kb.add("pool", lambda e, c=c, c0=c0, n=n: e.dma_start(out=Wt[:, :, c:c + n], in_=self.w_mi[l][:, c0:c0 + n].rearrange("(kc p) m -> p kc m", p=128)),
                       writes=[self.r_W[s]], dma=True, par=(ri > 0))
                c += n
            return Wt, self.r_W[s], c
        import os
        mlist = [int(v) for v in os.environ.get("MLIST", "").split(",") if v] or list(range(NPROJ))
        nxtw = ld(mlist[0])
        for mi, m in enumerate(mlist):
            Wt, rW, ncol = nxtw
            if mi + 1 < len(mlist):
                nxtw = ld(mlist[mi + 1])
            for h in range(2):
                base = (unit % 4) * 2
                unit += 1
                sl = slice(h * 1024, (h + 1) * 1024)
                for n in range(2):
                    bank = base + n
                    tsl = slice(h * 1024 + n * 512, h * 1024 + (n + 1) * 512)
                    for k in range(NB):
                        kb.add("pe", lambda e, Wt=Wt, k=k, tsl=tsl, bank=bank, ncol=ncol: e.matmul(self.PS[0:ncol, bank * 512:(bank + 1) * 512], Wt[:, k, 0:ncol], self.HT[:, k, tsl], start=(k == 0), stop=(k == NB - 1)),
                               reads=[rW, self.r_HT[k][h]], writes=[self.r_PS[bank]])
                st, rst = self.next_stg()
                kb.add("dve", lambda e, st=st, base=base, sl=sl, ncol=ncol: e.tensor_tensor(out=st[0:ncol, :], in0=self.PS[0:ncol, base * 512:(base + 2) * 512], in1=rstd[0:ncol, sl], op=ALU.mult),
                       reads=[self.r_PS[base], self.r_PS[base + 1], self.r_F[0][h]], writes=[rst])
                kb.add("sp", lambda e, st=st, m=m, sl=sl, ncol=ncol: e.dma_start(out=self.PROJ[m][0:ncol, sl], in_=st[0:ncol, :]), reads=[rst], writes=[self.r_PROJ], dma=True, par=True)

    def rope_block(self, m, dst_lo, COS, SIN, rT):
        kb = self.kb
        for h in range(2):
            sl = slice(h * 1024, (h + 1) * 1024)
            X, rX = self.F[2][:, sl], [self.r_F[2][h]]
            Tm, rTm = self.F[3][:, sl], [self.r_F[3][h]]
            kb.add("sp", lambda e, X=X, sl=sl: e.dma_start(out=X, in_=self.PROJ[m][:, sl]), reads=[self.r_PROJ], writes=rX, dma=True)
            pb = 4 + 2 * h
            for n in range(2):
                kb.add("pe", lambda e, n=n, X=X, pb=pb: e.matmul(self.PS[:, (pb + n) * 512:(pb + n + 1) * 512], self.cst(self.C_PERM, 128), X[:, n * 512:(n + 1) * 512], start=True, stop=True),
                       reads=rX + [self.r_CST], writes=[self.r_PS[pb + n]])
            kb.add("dve", lambda e, Tm=Tm, pb=pb, sl=sl: e.tensor_tensor(out=Tm, in0=self.PS[:, pb * 512:(pb + 2) * 512], in1=SIN[:, sl], op=ALU.mult), reads=[self.r_PS[pb], self.r_PS[pb + 1]] + rT, writes=rTm)
            kb.add("pool", lambda e, X=X, sl=sl: e.tensor_tensor(out=X, in0=X, in1=COS[:, sl], op=ALU.mult), reads=rX + rT, writes=rX)
            kb.add("dve", lambda e, X=X, Tm=Tm, h=h: e.tensor_tensor(out=self.bigf(dst_lo + h * 1024, 1024), in0=X, in1=Tm, op=ALU.add), reads=rX + rTm, writes=self.bigres(dst_lo + h * 1024, 1024))

    def phase_attn(self, l):
        kb = self.kb
        COS, SIN = self.F[0], self.F[1]
        rT = self.Fres(0) + self.Fres(1)
        kb.add("sp", lambda e: e.dma_start(out=COS[:], in_=self.COSD), reads=[self.r_ROPE], writes=self.Fres(0), dma=True, key="ldcos")
        kb.add("sp", lambda e: e.dma_start(out=SIN[:], in_=self.SIND), reads=[self.r_ROPE], writes=self.Fres(1), dma=True, key="ldsin")
        kb.add("sp", lambda e: e.dma_start(out=self.SINK[:], in_=self.sinks_d[l:l + 1, :].partition_broadcast(128)), writes=[self.r_SINK], dma=True)
        kb.add("dve", lambda e: e.tensor_copy(out=self.SINKP[:].rearrange("p (g b a) -> p g b a", b=2, a=2), in_=self.SINK[:].rearrange("p (g a b) -> p g b a", a=2, b=2)), reads=[self.r_SINK], writes=[self.r_SINKP])
        for g in range(4):
            self.rope_block(P_KA + g, O_KT + g * KTW + 128, COS, SIN, rT)
        import os
        ATT = int(os.environ.get("ATT", "9"))
        if ATT < 2:
            return
        X = self.F[2]; rX = self.Fres(2)
        for g in range(4):
            kb.add("sp", lambda e, g=g: e.dma_start(out=X[:], in_=self.PROJ[P_VA + g]), reads=[self.r_PROJ], writes=rX, dma=True, key="ropeX")
            kb.add("act", lambda e: e.activation(out=self.bigf(O_TMPB, T), in_=X[:], func=AF.Copy), reads=rX, writes=self.bigres(O_TMPB, T))
            for half in range(2):
                bank = 4 + half
                pst = self.PS[:, bank * 512:(bank + 1) * 512].bitcast(BF16)
                for tt in range(8):
                    t = half * 8 + tt
                    kb.add("pe", lambda e, t=t, tt=tt, pst=pst: e.transpose(out=pst[:, tt * 128:(tt + 1) * 128], in_=self.bigf(O_TMPB + t * 128, 128), identity=self.IDB[:]),
                           reads=self.bigres(O_TMPB + t * 128, 128) + [self.r_cb], writes=[self.r_PS[bank]])
                lo = O_V2 + (1 + half * 8) * 512 + g * 128
                dst = self.BIGF[:, lo:lo + 8 * 512].rearrange("p (t c) -> p t c", c=512)[:, :, 0:128]
                kb.add("act", lambda e, dst=dst, pst=pst: e.activation(out=dst, in_=pst.rearrange("p (t c) -> p t c", c=128), func=AF.Copy),
                       reads=[self.r_PS[bank]], writes=self.bigres(O_V2 + (1 + half * 8) * 512, 8 * 512))
        if ATT < 3:
            return
        for b in range(8):
            self.rope_block(P_QA + b, O_QR + b * T, COS, SIN, rT)
        if ATT < 4:
            return
        for g in range(4):
            kb.add("sp", lambda e, g=g: e.dma_start(out=self.KVsrc[:, g * 128:(g + 1) * 128], in_=self.bigf(O_KT + g * KTW + T, 128)),
                   reads=self.bigres(O_KT + g * KTW + T, 128), writes=[self.r_KVsrc], dma=True, par=(g > 0))
        kb.add("sp", lambda e: e.dma_start(out=self.KVsrc[:, 512:1024], in_=self.bigf(O_V2 + 16 * 512, 512)), reads=self.bigres(O_V2 + 16 * 512, 512), writes=[self.r_KVsrc], dma=True, par=True)
        kb.add("pool", lambda e: e.collective_compute("AllGather", ALU.bypass, replica_groups=[[0, 1, 2, 3], [4, 5, 6, 7]], ins=[self.KVsrc], outs=[self.KVdst]),
               reads=[self.r_KVsrc], writes=[self.r_KVdst], dma=True, key="cc1", inc=1)
        G = self.bigf(O_GKV, 4096).rearrange("p (r c) -> p r c", c=1024)
        rG = self.bigres(O_GKV, 4096)
        kb.add("sp", lambda e: e.dma_start(out=G, in_=self.KVdst.rearrange("(r p) c -> p r c", p=128)), reads=[self.r_KVdst], writes=rG, dma=True, key="gkv")
        if ATT < 5:
            return
        sel = lambda r: self.CORE[:, 512 + r:513 + r]
        for g in range(4):
            dK = self.bigf(O_KT + g * KTW, 128)
            rK = self.bigres(O_KT + g * KTW, 128)
            for r in range(4):
                src = G[:, r, g * 128:(g + 1) * 128]
                if r == 0:
                    kb.add("dve", lambda e, dK=dK, src=src, r=r: e.tensor_scalar(out=dK, in0=src, scalar1=sel(r), scalar2=0.0, op0=ALU.mult, op1=ALU.add), reads=rG + [self.r_CORE], writes=rK)
                else:
                    kb.add("dve", lambda e, dK=dK, src=src, r=r: e.scalar_tensor_tensor(out=dK, in0=src, scalar=sel(r), in1=dK, op0=ALU.mult, op1=ALU.add), reads=rG + rK + [self.r_CORE], writes=rK)
        dV = self.bigf(O_V2, 512); rV = self.bigres(O_V2, 512)
        for r in range(4):
            src = G[:, r, 512:1024]
            if r == 0:
                kb.add("dve", lambda e, src=src, r=r: e.tensor_scalar(out=dV, in0=src, scalar1=sel(r), scalar2=0.0, op0=ALU.mult, op1=ALU.add), reads=rG + [self.r_CORE], writes=rV)
            else:
                kb.add("dve", lambda e, src=src, r=r: e.scalar_tensor_tensor(out=dV, in0=src, scalar=sel(r), in1=dV, op0=ALU.mult, op1=ALU.add), reads=rG + rV + [self.r_CORE], writes=rV)
        if ATT < 6:
            return
        kb.add("dve", lambda e: e.tensor_reduce(out=self.SMALL[:, S_ATT + 64:S_ATT + 68], in_=self.SINK[:].rearrange("p (g j) -> p g j", j=4), axis=AX.X, op=ALU.max), reads=[self.r_SINK], writes=[self.r_SMALL])
        smax = lambda g: self.SMALL[:, S_ATT + 64 + g:S_ATT + 65 + g]
        units = [(g, t) for g in range(4) for t in range(16)]

        def stageA(ui, g, t):
            u = ui % 2
            bS = u * 4
            S = self.PS[:, bS * 512:(bS + 2) * 512]
            rS = [self.r_PS[bS], self.r_PS[bS + 1]]
            for j in range(4):
                blk, hf = 2 * g + j // 2, j % 2
                psl = slice(hf * 64, hf * 64 + 64)
                qlo = O_QR + blk * T + t * 128
                klo = O_KT + g * KTW + t * 128
                pj = (j % 2) * 2 + j // 2
                kb.add("pe", lambda e, pj=pj, psl=psl, qlo=qlo, klo=klo, S=S: e.matmul(S[:, pj * 256:(pj + 1) * 256], self.BIGF[psl, qlo:qlo + 128], self.BIGF[psl, klo:klo + 256], start=True, stop=True),
                       reads=self.bigres(qlo, 128) + self.bigres(klo, 256), writes=[rS[pj // 2]])
            sm, rsm = self.next_stg()
            mask = self.CORE[:, 256:512] if t == 0 else self.CORE[:, 0:256]
            st0 = S_ATT + u * 32
            sml = lambda a, st0=st0: self.SMALL[:, st0 + a * 4:st0 + a * 4 + 4]
            rsml = self.r_att[u]
            sm3 = sm.rearrange("p (j k) -> p j k", k=256)
            S3 = S.rearrange("p (j k) -> p j k", k=256)
            kb.add("dve", lambda e: e.scalar_tensor_tensor(out=sm3, in0=S3, scalar=0.125, in1=mask.unsqueeze(1).to_broadcast([128, 4, 256]), op0=ALU.mult, op1=ALU.add),
                   reads=rS + [self.r_CORE], writes=[rsm])
            kb.add("dve", lambda e: e.tensor_reduce(out=sml(0)[:, 0:1], in_=sm, axis=AX.X, op=ALU.max), reads=[rsm], writes=[rsml])
            kb.add("dve", lambda e: e.tensor_scalar(out=sml(1)[:, 0:1], in0=sml(0)[:, 0:1], scalar1=smax(g), scalar2=-1.0, op0=ALU.max, op1=ALU.mult), reads=[rsml, self.r_SMALL], writes=[rsml])
            kb.add("act", lambda e: e.activation(out=sm, in_=sm, func=AF.Exp, bias=sml(1)[:, 0:1]), reads=[rsm, rsml], writes=[rsm])
            kb.add("act", lambda e: e.activation(out=sml(3), in_=self.SINKP[:, 4 * g:4 * g + 4], func=AF.Exp, bias=sml(1)[:, 0:1]), reads=[rsml, self.r_SINKP], writes=[self.r_att2[u]])
            kb.add("dve", lambda e: e.tensor_reduce(out=sml(2), in_=sm3, axis=AX.X, op=ALU.add), reads=[rsm], writes=[rsml])
            kb.add("dve", lambda e: e.tensor_tensor(out=sml(4), in0=sml(2), in1=sml(3), op=ALU.add), reads=[rsml, self.r_att2[u]], writes=[rsml])
            kb.add("dve", lambda e: e.reciprocal(out=sml(4), in_=sml(4)), reads=[rsml], writes=[rsml])
            pn = self.bigf(O_PN + u * 1024, 1024); rpn = self.bigres(O_PN + u * 1024, 1024)
            kb.add("dve", lambda e: e.tensor_tensor(out=pn.rearrange("p (j k) -> p j k", k=256), in0=sm3, in1=sml(4).unsqueeze(2).to_broadcast([128, 4, 256]), op=ALU.mult),
                   reads=[rsm, rsml], writes=rpn)

        def stageB(ui, g, t):
            u = ui % 2
            bT, bO = u * 4 + 2, u * 4 + 3
            pn = self.bigf(O_PN + u * 1024, 1024); rpn = self.bigres(O_PN + u * 1024, 1024)
            pst = self.PS[:, bT * 512:(bT + 1) * 512].bitcast(BF16)
            for jc in range(8):
                kb.add("pe", lambda e, jc=jc: e.transpose(out=pst[:, jc * 128:(jc + 1) * 128], in_=pn[:, jc * 128:(jc + 1) * 128], identity=self.IDB[:]),
                       reads=rpn + [self.r_cb], writes=[self.r_PS[bT]])
            pT = self.bigf(O_PT + u * 1024, 1024); rpT = self.bigres(O_PT + u * 1024, 1024)
            kb.add("act", lambda e: e.activation(out=pT, in_=pst, func=AF.Copy), reads=[self.r_PS[bT]], writes=rpT)
            O = self.PS[:, bO * 512:(bO + 1) * 512]
            for j in range(4):
                for c in range(2):
                    vlo = O_V2 + (t + c) * 512 + g * 128
                    kb.add("pe", lambda e, j=j, c=c, vlo=vlo: e.matmul(O[:, j * 128:(j + 1) * 128], self.bigf(vlo, 128), pT[:, (2 * j + c) * 128:(2 * j + c + 1) * 128], start=(c == 0), stop=(c == 1)),
                           reads=self.bigres(vlo, 128) + rpT, writes=[self.r_PS[bO]])
            for hf in range(2):
                psl = slice(hf * 64, hf * 64 + 64)
                kb.add("act", lambda e, hf=hf, psl=psl: e.activation(out=self.HT[psl, 2 * g:2 * g + 2, t * 128:(t + 1) * 128], in_=O[psl, hf * 256:(hf + 1) * 256].rearrange("p (b q) -> p b q", q=128), func=AF.Copy),
                       reads=[self.r_PS[bO]], writes=[self.r_HT[2 * g][t // 8], self.r_HT[2 * g + 1][t // 8]])

        stageA(0, *units[0])
        for ui, (g, t) in enumerate(units):
            if ui + 1 < len(units):
                stageA(ui + 1, *units[ui + 1])
            stageB(ui, g, t)

    def phase_gla_prep(self, l):
        kb = self.kb
        Fc, Fe, Fq, Fs = self.F[0], self.F[1], self.F[2], self.F[3]
        rc, re, rq, rs_ = self.Fres(0), self.Fres(1), self.Fres(2), self.Fres(3)
        Wg, rWg = self.W[0], self.r_W[0]
        Ww, rWw = self.W[1], self.r_W[1]
        glr = Wg[:].rearrange("p a b -> p (a b)")
        w2 = Ww[:].rearrange("p a b -> p (a b)")
        kb.add("pool", lambda e: e.dma_start(out=glr[0:16, :], in_=self.PROJ[P_GLR][0:16, :]), reads=[self.r_PROJ], writes=[rWg], dma=True)
        kb.add("pool", lambda e: e.dma_start(out=w2[0:16, 0:512], in_=self.w2_d[l]), writes=[rWw], dma=True)
        kb.add("dve", lambda e: e.memset(Fs[:, 1024:2048], 0.0), writes=[self.r_F[3][1]])
        dec = lambda h: self.SMALL[:, S_DEC + h * 32:S_DEC + (h + 1) * 32]
        for h in range(4):
            base = h * 5 * T
            QD, KD, KRT, VT = base, base + T, base + 2 * T, base + 3 * T
            for n in range(4):
                kb.add("pe", lambda e, n=n, h=h: e.matmul(self.PS[:, n * 512:(n + 1) * 512], w2[0:16, h * 128:(h + 1) * 128], glr[0:16, n * 512:(n + 1) * 512], start=True, stop=True),
                       reads=[rWg, rWw], writes=[self.r_PS[n]])
            gb = self.GBN[:, l * 4 + h:l * 4 + h + 1]
            kb.add("act", lambda e, gb=gb: e.activation(out=Fe[:], in_=self.PS[:, 0:2048], func=AF.Identity, bias=gb, scale=1.0), reads=[self.r_PS[i] for i in range(4)] + [self.r_GBN], writes=re)
            kb.add("act", lambda e: e.activation(out=Fe[:], in_=Fe[:], func=AF.Exp, scale=-1.0), reads=re, writes=re)
            kb.add("act", lambda e: e.activation(out=Fe[:], in_=Fe[:], func=AF.Ln, bias=1.0), reads=re, writes=re)
            for n in range(32):
                kb.add("dve", lambda e, n=n: e.tensor_tensor_scan(out=Fc[:, n * 64:(n + 1) * 64], data0=self.cst(self.C_ONES, 64), data1=Fe[:, n * 64:(n + 1) * 64], initial=0.0, op0=ALU.mult, op1=ALU.add),
                       reads=re + [self.r_CST], writes=[self.r_F[0][n // 16]])
            c3 = Fc[:].rearrange("p (n t) -> p n t", t=64)
            kb.add("act", lambda e, h=h: e.activation(out=dec(h), in_=c3[:, :, 63], func=AF.Exp, scale=-1.0 / 16), reads=rc, writes=[self.r_SMALL])
            kb.add("dve", lambda e, h=h: e.tensor_reduce(out=self.SMALL[:, S_DTOT + h:S_DTOT + h + 1], in_=c3[:, :, 63], axis=AX.X, op=ALU.add), reads=rc, writes=[self.r_SMALL])
            kb.add("act", lambda e, h=h: e.activation(out=self.SMALL[:, S_DTOT + h:S_DTOT + h + 1], in_=self.SMALL[:, S_DTOT + h:S_DTOT + h + 1], func=AF.Exp, scale=-1.0 / 16), reads=[self.r_SMALL], writes=[self.r_SMALL])
            kb.add("sp", lambda e, h=h: e.dma_start(out=Fq[:], in_=self.PROJ[P_QG + h]), reads=[self.r_PROJ], writes=rq, dma=True, key="glaQ")
            kb.add("act", lambda e: e.activation(out=Fe[:], in_=Fc[:], func=AF.Exp, scale=-1.0 / 16), reads=rc, writes=re)
            kb.add("dve", lambda e, QD=QD: e.scalar_tensor_tensor(out=self.bigf(QD, T), in0=Fq[:], scalar=float(128 ** -0.5), in1=Fe[:], op0=ALU.mult, op1=ALU.mult), reads=rq + re, writes=self.bigres(QD, T))
            kb.add("sp", lambda e, h=h: e.dma_start(out=Fq[:], in_=self.PROJ[P_KG + h]), reads=[self.r_PROJ], writes=rq, dma=True, key="glaQ")
            kb.add("act", lambda e: e.activation(out=Fe[:], in_=Fc[:], func=AF.Exp, scale=1.0 / 16), reads=rc, writes=re)
            kb.add("dve", lambda e, KD=KD: e.tensor_tensor(out=self.bigf(KD, T), in0=Fq[:], in1=Fe[:], op=ALU.mult), reads=rq + re, writes=self.bigres(KD, T))
            kb.add("dve", lambda e: e.tensor_tensor(out=c3, in0=c3, in1=c3[:, :, 63:64].to_broadcast([128, 32, 64]), op=ALU.subtract), reads=rc, writes=rc)
            kb.add("act", lambda e: e.activation(out=Fe[:], in_=Fc[:], func=AF.Exp, scale=1.0 / 16), reads=rc, writes=re)
            kb.add("dve", lambda e: e.tensor_tensor(out=self.bigf(O_TMP2, T), in0=Fq[:], in1=Fe[:], op=ALU.mult), reads=rq + re, writes=self.bigres(O_TMP2, T))
            for half in range(2):
                bank = 4 + half
                pst = self.PS[:, bank * 512:(bank + 1) * 512].bitcast(BF16)
                for tt in range(8):
                    t = half * 8 + tt
                    kb.add("pe", lambda e, t=t, tt=tt, pst=pst: e.transpose(out=pst[:, tt * 128:(tt + 1) * 128], in_=self.bigf(O_TMP2 + t * 128, 128), identity=self.IDB[:]),
                           reads=self.bigres(O_TMP2 + t * 128, 128) + [self.r_cb], writes=[self.r_PS[bank]])
                kb.add("act", lambda e, pst=pst, KRT=KRT, half=half: e.activation(out=self.bigf(KRT + half * 1024, 1024), in_=pst, func=AF.Copy), reads=[self.r_PS[bank]], writes=self.bigres(KRT + half * 1024, 1024))
            for e2 in range(2):
                kb.add("sp", lambda e, h=h, e2=e2: e.dma_start(out=Fq[:], in_=self.PROJ[P_VG + 2 * h + e2]), reads=[self.r_PROJ], writes=rq, dma=True, key="glaQ")
                kb.add("act", lambda e: e.activation(out=self.bigf(O_TMP2, T), in_=Fq[:], func=AF.Copy), reads=rq, writes=self.bigres(O_TMP2, T))
                for half in range(2):
                    bank = 6 + half
                    pst = self.PS[:, bank * 512:(bank + 1) * 512].bitcast(BF16)
                    for tt in range(8):
                        t = half * 8 + tt
                        kb.add("pe", lambda e, t=t, tt=tt, pst=pst: e.transpose(out=pst[:, tt * 128:(tt + 1) * 128], in_=self.bigf(O_TMP2 + t * 128, 128), identity=self.IDB[:]),
                               reads=self.bigres(O_TMP2 + t * 128, 128) + [self.r_cb], writes=[self.r_PS[bank]])
                    lo = VT + half * 8 * 256 + e2 * 128
                    dst = self.BIGF[:, lo:lo + 8 * 256].rearrange("p (t c) -> p t c", c=256)[:, :, 0:128]
                    kb.add("act", lambda e, dst=dst, pst=pst: e.activation(out=dst, in_=pst.rearrange("p (t c) -> p t c", c=128), func=AF.Copy),
                           reads=[self.r_PS[bank]], writes=self.bigres(VT + half * 8 * 256, 8 * 256))
            Sm = Fs[:, 1024 + h * 256:1024 + (h + 1) * 256]
            for n in range(32):
                tile, par = n // 2, n % 2
                psl = slice(par * 64, par * 64 + 64)
                bank = 2 + (n % 2)
                dS = self.PS[:, bank * 512:bank * 512 + 256]
                kb.add("pe", lambda e, tile=tile, psl=psl, dS=dS, KRT=KRT, VT=VT: e.matmul(dS, self.BIGF[psl, KRT + tile * 128:KRT + tile * 128 + 128], self.BIGF[psl, VT + tile * 256:VT + tile * 256 + 256], start=True, stop=True),
                       reads=self.bigres(KRT + tile * 128, 128) + self.bigres(VT + tile * 256, 256), writes=[self.r_PS[bank]])
                kb.add("dve", lambda e, Sm=Sm, dS=dS, n=n, h=h: e.scalar_tensor_tensor(out=Sm, in0=Sm, scalar=dec(h)[:, n:n + 1], in1=dS, op0=ALU.mult, op1=ALU.add),
                       reads=[self.r_PS[bank], self.r_SMALL, self.r_F[3][1]], writes=[self.r_F[3][1]])
        kb.add("sp", lambda e: e.dma_start(out=self.Ssrc[:, 0:1024], in_=Fs[:, 1024:2048]), reads=[self.r_F[3][1]], writes=[self.r_Ssrc], dma=True)
        kb.add("sp", lambda e: e.dma_start(out=self.Ssrc[:, 1024:1028], in_=self.SMALL[:, S_DTOT:S_DTOT + 4]), reads=[self.r_SMALL], writes=[self.r_Ssrc], dma=True, par=True)
        kb.add("pool", lambda e: e.collective_compute("AllGather", ALU.bypass, replica_groups=[[0, 1, 2, 3], [4, 5, 6, 7]], ins=[self.Ssrc], outs=[self.Sdst]),
               reads=[self.r_Ssrc], writes=[self.r_Sdst], dma=True, key="cc2", inc=1)

    def phase_gla_out(self, l):
        kb = self.kb
        F0, F1, F2, F3 = self.F
        GA = F0[:].rearrange("p (r c) -> p r c", c=1024)
        GB2 = F1[:, 0:1024]
        GD = F1[:, 1024:1036]
        rG = self.Fres(0) + self.Fres(1)
        kb.add("sp", lambda e: e.dma_start(out=GA, in_=self.Sdst[0:256, 0:1024].rearrange("(r p) c -> p r c", p=128)), reads=[self.r_Sdst], writes=self.Fres(0), dma=True, key="gS0")
        kb.add("sp", lambda e: e.dma_start(out=GB2, in_=self.Sdst[256:384, 0:1024]), reads=[self.r_Sdst], writes=[self.r_F[1][0]], dma=True, key="gS1")
        kb.add("sp", lambda e: e.dma_start(out=GD.rearrange("p (r c) -> p r c", c=4), in_=self.Sdst[0:384, 1024:1028].rearrange("(r p) c -> p r c", p=128)), reads=[self.r_Sdst], writes=[self.r_F[1][1]], dma=True, key="gS2")
        Sin = F3[:, 1024:2048]
        rSin = [self.r_F[3][1]]
        Tt = F3[:, 0:1024]
        rTt = [self.r_F[3][0]]
        lt = lambda r: self.CORE[:, 516 + r:517 + r]
        kb.add("dve", lambda e: e.memset(Sin, 0.0), writes=rSin)
        for r in range(3):
            Sr = GA[:, r, :] if r < 2 else GB2
            for h in range(4):
                hs = slice(h * 256, (h + 1) * 256)
                kb.add("dve", lambda e, r=r, h=h, hs=hs, Sr=Sr: e.scalar_tensor_tensor(out=Tt[:, hs], in0=Sin[:, hs], scalar=GD[:, r * 4 + h:r * 4 + h + 1], in1=Sr[:, hs], op0=ALU.mult, op1=ALU.add),
                       reads=rG + rSin, writes=rTt)
            kb.add("dve", lambda e: e.tensor_tensor(out=Tt, in0=Tt, in1=Sin, op=ALU.subtract), reads=rTt + rSin, writes=rTt)
            kb.add("dve", lambda e, r=r: e.scalar_tensor_tensor(out=Sin, in0=Tt, scalar=lt(r), in1=Sin, op0=ALU.mult, op1=ALU.add), reads=rTt + rSin + [self.r_CORE], writes=rSin)
        dec = lambda h: self.SMALL[:, S_DEC + h * 32:S_DEC + (h + 1) * 32]
        gmask = self.cst(self.C_GMASK, 128)
        for bi in range(8):
            for hf in range(2):
                sl = slice(hf * 1024, (hf + 1) * 1024)
                R, rR = self.F[bi % 2][:, sl], [self.r_F[bi % 2][hf]]
                kb.add("sp", lambda e, R=R, bi=bi, sl=sl: e.dma_start(out=R, in_=self.PROJ[P_RG + bi][:, sl]), reads=[self.r_PROJ], writes=rR, dma=True)
                kb.add("act", lambda e, R=R, bi=bi, sl=sl: e.activation(out=self.HT[:, 8 + bi, sl], in_=R, func=AF.Silu), reads=rR, writes=[self.r_HT[8 + bi][hf]])
        import os
        GL2 = int(os.environ.get("GL2", "99"))
        if GL2 < 1:
            return
        O_SB2, O_AM2, O_SQ2 = 40960, 43008, 43520
        rr = lambda n: [kb.res(f"g2{n}{h}") for h in range(4)]
        r_bA, r_bB = rr("bA"), rr("bB")
        r_I, r_dSl, r_St, r_Op, r_dSh = r_bA, r_bA, r_bA, r_bB, r_bB
        r_rs, r_tmp = rr("rs"), rr("tmp")
        bankA = lambda h: self.PS[:, (2 * h) * 512:(2 * h + 1) * 512]
        bankB = lambda h: self.PS[:, (2 * h + 1) * 512:(2 * h + 2) * 512]
        I_ps = lambda h: bankA(h)[:, 0:128]
        dS_ps = lambda h, par: bankA(h)[:, 128:384] if par == 0 else bankB(h)[:, 256:512]
        r_dS = lambda h, par: r_dSl[h] if par == 0 else r_dSh[h]
        St_ps = lambda h: bankA(h)[:, 384:512]
        Op_ps = lambda h: bankB(h)[:, 0:256]
        Sm = lambda h: Sin[:, h * 256:(h + 1) * 256]
        Sb = lambda h: self.bigf(O_SB2 + h * 256, 256)
        rSb = lambda h: self.bigres(O_SB2 + h * 256, 256)
        AM = lambda h: self.bigf(O_AM2 + h * 128, 128)
        rAM = lambda h: self.bigres(O_AM2 + h * 128, 128)
        SQ = lambda h: self.bigf(O_SQ2 + h * 256, 256)
        rSQ = lambda h: self.bigres(O_SQ2 + h * 256, 256)
        rsp = lambda h: F2[:, h * 128:(h + 1) * 128]
        tmp = lambda h, e2: F2[:, 512 + (2 * h + e2) * 128:512 + (2 * h + e2 + 1) * 128]
        base = lambda h: h * 5 * T
        QD = lambda h: base(h); KD = lambda h: base(h) + T; KRT = lambda h: base(h) + 2 * T; VT = lambda h: base(h) + 3 * T
        def fence(rlist):
            kb.add("dve", lambda e: e.memset(self.SMALL[:, 255:256], 0.0), reads=rlist, writes=rlist)
        sub = r_bA + r_bB + r_rs + r_tmp + self.r_Sm
        fence(sub + self.r_PS + self.Fres(2) + rSin)
        for h in range(4):
            kb.add("act", lambda e, h=h: e.activation(out=Sb(h), in_=Sm(h), func=AF.Copy), reads=[self.r_Sm[h]], writes=rSb(h))
        for m in range(min(16, GL2 - 1)):
            tok = slice(m * 128, (m + 1) * 128)
            for h in range(4):
                kb.add("pe", lambda e, h=h, m=m: e.matmul(I_ps(h), self.bigf(KD(h) + m * 128, 128), self.bigf(QD(h) + m * 128, 128), start=True, stop=True),
                       reads=self.bigres(KD(h) + m * 128, 128) + self.bigres(QD(h) + m * 128, 128), writes=[r_I[h]])
            for h in range(4):
                kb.add("dve", lambda e, h=h: e.tensor_tensor(out=AM(h), in0=I_ps(h), in1=gmask, op=ALU.mult), reads=[r_I[h], self.r_CST], writes=rAM(h))
            for par in range(2):
                n = 2 * m + par
                psl = slice(par * 64, par * 64 + 64)
                for h in range(4):
                    for e2 in range(2):
                        oc = Op_ps(h)[:, e2 * 128 + par * 64:e2 * 128 + (par + 1) * 64]
                        kb.add("pe", lambda e, oc=oc, e2=e2, par=par, h=h, m=m: e.matmul(oc, self.bigf(VT(h) + m * 256 + e2 * 128, 128), AM(h)[:, par * 64:(par + 1) * 64], start=True, stop=False),
                               reads=self.bigres(VT(h) + m * 256, 256) + rAM(h), writes=[r_Op[h]])
                        kb.add("pe", lambda e, oc=oc, e2=e2, par=par, h=h, m=m: e.matmul(oc, Sb(h)[:, e2 * 128:(e2 + 1) * 128], self.bigf(QD(h) + m * 128 + par * 64, 64), start=False, stop=True),
                               reads=rSb(h) + self.bigres(QD(h) + m * 128, 128), writes=[r_Op[h]])
                for h in range(4):
                    kb.add("pe", lambda e, psl=psl, h=h, m=m, par=par: e.matmul(dS_ps(h, par), self.BIGF[psl, KRT(h) + m * 128:KRT(h) + m * 128 + 128], self.BIGF[psl, VT(h) + m * 256:VT(h) + m * 256 + 256], start=True, stop=True),
                           reads=self.bigres(KRT(h) + m * 128, 128) + self.bigres(VT(h) + m * 256, 256), writes=[r_dS(h, par)])
                for h in range(4):
                    kb.add("dve", lambda e, h=h, n=n, par=par: e.scalar_tensor_tensor(out=Sm(h), in0=Sm(h), scalar=dec(h)[:, n:n + 1], in1=dS_ps(h, par), op0=ALU.mult, op1=ALU.add),
                           reads=[r_dS(h, par), self.r_SMALL, self.r_Sm[h]], writes=[self.r_Sm[h]])
                for h in range(4):
                    kb.add("act", lambda e, h=h: e.activation(out=Sb(h), in_=Sm(h), func=AF.Copy), reads=[self.r_Sm[h]], writes=rSb(h))
            for h in range(4):
                kb.add("act", lambda e, h=h: e.activation(out=SQ(h), in_=Op_ps(h), func=AF.Square), reads=[r_Op[h]], writes=rSQ(h))
            for h in range(4):
                for e2 in range(2):
                    kb.add("pe", lambda e, h=h, e2=e2: e.matmul(St_ps(h), self.ONESB[:], SQ(h)[:, e2 * 128:(e2 + 1) * 128], start=(e2 == 0), stop=(e2 == 1)), reads=rSQ(h) + [self.r_cb], writes=[r_St[h]])
            for h in range(4):
                kb.add("dve", lambda e, h=h: e.tensor_scalar(out=rsp(h), in0=St_ps(h), scalar1=1.0 / 256, scalar2=EPS, op0=ALU.mult, op1=ALU.add), reads=[r_St[h]], writes=[r_rs[h]])
            for h in range(4):
                kb.add("act", lambda e, h=h: e.activation(out=rsp(h), in_=rsp(h), func=AF.Sqrt), reads=[r_rs[h]], writes=[r_rs[h]])
            for h in range(4):
                kb.add("dve", lambda e, h=h: e.reciprocal(out=rsp(h), in_=rsp(h)), reads=[r_rs[h]], writes=[r_rs[h]])
            for h in range(4):
                for e2 in range(2):
                    gn = self.GBN[:, 4 * self.depth + l * 2 + e2:4 * self.depth + l * 2 + e2 + 1]
                    kb.add("dve", lambda e, h=h, e2=e2, gn=gn: e.scalar_tensor_tensor(out=tmp(h, e2), in0=Op_ps(h)[:, e2 * 128:(e2 + 1) * 128], scalar=gn, in1=rsp(h), op0=ALU.mult, op1=ALU.mult),
                           reads=[r_Op[h], r_rs[h], self.r_GBN], writes=[r_tmp[h]])
            for h in range(4):
                for e2 in range(2):
                    blk = 8 + 2 * h + e2
                    kb.add("pool", lambda e, h=h, e2=e2, blk=blk, tok=tok: e.tensor_tensor(out=self.HT[:, blk, tok], in0=tmp(h, e2), in1=self.HT[:, blk, tok], op=ALU.mult),
                           reads=[r_tmp[h], self.r_HT[blk][m // 8]], writes=[self.r_HT[blk][m // 8]])
        fence(sub + self.r_PS + self.Fres(2) + rSin)

    def mixer(self, l, x_src, r_xs, x_dst, r_xd, nxt):
        self.phase_inproj(l)
        self.phase_attn(l)
        self.phase_gla_prep(l)
        self.phase_gla_out(l)
        ht = lambda k, h: (self.HT[:, k, :], self.r_HT[k][h])
        self.phase_C(self.w_mo[l], NB, ht, None, True, False, self.OUT, self.r_OUT, 1.0)
        self.phase_E(l, 3, x_src, r_xs, x_dst, r_xd, nxt)


D = 2048
T = 2048
NB = 16
DFF = 5504
NFB = 43
FA = 22
EPS = 1e-6
MIXW = 4624


class Prog(MixerMixin):
    def __init__(self, depth=4, dbg=(), with_ffn=True):
        self.dbg = dbg
        DEPTH = depth
        self.depth = depth
        nc = bass.Bass("TRN2", target_bir_lowering=False)
        self.nc = nc
        self.kb = KB(nc)
        kb = self.kb

        def din(name, shape, dt=F32):
            return nc.dram_tensor(name, list(shape), dt, kind="ExternalInput").ap()

        def dscr(name, shape, dt=F32):
            return nc.dram_tensor(name, list(shape), dt).ap()

        self.xT = din("xT", [NB, 128, T])
        self.yT = nc.dram_tensor("yT", [NB, 128, T], F32, kind="ExternalOutput").ap()
        self.pos = din("pos", [1, T], I32)
        self.gcol_d = din("gcol", [128, DEPTH * 6 * NB])
        if with_ffn:
            self.w_in = din("ffn_w_in", [DEPTH, 2, D, 2 * DFF])
            self.w_out = din("ffn_w_out", [DEPTH, 2, DFF, D])
        self.w_mi = din("w_mix_in", [DEPTH, D, MIXW])
        self.w_mo = din("w_mix_out", [DEPTH, D, D])
        self.sinks_d = din("attn_sinks", [DEPTH, 16])
        self.w2_d = din("gla_gate_w2", [DEPTH, 16, 512])
        self.gb_d = din("gb_col", [128, DEPTH * 4])
        self.gn_d = din("gn_col", [128, DEPTH * 2])
        self.cst_d = din("cst", [128, 512])
        self.ident_d = din("ident", [128, 128])
        self.core_d = din("corec", [128, 520])
        self.XA = dscr("XA", [NB, 128, T])
        self.XB = dscr("XB", [NB, 128, T])
        self.PART = dscr("PART", [NB, 128, T])
        self.OUT = dscr("OUT", [NB, 128, T])
        self.r_XA, self.r_XB, self.r_PART, self.r_OUT = [kb.res(n) for n in ("XA", "XB", "PART", "OUT")]
        self.r_xin = kb.res("xin")
        self.r_y = kb.res("yT")
        self.HT = kb.sbuf("HT", [128, NB, T], BF16)
        self.r_HT = [[kb.res(f"HT{j}_{h}") for h in range(2)] for j in range(NB)]
        self.BIGF = kb.sbuf("BIG", [128, FA * T], BF16)
        self.BIG = self.BIGF[:].rearrange("p (a b) -> p a b", b=T)
        self.r_BIG = [[kb.res(f"BIG{j}_{h}") for h in range(2)] for j in range(FA)]
        self.W = [kb.sbuf(f"W{i}", [128, NB, 128], BF16) for i in range(4)]
        self.r_W = [kb.res(f"W{i}") for i in range(4)]
        self.F = [kb.sbuf(f"F{i}", [128, T], F32) for i in range(4)]
        self.r_F = [[kb.res(f"F{i}_{h}") for h in range(2)] for i in range(4)]
        self.CST = kb.sbuf("CST", [128, 512], F32)
        self.r_CST = kb.res("CST")
        self.GCOL = kb.sbuf("GCOL", [128, DEPTH * 6 * NB], F32)
        self.r_GCOL = kb.res("GCOL")
        self.ONESB = kb.sbuf("ONESB", [128, 128], BF16)
        self.IDB = kb.sbuf("IDB", [128, 128], BF16)
        self.r_cb = kb.res("cb")
        self.PS = kb.psum("PS", [128, 8 * 512], F32)
        self.r_PS = [kb.res(f"PS{i}") for i in range(8)]
        self.wslot = 0
        self.stg = 0
        self.load_consts()
        self.mixer_init()

    C_ONES = 0
    C_PERM = 128
    C_GMASK = 256
    C_INVF = 384
    C_SIGN = 385
    def cst(self, c0, n):
        return self.CST[:, c0:c0 + n]

    def load_consts(self):
        kb = self.kb
        kb.add("sp", lambda e: e.dma_start(out=self.CST[:], in_=self.cst_d), writes=[self.r_CST], dma=True)
        kb.add("sp", lambda e: e.dma_start(out=self.GCOL[:], in_=self.gcol_d), writes=[self.r_GCOL], dma=True)
        kb.add("act", lambda e: e.activation(out=self.ONESB[:], in_=self.cst(self.C_ONES, 128), func=AF.Copy), reads=[self.r_CST], writes=[self.r_cb])
        kb.add("pool", lambda e: e.dma_start(out=self.IDB[:], in_=self.ident_d), writes=[self.r_cb], dma=True, key="identld")

    def gcol(self, l, i, j):
        c = (l * 6 + i) * NB + j
        return self.GCOL[:, c:c + 1]

    def stats_to_rstd(self, fi, c):
        kb = self.kb
        Fb = self.F[fi]
        rF = self.r_F[fi]
        for n in range(4):
            h = n // 2
            kb.add("pe", lambda e, n=n: e.matmul(self.PS[:, n * 512:(n + 1) * 512], self.cst(self.C_ONES, 128), Fb[:, n * 512:(n + 1) * 512], start=True, stop=True),
                   reads=[rF[h], self.r_CST], writes=[self.r_PS[n]])
        for h in range(2):
            sl = slice(h * 1024, (h + 1) * 1024)
            kb.add("dve", lambda e, sl=sl: e.tensor_scalar(out=Fb[:, sl], in0=self.PS[:, sl], scalar1=1.0 / (D * c * c), scalar2=EPS / (c * c), op0=ALU.mult, op1=ALU.add),
                   reads=[self.r_PS[2 * h], self.r_PS[2 * h + 1]], writes=[rF[h]])
            kb.add("act", lambda e, sl=sl: e.activation(out=Fb[:, sl], in_=Fb[:, sl], func=AF.Sqrt), reads=[rF[h]], writes=[rF[h]])
            kb.add("dve", lambda e, sl=sl: e.reciprocal(out=Fb[:, sl], in_=Fb[:, sl]), reads=[rF[h]], writes=[rF[h]])

    def next_stg(self):
        s = self.stg
        self.stg = (s + 1) % 4
        fi, h = 2 + s // 2, s % 2
        return self.F[fi][:, h * 1024:(h + 1) * 1024], self.r_F[fi][h]

    def phase_prenorm_from(self, x_ap, r_x, l, gi):
        kb = self.kb
        acc = self.F[0]
        kb.add("dve", lambda e: e.memset(acc[:], 0.0), writes=self.r_F[0])
        for j in range(NB):
            for h in range(2):
                sl = slice(h * 1024, (h + 1) * 1024)
                st, rst = self.next_stg()
                kb.add("sp", lambda e, st=st, j=j, sl=sl: e.dma_start(out=st, in_=x_ap[j][:, sl]), reads=[r_x], writes=[rst], dma=True)
                kb.add("act", lambda e, st=st, j=j, sl=sl: e.activation(out=self.HT[:, j, sl], in_=st, func=AF.Copy, scale=self.gcol(l, gi, j)),
                       reads=[rst, self.r_GCOL], writes=[self.r_HT[j][h]])
                kb.add("pool", lambda e, st=st: e.tensor_tensor(out=st, in0=st, in1=st, op=ALU.mult), reads=[rst], writes=[rst])
                kb.add("pool", lambda e, st=st, sl=sl: e.tensor_tensor(out=acc[:, sl], in0=acc[:, sl], in1=st, op=ALU.add), reads=[rst, self.r_F[0][h]], writes=[self.r_F[0][h]])
        self.stats_to_rstd(0, 1.0)

    def load_w(self, src_ap, nk):
        kb = self.kb
        s = self.wslot
        self.wslot = (s + 1) % 4
        Wt = self.W[s]
        kb.add("pool", lambda e: e.dma_start(out=Wt[:, 0:nk, :], in_=src_ap.rearrange("(kc p) m -> p kc m", p=128)), writes=[self.r_W[s]], dma=True)
        return Wt, self.r_W[s]

    def phase_ffn_B(self, l, i, f0, nf):
        kb = self.kb
        rstd = self.F[0]
        unit = getattr(self, "_bunit", 0)
        def ldB(b):
            fb = f0 + b
            return (self.load_w(self.w_in[l, i][:, fb * 128:(fb + 1) * 128], NB),
                    self.load_w(self.w_in[l, i][:, DFF + fb * 128:DFF + (fb + 1) * 128], NB))
        nxtw = ldB(0)
        for b in range(nf):
            (Wg, rWg), (Wu, rWu) = nxtw
            if b + 1 < nf:
                nxtw = ldB(b + 1)
            for h in range(2):
                base = (unit % 2) * 4
                unit += 1
                for (Wt, rW, boff) in ((Wg, rWg, 0), (Wu, rWu, 2)):
                    for n in range(2):
                        bank = base + boff + n
                        tsl = slice(h * 1024 + n * 512, h * 1024 + (n + 1) * 512)
                        for k in range(NB):
                            kb.add("pe", lambda e, Wt=Wt, k=k, tsl=tsl, bank=bank: e.matmul(self.PS[:, bank * 512:(bank + 1) * 512], Wt[:, k, :], self.HT[:, k, tsl], start=(k == 0), stop=(k == NB - 1)),
                                   reads=[rW, self.r_HT[k][h]], writes=[self.r_PS[bank]])
                sl = slice(h * 1024, (h + 1) * 1024)
                g_ps = self.PS[:, base * 512:(base + 2) * 512]
                u_ps = self.PS[:, (base + 2) * 512:(base + 4) * 512]
                s1, r1 = self.next_stg()
                s2, r2 = self.next_stg()
                kb.add("dve", lambda e, s1=s1, g_ps=g_ps, sl=sl: e.tensor_tensor(out=s1, in0=g_ps, in1=rstd[:, sl], op=ALU.mult),
                       reads=[self.r_PS[base], self.r_PS[base + 1], self.r_F[0][h]], writes=[r1])
                kb.add("act", lambda e, s1=s1: e.activation(out=s1, in_=s1, func=AF.Silu), reads=[r1], writes=[r1])
                kb.add("dve", lambda e, s2=s2, u_ps=u_ps, sl=sl: e.tensor_tensor(out=s2, in0=u_ps, in1=rstd[:, sl], op=ALU.mult),
                       reads=[self.r_PS[base + 2], self.r_PS[base + 3], self.r_F[0][h]], writes=[r2])
                kb.add("dve", lambda e, s1=s1, s2=s2, b=b, sl=sl: e.tensor_tensor(out=self.BIG[:, b, sl], in0=s1, in1=s2, op=ALU.mult),
                       reads=[r1, r2], writes=[self.r_BIG[b][h]])
        self._bunit = unit

    def phase_C(self, w_ap, nk, xsrc_blocks, r_src, last, part_in, out_ap, r_out, post_c):
        kb = self.kb
        acc = self.F[1]
        if last:
            kb.add("dve", lambda e: e.memset(acc[:], 0.0), writes=self.r_F[1])
        unit = getattr(self, "_cunit", 0)
        def ldC(j):
            a = self.load_w(w_ap[0:min(nk, NB) * 128, j * 128:(j + 1) * 128], min(nk, NB))
            b_ = self.load_w(w_ap[NB * 128:nk * 128, j * 128:(j + 1) * 128], nk - NB) if nk > NB else (None, None)
            return a, b_
        nxtw = ldC(0)
        for j in range(NB):
            (Wa, rWa), (Wb, rWb) = nxtw
            if j + 1 < NB:
                nxtw = ldC(j + 1)
            for h in range(2):
                base = (unit % 4) * 2
                unit += 1
                sl = slice(h * 1024, (h + 1) * 1024)
                st, rst = self.next_stg()
                if last and part_in:
                    kb.add("sp", lambda e, st=st, j=j, sl=sl: e.dma_start(out=st, in_=self.PART[j][:, sl]), reads=[self.r_PART], writes=[rst], dma=True)
                for n in range(2):
                    bank = base + n
                    tsl = slice(h * 1024 + n * 512, h * 1024 + (n + 1) * 512)
                    for k in range(nk):
                        Wt, rW, kk = (Wa, rWa, k) if k < NB else (Wb, rWb, k - NB)
                        src, rs = xsrc_blocks(k, h)
                        kb.add("pe", lambda e, Wt=Wt, kk=kk, src=src, tsl=tsl, bank=bank, k=k: e.matmul(self.PS[:, bank * 512:(bank + 1) * 512], Wt[:, kk, :], src[:, tsl], start=(k == 0), stop=(k == nk - 1)),
                               reads=[rW, rs], writes=[self.r_PS[bank]])
                ps = self.PS[:, base * 512:(base + 2) * 512]
                rps = [self.r_PS[base], self.r_PS[base + 1]]
                if not last:
                    kb.add("act", lambda e, st=st, ps=ps: e.activation(out=st, in_=ps, func=AF.Copy), reads=rps, writes=[rst])
                    kb.add("sp", lambda e, st=st, j=j, sl=sl: e.dma_start(out=self.PART[j][:, sl], in_=st), reads=[rst], writes=[self.r_PART], dma=True, par=True)
                else:
                    if part_in:
                        kb.add("dve", lambda e, st=st, ps=ps: e.tensor_tensor(out=st, in0=ps, in1=st, op=ALU.add), reads=rps + [rst], writes=[rst])
                    else:
                        kb.add("act", lambda e, st=st, ps=ps: e.activation(out=st, in_=ps, func=AF.Copy), reads=rps, writes=[rst])
                    kb.add("sp", lambda e, st=st, j=j, sl=sl: e.dma_start(out=out_ap[j][:, sl], in_=st), reads=[rst], writes=[r_out], dma=True, par=True)
                    s2, r2 = self.next_stg()
                    kb.add("act", lambda e, st=st, s2=s2: e.activation(out=s2, in_=st, func=AF.Square), reads=[rst], writes=[r2])
                    kb.add("dve", lambda e, s2=s2, sl=sl: e.tensor_tensor(out=acc[:, sl], in0=acc[:, sl], in1=s2, op=ALU.add), reads=[r2, self.r_F[1][h]], writes=[self.r_F[1][h]])
        self._cunit = unit
        if last:
            self.stats_to_rstd(1, post_c)

    def phase_E(self, l, gi_post, x_src, r_xs, x_dst, r_xd, nxt):
        kb = self.kb
        rstd = self.F[1]
        acc = self.F[0]
        if nxt is not None:
            kb.add("dve", lambda e: e.memset(acc[:], 0.0), writes=self.r_F[0])
        units = [(j, h) for j in range(NB) for h in range(2)]

        def loads(j, h):
            sl = slice(h * 1024, (h + 1) * 1024)
            s1, r1 = self.next_stg()
            s2, r2 = self.next_stg()
            kb.add("sp", lambda e, s1=s1, j=j, sl=sl: e.dma_start(out=s1, in_=self.OUT[j][:, sl]), reads=[self.r_OUT], writes=[r1], dma=True)
            kb.add("sp", lambda e, s2=s2, j=j, sl=sl: e.dma_start(out=s2, in_=x_src[j][:, sl]), reads=[r_xs], writes=[r2], dma=True)
            return s1, r1, s2, r2
        pre = loads(*units[0])
        for ui, (j, h) in enumerate(units):
            sl = slice(h * 1024, (h + 1) * 1024)
            s1, r1, s2, r2 = pre
            if ui + 1 < len(units):
                pre = loads(*units[ui + 1])
            kb.add("dve", lambda e, s1=s1, sl=sl: e.tensor_tensor(out=s1, in0=s1, in1=rstd[:, sl], op=ALU.mult), reads=[r1, self.r_F[1][h]], writes=[r1])
            kb.add("dve", lambda e, s1=s1, s2=s2, j=j: e.scalar_tensor_tensor(out=s2, in0=s1, scalar=self.gcol(l, gi_post, j), in1=s2, op0=ALU.mult, op1=ALU.add),
                   reads=[r1, r2, self.r_GCOL], writes=[r2])
            kb.add("sp", lambda e, s2=s2, j=j, sl=sl: e.dma_start(out=x_dst[j][:, sl], in_=s2), reads=[r2], writes=[r_xd], dma=True, par=True)
            if nxt is not None:
                nl, ng = nxt
                kb.add("act", lambda e, s2=s2, j=j, sl=sl, nl=nl, ng=ng: e.activation(out=self.HT[:, j, sl], in_=s2, func=AF.Copy, scale=self.gcol(nl, ng, j)),
                       reads=[r2, self.r_GCOL], writes=[self.r_HT[j][h]])
                kb.add("act", lambda e, s1=s1, s2=s2: e.activation(out=s1, in_=s2, func=AF.Square), reads=[r2], writes=[r1])
                kb.add("pool", lambda e, s1=s1, sl=sl: e.tensor_tensor(out=acc[:, sl], in0=acc[:, sl], in1=s1, op=ALU.add), reads=[r1, self.r_F[0][h]], writes=[self.r_F[0][h]])
        if nxt is not None:
            self.stats_to_rstd(0, 1.0)

    def ffn(self, l, i, x_src, r_xs, x_dst, r_xd, nxt):
        big = lambda k, h: (self.BIG[:, k, :], self.r_BIG[k][h])
        self.phase_ffn_B(l, i, 0, FA)
        self.phase_C(self.w_out[l, i][0:FA * 128, :], FA, big, None, False, False, None, None, None)
        self.phase_ffn_B(l, i, FA, NFB - FA)
        self.phase_C(self.w_out[l, i][FA * 128:DFF, :], NFB - FA, big, None, True, True, self.OUT, self.r_OUT, 0.5)
        self.phase_E(l, 1 + 4 * i, x_src, r_xs, x_dst, r_xd, nxt)

    def finish(self):
        kb = self.kb
        kb.add("sp", lambda e: e.nop(), reads=[self.r_y])
        kb.emit()
        kb.close()
        return self.nc


def make_consts():
    c = np.zeros((128, 512), np.float32)
    c[:, 0:128] = 1.0
    P = np.zeros((128, 128), np.float32)
    for m in range(128):
        k = (m // 64) * 64 + ((m % 64) + 32) % 64
        P[k, m] = 1.0
    c[:, 128:256] = P
    s = np.arange(128)[:, None]; t = np.arange(128)[None, :]
    c[:, 256:384] = ((s // 64 == t // 64) & (s <= t)).astype(np.float32)
    inv_freq = (10000.0 ** (-np.arange(0, 64, 2, dtype=np.float32) / 64)).astype(np.float32)
    d = np.arange(128) % 64
    c[:, 384] = inv_freq[d % 32]
    c[:, 385] = np.where(d < 32, -1.0, 1.0)
    return c

def make_core_consts(rank):
    c = np.zeros((128, 520), np.float32)
    qi = np.arange(128)[:, None]; kj = np.arange(256)[None, :]
    diff = qi + 128 - kj
    band = (diff >= 0) & (diff < 128)
    c[:, 0:256] = np.where(band, 0.0, -30000.0)
    first = band & (kj >= 128) if rank == 0 else band
    c[:, 256:512] = np.where(first, 0.0, -30000.0)
    if rank > 0:
        c[:, 512 + rank - 1] = 1.0
    for r in range(4):
        if r < rank:
            c[:, 516 + r] = 1.0
    return c


def build_full(depth=4):
    p = Prog(depth=depth)
    p.phase_rope_tables()
    p.phase_prenorm_from(p.xT, p.r_xin, 0, 0)
    locs = [(p.XA, p.r_XA), (p.XB, p.r_XB)]
    src = (p.xT, p.r_xin)
    k = 0
    nsub = 3 * depth
    for l in range(depth):
        for sub in range(3):
            dst = (p.yT, p.r_y) if k == nsub - 1 else locs[k % 2]
            if sub == 0:
                p.ffn(l, 0, src[0], src[1], dst[0], dst[1], (l, 2))
            elif sub == 1:
                p.mixer(l, src[0], src[1], dst[0], dst[1], (l, 4))
            else:
                p.ffn(l, 1, src[0], src[1], dst[0], dst[1], (l + 1, 0) if l + 1 < depth else None)
            src = dst
            k += 1
    return p.finish()


def kernel(x, positions, norm_gains, ffn_w_in, ffn_w_out, w_mix_in, attn_sinks, gla_gate_w2, gla_gate_b, gla_norm_gain, w_mix_out):
    x = np.asarray(x, dtype=np.float32)
    positions = np.asarray(positions, dtype=np.int32)
    depth = int(np.asarray(norm_gains).shape[0])
    B, S, _ = x.shape
    f32 = lambda a: np.ascontiguousarray(np.asarray(a, dtype=np.float32))
    shared = dict(
        gcol=np.ascontiguousarray(f32(norm_gains).reshape(depth * 6 * NB, 128).T),
        ffn_w_in=f32(ffn_w_in), ffn_w_out=f32(ffn_w_out), w_mix_in=f32(w_mix_in), w_mix_out=f32(w_mix_out),
        attn_sinks=f32(attn_sinks), gla_gate_w2=f32(gla_gate_w2),
        gb_col=np.ascontiguousarray(f32(gla_gate_b).reshape(depth * 4, 128).T),
        gn_col=np.ascontiguousarray(f32(gla_norm_gain).reshape(depth * 2, 128).T),
        cst=make_consts(), ident=np.eye(128, dtype=np.float32))
    ins = []
    for core in range(8):
        b, r = core // 4, core % 4
        d = dict(shared)
        d["xT"] = np.ascontiguousarray(x[b, r * T:(r + 1) * T].T).reshape(NB, 128, T)
        d["pos"] = np.ascontiguousarray(positions[b, r * T:(r + 1) * T][None])
        d["corec"] = make_core_consts(r)
        ins.append(d)
    nc = build_full(depth)
    res = run_bass_kernel_spmd(nc, ins, core_ids=list(range(8)))
    y = np.empty((B, S, D), np.float32)
    for core in range(8):
        b, r = core // 4, core % 4
        y[b, r * T:(r + 1) * T] = res.results[core]["yT"].reshape(D, T).T
    return y
```

```python
import numpy as np
from contextlib import ExitStack
import concourse.bass as bass
import concourse.mybir as mybir
from concourse.bass_utils import run_bass_kernel_spmd

F32 = mybir.dt.float32
BF16 = mybir.dt.bfloat16
I32 = mybir.dt.int32
AF = mybir.ActivationFunctionType
ALU = mybir.AluOpType
AX = mybir.AxisListType

ENGS = ("pe", "act", "dve", "pool", "sp")


class Res:
    __slots__ = ("name", "ws", "rs", "prs")

    def __init__(self, name):
        self.name = name
        self.ws = []
        self.rs = []
        self.prs = []


class Op:
    __slots__ = ("eng", "fn", "deps", "marked", "val", "dma", "key", "idx", "inc")

    def __init__(self, eng, fn, dma, key, idx, inc=16):
        self.inc = inc
        self.eng = eng
        self.fn = fn
        self.deps = []
        self.marked = False
        self.val = None
        self.dma = dma
        self.key = key
        self.idx = idx


class KB:
    def __init__(self, nc):
        self.nc = nc
        self.ops = {e: [] for e in ENGS}
        self.stack = ExitStack()
        self.nres = 0

    def sbuf(self, name, shape, dtype):
        return self.stack.enter_context(self.nc.sbuf_tensor(name, list(shape), dtype))

    def psum(self, name, shape, dtype):
        return self.stack.enter_context(self.nc.psum_tensor(name, list(shape), dtype))

    def res(self, name=None):
        self.nres += 1
        return Res(name or f"r{self.nres}")

    def add(self, eng, fn, reads=(), writes=(), dma=False, key=None, inc=16, par=False):
        ops = self.ops[eng]
        if dma and key is None:
            key = "dma_" + writes[0].name
        op = Op(eng, fn, dma, key, len(ops), inc)
        deps = []
        for r in reads:
            for o in r.ws:
                deps.append((o, "raw"))
        for w in writes:
            if w.rs:
                for o in w.rs:
                    deps.append((o, "war"))
            elif par:
                for o in w.prs:
                    deps.append((o, "war"))
            else:
                for o in w.ws:
                    deps.append((o, "waw"))
        for d, kind in deps:
            if d is op:
                continue
            if d.eng == eng and not d.dma and not dma:
                if eng == "pe":
                    continue
                if kind != "raw" or (op.idx - d.idx) > 2:
                    continue
            if d.eng == eng and d.dma and dma and False:
                continue
            d.marked = True
            op.deps.append(d)
        for r in reads:
            r.rs.append(op)
        for w in writes:
            if w.rs:
                w.prs = w.rs
                w.rs = []
                w.ws = [op]
            elif par:
                w.ws.append(op)
            else:
                w.ws = [op]
        ops.append(op)
        return op

    def emit(self):
        nc = self.nc
        keys = {}
        for e in ENGS:
            cnt = 0
            for op in self.ops[e]:
                if op.dma:
                    keys[op.key] = keys.get(op.key, 0) + op.inc
                    op.val = keys[op.key]
                elif op.marked:
                    cnt += 1
                    op.val = cnt
            self.nmarks = getattr(self, "nmarks", {})
            self.nmarks[e] = cnt
        EPOCH = 20000
        nep = {e: self.nmarks[e] // EPOCH + 1 for e in ENGS}
        semnames = [f"{e}{i}" for e in ENGS for i in range(nep[e])] + sorted(keys.keys())
        sems = {}
        for n in semnames:
            sems[n] = self.stack.enter_context(nc.semaphore("s_" + n))
        self.nsems = len(semnames)

        def run(engname, engobj):
            seen = {}
            for op in self.ops[engname]:
                need = {}
                for d in op.deps:
                    k = d.key if d.dma else d.eng
                    if d.val > need.get(k, 0):
                        need[k] = d.val
                for k, v in need.items():
                    if seen.get(k, 0) >= v:
                        continue
                    seen[k] = v
                    if k in ENGS:
                        ep, vv = (v - 1) // EPOCH, (v - 1) % EPOCH + 1
                        engobj.wait_ge(sems[f"{k}{ep}"], vv)
                    else:
                        engobj.wait_ge(sems[k], v)
                ins = op.fn(engobj)
                if op.dma:
                    ins.then_inc(sems[op.key], op.inc)
                elif op.marked:
                    ins.then_inc(sems[f"{engname}{(op.val - 1) // EPOCH}"], 1)

        with nc.Block() as block:
            @block.tensor
            def _(e):
                run("pe", e)

            @block.scalar
            def _(e):
                run("act", e)

            @block.vector
            def _(e):
                run("dve", e)

            @block.gpsimd
            def _(e):
                run("pool", e)

            @block.sync
            def _(e):
                run("sp", e)

    def close(self):
        self.stack.close()


NPROJ = 41
P_QA, P_KA, P_VA, P_QG, P_KG, P_VG, P_RG, P_GLR = 0, 8, 12, 16, 20, 24, 32, 40
O_QR, O_KT, O_V2, O_TMPB, O_GKV, O_PN, O_PT = 0, 16384, 25088, 34816, 36864, 40960, 44032
KTW = 2176
O_TMP2, O_AM, O_SQP, O_SB = 40960, 43008, 43264, 43776
S_DEC, S_DTOT, S_ATT = 0, 128, 136
TWO_PI_HI = 6.28125
TWO_PI_LO = 6.283185307179586 - 6.28125
MAGIC = 12582912.0


def _proj_cols(m):
    if m < 8:
        return [(m * 128, 128)]
    if m < 12:
        g = m - 8
        return [(1024 + g * 64, 64)] * 2
    if m < 16:
        g = m - 12
        return [(1280 + g * 64, 64)] * 2
    if m < 20:
        return [(1536 + (m - 16) * 128, 128)]
    if m < 24:
        return [(2048 + (m - 20) * 128, 128)]
    if m < 32:
        return [(2560 + (m - 24) * 128, 128)]
    if m < 40:
        return [(3584 + (m - 32) * 128, 128)]
    return [(4608, 16)]


class MixerMixin:
    def mixer_init(self):
        kb, nc = self.kb, self.nc
        dscr = lambda name, shape, dt=F32: nc.dram_tensor(name, list(shape), dt).ap()
        self.PROJ = dscr("PROJ", [NPROJ, 128, T])
        self.r_PROJ = kb.res("PROJ")
        self.COSD = dscr("COSD", [128, T]); self.SIND = dscr("SIND", [128, T])
        self.r_ROPE = kb.res("ROPE")
        self.KVsrc = dscr("KVsrc", [128, 1024], BF16); self.KVdst = dscr("KVdst", [512, 1024], BF16)
        self.Ssrc = dscr("Ssrc", [128, 1028]); self.Sdst = dscr("Sdst", [512, 1028])
        self.r_KVsrc, self.r_KVdst, self.r_Ssrc, self.r_Sdst = [kb.res(n) for n in ("KVsrc", "KVdst", "Ssrc", "Sdst")]
        self.CORE = kb.sbuf("CORE", [128, 520], F32); self.r_CORE = kb.res("CORE")
        self.SINK = kb.sbuf("SINK", [128, 16], F32); self.r_SINK = kb.res("SINK")
        self.SINKP = kb.sbuf("SINKP", [128, 16], F32); self.r_SINKP = kb.res("SINKP")
        self.GBN = kb.sbuf("GBN", [128, 6 * self.depth], F32); self.r_GBN = kb.res("GBN")
        self.SMALL = kb.sbuf("SMALL", [128, 256], F32); self.r_SMALL = kb.res("SMALL")
        self.r_att = [kb.res("att0"), kb.res("att1"), kb.res("att2")]
        self.r_att2 = [kb.res("att20"), kb.res("att21"), kb.res("att22")]
        self.r_Sm = [kb.res(f"Sm{h}") for h in range(4)]
        kb.add("sp", lambda e: e.dma_start(out=self.CORE[:], in_=self.core_d), writes=[self.r_CORE], dma=True)
        kb.add("sp", lambda e: e.dma_start(out=self.GBN[:, 0:4 * self.depth], in_=self.gb_d), writes=[self.r_GBN], dma=True, key="gbn")
        kb.add("sp", lambda e: e.dma_start(out=self.GBN[:, 4 * self.depth:6 * self.depth], in_=self.gn_d), writes=[self.r_GBN], dma=True, key="gbn", par=True)

    def bigf(self, lo, n):
        return self.BIGF[:, lo:lo + n]

    def bigres(self, lo, n):
        out = []
        for blk in range(lo // 2048, (lo + n - 1) // 2048 + 1):
            for h in range(2):
                a = blk * 2048 + h * 1024
                if a < lo + n and a + 1024 > lo:
                    out.append(self.r_BIG[blk][h])
        return out

    def Fres(self, i):
        return [self.r_F[i][0], self.r_F[i][1]]

    def fence(self, rlist):
        self.kb.add("dve", lambda e: e.memset(self.SMALL[:, 255:256], 0.0), reads=rlist, writes=rlist)

    def phase_rope_tables(self):
        kb = self.kb
        A, B_, C_, Pi = self.F[0], self.F[1], self.F[2], self.F[3]
        rA, rB, rC, rP = self.Fres(0), self.Fres(1), self.Fres(2), self.Fres(3)
        invf = self.cst(self.C_INVF, 1); sign = self.cst(self.C_SIGN, 1)
        kb.add("sp", lambda e: e.dma_start(out=Pi[:].bitcast(I32), in_=self.pos.partition_broadcast(128)), writes=rP, dma=True, key="posld")
        kb.add("dve", lambda e: e.tensor_copy(out=A[:], in_=Pi[:].bitcast(I32)), reads=rP, writes=rA)
        kb.add("dve", lambda e: e.tensor_scalar(out=A[:], in0=A[:], scalar1=invf, scalar2=0.0, op0=ALU.mult, op1=ALU.add), reads=rA + [self.r_CST], writes=rA)

        def reduce_sin(src, rsrc, dst, rdst, tmp, rtmp):
            kb.add("dve", lambda e: e.tensor_scalar(out=tmp[:], in0=src[:], scalar1=float(1.0 / (2 * np.pi)), scalar2=MAGIC, op0=ALU.mult, op1=ALU.add), reads=rsrc, writes=rtmp)
            kb.add("dve", lambda e: e.tensor_scalar(out=tmp[:], in0=tmp[:], scalar1=-MAGIC, scalar2=1.0, op0=ALU.add, op1=ALU.mult), reads=rtmp, writes=rtmp)
            kb.add("dve", lambda e: e.scalar_tensor_tensor(out=dst[:], in0=tmp[:], scalar=-TWO_PI_HI, in1=src[:], op0=ALU.mult, op1=ALU.add), reads=rtmp + rsrc, writes=rdst)
            kb.add("dve", lambda e: e.scalar_tensor_tensor(out=dst[:], in0=tmp[:], scalar=-TWO_PI_LO, in1=dst[:], op0=ALU.mult, op1=ALU.add), reads=rtmp + rdst, writes=rdst)
            kb.add("dve", lambda e: e.tensor_scalar(out=dst[:], in0=dst[:], scalar1=3.1415925, scalar2=-3.1415925, op0=ALU.min, op1=ALU.max), reads=rdst, writes=rdst)

        reduce_sin(A, rA, B_, rB, C_, rC)
        kb.add("act", lambda e: e.activation(out=C_[:], in_=B_[:], func=AF.Sin), reads=rB, writes=rC)
        kb.add("dve", lambda e: e.tensor_scalar(out=C_[:], in0=C_[:], scalar1=sign, scalar2=0.0, op0=ALU.mult, op1=ALU.add), reads=rC + [self.r_CST], writes=rC)
        kb.add("sp", lambda e: e.dma_start(out=self.SIND, in_=C_[:]), reads=rC, writes=[self.r_ROPE], dma=True, key="rope")
        kb.add("dve", lambda e: e.tensor_scalar(out=A[:], in0=B_[:], scalar1=float(np.pi / 2), scalar2=0.0, op0=ALU.add, op1=ALU.add), reads=rB, writes=rA)
        reduce_sin(A, rA, B_, rB, Pi, rP)
        kb.add("act", lambda e: e.activation(out=A[:], in_=B_[:], func=AF.Sin), reads=rB, writes=rA)
        kb.add("sp", lambda e: e.dma_start(out=self.COSD, in_=A[:]), reads=rA, writes=[self.r_ROPE], dma=True, key="rope", par=True)

    def phase_inproj(self, l):
        kb = self.kb
        rstd = self.F[0]
        unit = 0

        def ld(m):
            runs = _proj_cols(m)
            s = self.wslot
            self.wslot = (s + 1) % 4
            Wt = self.W[s]
            c = 0
            for ri, (c0, n) in enumerate(runs):
                kb.add("pool", lambda e, c=c, c0=c0, n=n: e.dma_start(out=Wt[:, :, c:c + n], in_=self.w_mi[l][:, c0:c0 + n].rearrange("(kc p) m -> p kc m", p=128)),
                       writes=[self.r_W[s]], dma=True, par=(ri > 0))
                c += n
            return Wt, self.r_W[s], c
        import os
        mlist = [int(v) for v in os.environ.get("MLIST", "").split(",") if v] or list(range(NPROJ))
        nxtw = ld(mlist[0])
        for mi, m in enumerate(mlist):
            Wt, rW, ncol = nxtw
            if mi + 1 < len(mlist):
                nxtw = ld(mlist[mi + 1])
            for h in range(2):
                base = (unit % 4) * 2
                unit += 1
                sl = slice(h * 1024, (h + 1) * 1024)
                for n in range(2):
                    bank = base + n
                    tsl = slice(h * 1024 + n * 512, h * 1024 + (n + 1) * 512)
                    for k in range(NB):
                        kb.add("pe", lambda e, Wt=Wt, k=k, tsl=tsl, bank=bank, ncol=ncol: e.matmul(self.PS[0:ncol, bank * 512:(bank + 1) * 512], Wt[:, k, 0:ncol], self.HT[:, k, tsl], start=(k == 0), stop=(k == NB - 1)),
                               reads=[rW, self.r_HT[k][h]], writes=[self.r_PS[bank]])
                st, rst = self.next_stg()
                kb.add("dve", lambda e, st=st, base=base, sl=sl, ncol=ncol: e.tensor_tensor(out=st[0:ncol, :], in0=self.PS[0:ncol, base * 512:(base + 2) * 512], in1=rstd[0:ncol, sl], op=ALU.mult),
                       reads=[self.r_PS[base], self.r_PS[base + 1], self.r_F[0][h]], writes=[rst])
                kb.add("sp", lambda e, st=st, m=m, sl=sl, ncol=ncol: e.dma_start(out=self.PROJ[m][0:ncol, sl], in_=st[0:ncol, :]), reads=[rst], writes=[self.r_PROJ], dma=True, par=True)

    def rope_block(self, m, dst_lo, COS, SIN, rT):
        kb = self.kb
        for h in range(2):
            sl = slice(h * 1024, (h + 1) * 1024)
            X, rX = self.F[2][:, sl], [self.r_F[2][h]]
            Tm, rTm = self.F[3][:, sl], [self.r_F[3][h]]
            kb.add("sp", lambda e, X=X, sl=sl: e.dma_start(out=X, in_=self.PROJ[m][:, sl]), reads=[self.r_PROJ], writes=rX, dma=True)
            pb = 4 + 2 * h
            for n in range(2):
                kb.add("pe", lambda e, n=n, X=X, pb=pb: e.matmul(self.PS[:, (pb + n) * 512:(pb + n + 1) * 512], self.cst(self.C_PERM, 128), X[:, n * 512:(n + 1) * 512], start=True, stop=True),
                       reads=rX + [self.r_CST], writes=[self.r_PS[pb + n]])
            kb.add("dve", lambda e, Tm=Tm, pb=pb, sl=sl: e.tensor_tensor(out=Tm, in0=self.PS[:, pb * 512:(pb + 2) * 512], in1=SIN[:, sl], op=ALU.mult), reads=[self.r_PS[pb], self.r_PS[pb + 1]] + rT, writes=rTm)
            kb.add("pool", lambda e, X=X, sl=sl: e.tensor_tensor(out=X, in0=X, in1=COS[:, sl], op=ALU.mult), reads=rX + rT, writes=rX)
            kb.add("dve", lambda e, X=X, Tm=Tm, h=h: e.tensor_tensor(out=self.bigf(dst_lo + h * 1024, 1024), in0=X, in1=Tm, op=ALU.add), reads=rX + rTm, writes=self.bigres(dst_lo + h * 1024, 1024))

    def phase_attn(self, l):
        kb = self.kb
        COS, SIN = self.F[0], self.F[1]
        rT = self.Fres(0) + self.Fres(1)
        kb.add("sp", lambda e: e.dma_start(out=COS[:], in_=self.COSD), reads=[self.r_ROPE], writes=self.Fres(0), dma=True, key="ldcos")
        kb.add("sp", lambda e: e.dma_start(out=SIN[:], in_=self.SIND), reads=[self.r_ROPE], writes=self.Fres(1), dma=True, key="ldsin")
        kb.add("sp", lambda e: e.dma_start(out=self.SINK[:], in_=self.sinks_d[l:l + 1, :].partition_broadcast(128)), writes=[self.r_SINK], dma=True)
        kb.add("dve", lambda e: e.tensor_copy(out=self.SINKP[:].rearrange("p (g b a) -> p g b a", b=2, a=2), in_=self.SINK[:].rearrange("p (g a b) -> p g b a", a=2, b=2)), reads=[self.r_SINK], writes=[self.r_SINKP])
        for g in range(4):
            self.rope_block(P_KA + g, O_KT + g * KTW + 128, COS, SIN, rT)
        import os
        ATT = int(os.environ.get("ATT", "9"))
        if ATT < 2:
            return
        X = self.F[2]; rX = self.Fres(2)
        for g in range(4):
            kb.add("sp", lambda e, g=g: e.dma_start(out=X[:], in_=self.PROJ[P_VA + g]), reads=[self.r_PROJ], writes=rX, dma=True, key="ropeX")
            kb.add("act", lambda e: e.activation(out=self.bigf(O_TMPB, T), in_=X[:], func=AF.Copy), reads=rX, writes=self.bigres(O_TMPB, T))
            for half in range(2):
                bank = 4 + half
                pst = self.PS[:, bank * 512:(bank + 1) * 512].bitcast(BF16)
                for tt in range(8):
                    t = half * 8 + tt
                    kb.add("pe", lambda e, t=t, tt=tt, pst=pst: e.transpose(out=pst[:, tt * 128:(tt + 1) * 128], in_=self.bigf(O_TMPB + t * 128, 128), identity=self.IDB[:]),
                           reads=self.bigres(O_TMPB + t * 128, 128) + [self.r_cb], writes=[self.r_PS[bank]])
                lo = O_V2 + (1 + half * 8) * 512 + g * 128
                dst = self.BIGF[:, lo:lo + 8 * 512].rearrange("p (t c) -> p t c", c=512)[:, :, 0:128]
                kb.add("act", lambda e, dst=dst, pst=pst: e.activation(out=dst, in_=pst.rearrange("p (t c) -> p t c", c=128), func=AF.Copy),
                       reads=[self.r_PS[bank]], writes=self.bigres(O_V2 + (1 + half * 8) * 512, 8 * 512))
        if ATT < 3:
            return
        for b in range(8):
            self.rope_block(P_QA + b, O_QR + b * T, COS, SIN, rT)
        if ATT < 4:
            return
        for g in range(4):
            kb.add("sp", lambda e, g=g: e.dma_start(out=self.KVsrc[:, g * 128:(g + 1) * 128], in_=self.bigf(O_KT + g * KTW + T, 128)),
                   reads=self.bigres(O_KT + g * KTW + T, 128), writes=[self.r_KVsrc], dma=True, par=(g > 0))
        kb.add("sp", lambda e: e.dma_start(out=self.KVsrc[:, 512:1024], in_=self.bigf(O_V2 + 16 * 512, 512)), reads=self.bigres(O_V2 + 16 * 512, 512), writes=[self.r_KVsrc], dma=True, par=True)
        kb.add("pool", lambda e: e.collective_compute("AllGather", ALU.bypass, replica_groups=[[0, 1, 2, 3], [4, 5, 6, 7]], ins=[self.KVsrc], outs=[self.KVdst]),
               reads=[self.r_KVsrc], writes=[self.r_KVdst], dma=True, key="cc1", inc=1)
        def halo_select():
            G = self.bigf(O_GKV, 4096).rearrange("p (r c) -> p r c", c=1024)
            rG = self.bigres(O_GKV, 4096)
            kb.add("sp", lambda e: e.dma_start(out=G, in_=self.KVdst.rearrange("(r p) c -> p r c", p=128)), reads=[self.r_KVdst], writes=rG, dma=True, key="gkv")
            sel = lambda r: self.CORE[:, 512 + r:513 + r]
            for g in range(4):
                dK = self.bigf(O_KT + g * KTW, 128)
                rK = self.bigres(O_KT + g * KTW, 128)
                for r in range(4):
                    src = G[:, r, g * 128:(g + 1) * 128]
                    if r == 0:
                        kb.add("dve", lambda e, dK=dK, src=src, r=r: e.tensor_scalar(out=dK, in0=src, scalar1=sel(r), scalar2=0.0, op0=ALU.mult, op1=ALU.add), reads=rG + [self.r_CORE], writes=rK)
                    else:
                        kb.add("dve", lambda e, dK=dK, src=src, r=r: e.scalar_tensor_tensor(out=dK, in0=src, scalar=sel(r), in1=dK, op0=ALU.mult, op1=ALU.add), reads=rG + rK + [self.r_CORE], writes=rK)
            dV = self.bigf(O_V2, 512); rV = self.bigres(O_V2, 512)
            for r in range(4):
                src = G[:, r, 512:1024]
                if r == 0:
                    kb.add("dve", lambda e, src=src, r=r: e.tensor_scalar(out=dV, in0=src, scalar1=sel(r), scalar2=0.0, op0=ALU.mult, op1=ALU.add), reads=rG + [self.r_CORE], writes=rV)
                else:
                    kb.add("dve", lambda e, src=src, r=r: e.scalar_tensor_tensor(out=dV, in0=src, scalar=sel(r), in1=dV, op0=ALU.mult, op1=ALU.add), reads=rG + rV + [self.r_CORE], writes=rV)
        kb.add("dve", lambda e: e.tensor_reduce(out=self.SMALL[:, 232:236], in_=self.SINK[:].rearrange("p (g j) -> p g j", j=4), axis=AX.X, op=ALU.max), reads=[self.r_SINK], writes=[self.r_SMALL])
        smax = lambda g: self.SMALL[:, 232 + g:233 + g]
        units = [(g, t) for g in range(4) for t in range(1, 16)] + [(g, 0) for g in range(4)]
        first_halo_unit = 60

        ctx = {}
        T_BANK, O_BANK = 6, 7

        def s0(ui, g, t):
            if ui == first_halo_unit:
                halo_select()
            u = ui % 3
            bS = u * 2
            S = self.PS[:, bS * 512:(bS + 2) * 512]
            rS = [self.r_PS[bS], self.r_PS[bS + 1]]
            for j in range(4):
                blk, hf = 2 * g + j // 2, j % 2
                psl = slice(hf * 64, hf * 64 + 64)
                qlo = O_QR + blk * T + t * 128
                klo = O_KT + g * KTW + t * 128
                pj = (j % 2) * 2 + j // 2
                kb.add("pe", lambda e, pj=pj, psl=psl, qlo=qlo, klo=klo, S=S: e.matmul(S[:, pj * 256:(pj + 1) * 256], self.BIGF[psl, qlo:qlo + 128], self.BIGF[psl, klo:klo + 256], start=True, stop=True),
                       reads=self.bigres(qlo, 128) + self.bigres(klo, 256), writes=[rS[pj // 2]])
            st0 = S_ATT + u * 32
            c = dict(u=u, S=S, rS=rS, sml=(lambda a, st0=st0: self.SMALL[:, st0 + a * 4:st0 + a * 4 + 4]), rsml=self.r_att[u])
            ctx[ui] = c

        def s1(ui, g, t):
            c = ctx[ui]
            sm, rsm = self.next_stg()
            c["sm"], c["rsm"] = sm, rsm
            S, rS, sml, rsml = c["S"], c["rS"], c["sml"], c["rsml"]
            mask = self.CORE[:, 256:512] if t == 0 else self.CORE[:, 0:256]
            sm3 = sm.rearrange("p (j k) -> p j k", k=256)
            c["sm3"] = sm3
            S3 = S.rearrange("p (j k) -> p j k", k=256)
            kb.add("dve", lambda e: e.scalar_tensor_tensor(out=sm3, in0=S3, scalar=0.125, in1=mask.unsqueeze(1).to_broadcast([128, 4, 256]), op0=ALU.mult, op1=ALU.add),
                   reads=rS + [self.r_CORE], writes=[rsm])
            kb.add("dve", lambda e: e.tensor_reduce(out=sml(0)[:, 0:1], in_=sm, axis=AX.X, op=ALU.max), reads=[rsm], writes=[rsml])
            kb.add("dve", lambda e: e.tensor_scalar(out=sml(1)[:, 0:1], in0=sml(0)[:, 0:1], scalar1=smax(g), scalar2=-1.0, op0=ALU.max, op1=ALU.mult), reads=[rsml, self.r_SMALL], writes=[rsml])

        def s2(ui, g, t):
            c = ctx[ui]
            sm, rsm, sml, rsml, u = c["sm"], c["rsm"], c["sml"], c["rsml"], c["u"]
            kb.add("act", lambda e: e.activation(out=sm, in_=sm, func=AF.Exp, bias=sml(1)[:, 0:1]), reads=[rsm, rsml], writes=[rsm])
            kb.add("act", lambda e: e.activation(out=sml(3), in_=self.SINKP[:, 4 * g:4 * g + 4], func=AF.Exp, bias=sml(1)[:, 0:1]), reads=[rsml, self.r_SINKP], writes=[self.r_att2[u]])

        def s3(ui, g, t):
            c = ctx[ui]
            sm3, rsm, sml, rsml, u = c["sm3"], c["rsm"], c["sml"], c["rsml"], c["u"]
            kb.add("dve", lambda e: e.tensor_reduce(out=sml(2), in_=sm3, axis=AX.X, op=ALU.add), reads=[rsm], writes=[rsml])
            kb.add("dve", lambda e: e.tensor_tensor(out=sml(4), in0=sml(2), in1=sml(3), op=ALU.add), reads=[rsml, self.r_att2[u]], writes=[rsml])
            kb.add("dve", lambda e: e.reciprocal(out=sml(4), in_=sml(4)), reads=[rsml], writes=[rsml])
            pn = self.bigf(O_PN + u * 1024, 1024); rpn = self.bigres(O_PN + u * 1024, 1024)
            c["pn"], c["rpn"] = pn, rpn
            kb.add("pool", lambda e: e.tensor_tensor(out=pn.rearrange("p (j k) -> p j k", k=256), in0=sm3, in1=sml(4).unsqueeze(2).to_broadcast([128, 4, 256]), op=ALU.mult),
                   reads=[rsm, rsml], writes=rpn)

        pst = self.PS[:, T_BANK * 512:(T_BANK + 1) * 512].bitcast(BF16)
        pT = self.bigf(O_PT, 1024); rpT = self.bigres(O_PT, 1024)
        O = self.PS[:, O_BANK * 512:(O_BANK + 1) * 512]

        def s4(ui, g, t):
            c = ctx[ui]
            pn, rpn = c["pn"], c["rpn"]
            for jc in range(8):
                kb.add("pe", lambda e, jc=jc: e.transpose(out=pst[:, jc * 128:(jc + 1) * 128], in_=pn[:, jc * 128:(jc + 1) * 128], identity=self.IDB[:]),
                       reads=rpn + [self.r_cb], writes=[self.r_PS[T_BANK]])

        def s5(ui, g, t):
            kb.add("act", lambda e: e.activation(out=pT, in_=pst, func=AF.Copy), reads=[self.r_PS[T_BANK]], writes=rpT)

        def s6(ui, g, t):
            for j in range(4):
                for c_ in range(2):
                    vlo = O_V2 + (t + c_) * 512 + g * 128
                    kb.add("pe", lambda e, j=j, c_=c_, vlo=vlo: e.matmul(O[:, j * 128:(j + 1) * 128], self.bigf(vlo, 128), pT[:, (2 * j + c_) * 128:(2 * j + c_ + 1) * 128], start=(c_ == 0), stop=(c_ == 1)),
                           reads=self.bigres(vlo, 128) + rpT, writes=[self.r_PS[O_BANK]])

        def s7(ui, g, t):
            for hf in range(2):
                psl = slice(hf * 64, hf * 64 + 64)
                kb.add("act", lambda e, hf=hf, psl=psl: e.activation(out=self.HT[psl, 2 * g:2 * g + 2, t * 128:(t + 1) * 128], in_=O[psl, hf * 256:(hf + 1) * 256].rearrange("p (b q) -> p b q", q=128), func=AF.Copy),
                       reads=[self.r_PS[O_BANK]], writes=[self.r_HT[2 * g][t // 8], self.r_HT[2 * g + 1][t // 8]])
            ctx.pop(ui)

        stages = [s0, s1, s2, s3, s4, s5, s6, s7]
        nu = len(units)
        for i in range(nu + 7):
            for s in range(7, -1, -1):
                ui = i - s
                if 0 <= ui < nu:
                    stages[s](ui, *units[ui])

    def phase_gla_prep(self, l):
        kb = self.kb
        Fc, Fe, Fq, Fs = self.F[0], self.F[1], self.F[2], self.F[3]
        rc, re, rq, rs_ = self.Fres(0), self.Fres(1), self.Fres(2), self.Fres(3)
        Wg, rWg = self.W[0], self.r_W[0]
        Ww, rWw = self.W[1], self.r_W[1]
        glr = Wg[:].rearrange("p a b -> p (a b)")
        w2 = Ww[:].rearrange("p a b -> p (a b)")
        kb.add("pool", lambda e: e.dma_start(out=glr[0:16, :], in_=self.PROJ[P_GLR][0:16, :]), reads=[self.r_PROJ], writes=[rWg], dma=True)
        kb.add("pool", lambda e: e.dma_start(out=w2[0:16, 0:512], in_=self.w2_d[l]), writes=[rWw], dma=True)
        kb.add("dve", lambda e: e.memset(Fs[:, 1024:2048], 0.0), writes=[self.r_F[3][1]])
        dec = lambda h: self.SMALL[:, S_DEC + h * 32:S_DEC + (h + 1) * 32]
        for h in range(4):
            base = h * 5 * T
            QD, KD, KRT, VT = base, base + T, base + 2 * T, base + 3 * T
            for n in range(4):
                kb.add("pe", lambda e, n=n, h=h: e.matmul(self.PS[:, n * 512:(n + 1) * 512], w2[0:16, h * 128:(h + 1) * 128], glr[0:16, n * 512:(n + 1) * 512], start=True, stop=True),
                       reads=[rWg, rWw], writes=[self.r_PS[n]])
            gb = self.GBN[:, l * 4 + h:l * 4 + h + 1]
            kb.add("act", lambda e, gb=gb: e.activation(out=Fe[:], in_=self.PS[:, 0:2048], func=AF.Identity, bias=gb, scale=1.0), reads=[self.r_PS[i] for i in range(4)] + [self.r_GBN], writes=re)
            kb.add("act", lambda e: e.activation(out=Fe[:], in_=Fe[:], func=AF.Exp, scale=-1.0), reads=re, writes=re)
            kb.add("act", lambda e: e.activation(out=Fe[:], in_=Fe[:], func=AF.Ln, bias=1.0), reads=re, writes=re)
            for n in range(32):
                kb.add("dve", lambda e, n=n: e.tensor_tensor_scan(out=Fc[:, n * 64:(n + 1) * 64], data0=self.cst(self.C_ONES, 64), data1=Fe[:, n * 64:(n + 1) * 64], initial=0.0, op0=ALU.mult, op1=ALU.add),
                       reads=re + [self.r_CST], writes=[self.r_F[0][n // 16]])
            c3 = Fc[:].rearrange("p (n t) -> p n t", t=64)
            kb.add("act", lambda e, h=h: e.activation(out=dec(h), in_=c3[:, :, 63], func=AF.Exp, scale=-1.0 / 16), reads=rc, writes=[self.r_SMALL])
            kb.add("dve", lambda e, h=h: e.tensor_reduce(out=self.SMALL[:, S_DTOT + h:S_DTOT + h + 1], in_=c3[:, :, 63], axis=AX.X, op=ALU.add), reads=rc, writes=[self.r_SMALL])
            kb.add("act", lambda e, h=h: e.activation(out=self.SMALL[:, S_DTOT + h:S_DTOT + h + 1], in_=self.SMALL[:, S_DTOT + h:S_DTOT + h + 1], func=AF.Exp, scale=-1.0 / 16), reads=[self.r_SMALL], writes=[self.r_SMALL])
            kb.add("sp", lambda e, h=h: e.dma_start(out=Fq[:], in_=self.PROJ[P_QG + h]), reads=[self.r_PROJ], writes=rq, dma=True, key="glaQ")
            kb.add("act", lambda e: e.activation(out=Fe[:], in_=Fc[:], func=AF.Exp, scale=-1.0 / 16), reads=rc, writes=re)
            kb.add("dve", lambda e, QD=QD: e.scalar_tensor_tensor(out=self.bigf(QD, T), in0=Fq[:], scalar=float(128 ** -0.5), in1=Fe[:], op0=ALU.mult, op1=ALU.mult), reads=rq + re, writes=self.bigres(QD, T))
            kb.add("sp", lambda e, h=h: e.dma_start(out=Fq[:], in_=self.PROJ[P_KG + h]), reads=[self.r_PROJ], writes=rq, dma=True, key="glaQ")
            kb.add("act", lambda e: e.activation(out=Fe[:], in_=Fc[:], func=AF.Exp, scale=1.0 / 16), reads=rc, writes=re)
            kb.add("dve", lambda e, KD=KD: e.tensor_tensor(out=self.bigf(KD, T), in0=Fq[:], in1=Fe[:], op=ALU.mult), reads=rq + re, writes=self.bigres(KD, T))
            kb.add("dve", lambda e: e.tensor_tensor(out=c3, in0=c3, in1=c3[:, :, 63:64].to_broadcast([128, 32, 64]), op=ALU.subtract), reads=rc, writes=rc)
            kb.add("act", lambda e: e.activation(out=Fe[:], in_=Fc[:], func=AF.Exp, scale=1.0 / 16), reads=rc, writes=re)
            kb.add("dve", lambda e: e.tensor_tensor(out=self.bigf(O_TMP2, T), in0=Fq[:], in1=Fe[:], op=ALU.mult), reads=rq + re, writes=self.bigres(O_TMP2, T))
            for half in range(2):
                bank = 4 + half
                pst = self.PS[:, bank * 512:(bank + 1) * 512].bitcast(BF16)
                for tt in range(8):
                    t = half * 8 + tt
                    kb.add("pe", lambda e, t=t, tt=tt, pst=pst: e.transpose(out=pst[:, tt * 128:(tt + 1) * 128], in_=self.bigf(O_TMP2 + t * 128, 128), identity=self.IDB[:]),
                           reads=self.bigres(O_TMP2 + t * 128, 128) + [self.r_cb], writes=[self.r_PS[bank]])
                kb.add("act", lambda e, pst=pst, KRT=KRT, half=half: e.activation(out=self.bigf(KRT + half * 1024, 1024), in_=pst, func=AF.Copy), reads=[self.r_PS[bank]], writes=self.bigres(KRT + half * 1024, 1024))
            for e2 in range(2):
                kb.add("sp", lambda e, h=h, e2=e2: e.dma_start(out=Fq[:], in_=self.PROJ[P_VG + 2 * h + e2]), reads=[self.r_PROJ], writes=rq, dma=True, key="glaQ")
                kb.add("act", lambda e: e.activation(out=self.bigf(O_TMP2, T), in_=Fq[:], func=AF.Copy), reads=rq, writes=self.bigres(O_TMP2, T))
                for half in range(2):
                    bank = 6 + half
                    pst = self.PS[:, bank * 512:(bank + 1) * 512].bitcast(BF16)
                    for tt in range(8):
                        t = half * 8 + tt
                        kb.add("pe", lambda e, t=t, tt=tt, pst=pst: e.transpose(out=pst[:, tt * 128:(tt + 1) * 128], in_=self.bigf(O_TMP2 + t * 128, 128), identity=self.IDB[:]),
                               reads=self.bigres(O_TMP2 + t * 128, 128) + [self.r_cb], writes=[self.r_PS[bank]])
                    lo = VT + half * 8 * 256 + e2 * 128
                    dst = self.BIGF[:, lo:lo + 8 * 256].rearrange("p (t c) -> p t c", c=256)[:, :, 0:128]
                    kb.add("act", lambda e, dst=dst, pst=pst: e.activation(out=dst, in_=pst.rearrange("p (t c) -> p t c", c=128), func=AF.Copy),
                           reads=[self.r_PS[bank]], writes=self.bigres(VT + half * 8 * 256, 8 * 256))
        self.fence(self.r_Sm + [self.r_F[3][1]] + self.r_PS)
        for n in range(32):
            tile, par = n // 2, n % 2
            psl = slice(par * 64, par * 64 + 64)
            for h in range(4):
                base = h * 5 * T
                KRT, VT = base + 2 * T, base + 3 * T
                bank = 2 * h + par
                dS = self.PS[:, bank * 512:bank * 512 + 256]
                kb.add("pe", lambda e, tile=tile, psl=psl, dS=dS, KRT=KRT, VT=VT: e.matmul(dS, self.BIGF[psl, KRT + tile * 128:KRT + tile * 128 + 128], self.BIGF[psl, VT + tile * 256:VT + tile * 256 + 256], start=True, stop=True),
                       reads=self.bigres(KRT + tile * 128, 128) + self.bigres(VT + tile * 256, 256), writes=[self.r_PS[bank]])
            for h in range(4):
                bank = 2 * h + par
                dS = self.PS[:, bank * 512:bank * 512 + 256]
                Sm = Fs[:, 1024 + h * 256:1024 + (h + 1) * 256]
                kb.add("dve", lambda e, Sm=Sm, dS=dS, n=n, h=h: e.scalar_tensor_tensor(out=Sm, in0=Sm, scalar=dec(h)[:, n:n + 1], in1=dS, op0=ALU.mult, op1=ALU.add),
                       reads=[self.r_PS[bank], self.r_SMALL, self.r_Sm[h]], writes=[self.r_Sm[h]])
        self.fence(self.r_Sm + [self.r_F[3][1]])
        kb.add("sp", lambda e: e.dma_start(out=self.Ssrc[:, 0:1024], in_=Fs[:, 1024:2048]), reads=[self.r_F[3][1]], writes=[self.r_Ssrc], dma=True)
        kb.add("sp", lambda e: e.dma_start(out=self.Ssrc[:, 1024:1028], in_=self.SMALL[:, S_DTOT:S_DTOT + 4]), reads=[self.r_SMALL], writes=[self.r_Ssrc], dma=True, par=True)
        kb.add("pool", lambda e: e.collective_compute("AllGather", ALU.bypass, replica_groups=[[0, 1, 2, 3], [4, 5, 6, 7]], ins=[self.Ssrc], outs=[self.Sdst]),
               reads=[self.r_Ssrc], writes=[self.r_Sdst], dma=True, key="cc2", inc=1)

    def phase_gla_out(self, l):
        kb = self.kb
        F0, F1, F2, F3 = self.F
        GA = F0[:].rearrange("p (r c) -> p r c", c=1024)
        GB2 = F1[:, 0:1024]
        GD = F1[:, 1024:1036]
        rG = self.Fres(0) + self.Fres(1)
        kb.add("sp", lambda e: e.dma_start(out=GA, in_=self.Sdst[0:256, 0:1024].rearrange("(r p) c -> p r c", p=128)), reads=[self.r_Sdst], writes=self.Fres(0), dma=True, key="gS0")
        kb.add("sp", lambda e: e.dma_start(out=GB2, in_=self.Sdst[256:384, 0:1024]), reads=[self.r_Sdst], writes=[self.r_F[1][0]], dma=True, key="gS1")
        kb.add("sp", lambda e: e.dma_start(out=GD.rearrange("p (r c) -> p r c", c=4), in_=self.Sdst[0:384, 1024:1028].rearrange("(r p) c -> p r c", p=128)), reads=[self.r_Sdst], writes=[self.r_F[1][1]], dma=True, key="gS2")
        Sin = F3[:, 1024:2048]
        rSin = [self.r_F[3][1]]
        Tt = F3[:, 0:1024]
        rTt = [self.r_F[3][0]]
        lt = lambda r: self.CORE[:, 516 + r:517 + r]
        kb.add("dve", lambda e: e.memset(Sin, 0.0), writes=rSin)
        for r in range(3):
            Sr = GA[:, r, :] if r < 2 else GB2
            for h in range(4):
                hs = slice(h * 256, (h + 1) * 256)
                kb.add("dve", lambda e, r=r, h=h, hs=hs, Sr=Sr: e.scalar_tensor_tensor(out=Tt[:, hs], in0=Sin[:, hs], scalar=GD[:, r * 4 + h:r * 4 + h + 1], in1=Sr[:, hs], op0=ALU.mult, op1=ALU.add),
                       reads=rG + rSin, writes=rTt)
            kb.add("dve", lambda e: e.tensor_tensor(out=Tt, in0=Tt, in1=Sin, op=ALU.subtract), reads=rTt + rSin, writes=rTt)
            kb.add("dve", lambda e, r=r: e.scalar_tensor_tensor(out=Sin, in0=Tt, scalar=lt(r), in1=Sin, op0=ALU.mult, op1=ALU.add), reads=rTt + rSin + [self.r_CORE], writes=rSin)
        dec = lambda h: self.SMALL[:, S_DEC + h * 32:S_DEC + (h + 1) * 32]
        gmask = self.cst(self.C_GMASK, 128)
        for bi in range(8):
            for hf in range(2):
                sl = slice(hf * 1024, (hf + 1) * 1024)
                R, rR = self.F[bi % 2][:, sl], [self.r_F[bi % 2][hf]]
                kb.add("sp", lambda e, R=R, bi=bi, sl=sl: e.dma_start(out=R, in_=self.PROJ[P_RG + bi][:, sl]), reads=[self.r_PROJ], writes=rR, dma=True)
                kb.add("act", lambda e, R=R, bi=bi, sl=sl: e.activation(out=self.HT[:, 8 + bi, sl], in_=R, func=AF.Silu), reads=rR, writes=[self.r_HT[8 + bi][hf]])
        import os
        GL2 = int(os.environ.get("GL2", "99"))
        if GL2 < 1:
            return
        O_SB2, O_AM2, O_SQ2 = 40960, 43008, 43520
        rr = lambda n: [kb.res(f"g2{n}{h}") for h in range(4)]
        r_bA, r_bB = rr("bA"), rr("bB")
        r_I, r_dSl, r_St, r_Op, r_dSh = r_bA, r_bA, r_bA, r_bB, r_bB
        r_rs, r_tmp = rr("rs"), rr("tmp")
        bankA = lambda h: self.PS[:, (2 * h) * 512:(2 * h + 1) * 512]
        bankB = lambda h: self.PS[:, (2 * h + 1) * 512:(2 * h + 2) * 512]
        I_ps = lambda h: bankA(h)[:, 0:128]
        dS_ps = lambda h, par: bankA(h)[:, 128:384] if par == 0 else bankB(h)[:, 256:512]
        r_dS = lambda h, par: r_dSl[h] if par == 0 else r_dSh[h]
        St_ps = lambda h: bankA(h)[:, 384:512]
        Op_ps = lambda h: bankB(h)[:, 0:256]
        Sm = lambda h: Sin[:, h * 256:(h + 1) * 256]
        Sb = lambda h: self.bigf(O_SB2 + h * 256, 256)
        rSb = lambda h: self.bigres(O_SB2 + h * 256, 256)
        AM = lambda h: self.bigf(O_AM2 + h * 128, 128)
        rAM = lambda h: self.bigres(O_AM2 + h * 128, 128)
        SQ = lambda h: self.bigf(O_SQ2 + h * 256, 256)
        rSQ = lambda h: self.bigres(O_SQ2 + h * 256, 256)
        rsp = lambda h: F2[:, h * 128:(h + 1) * 128]
        tmp = lambda h, e2: F2[:, 512 + (2 * h + e2) * 128:512 + (2 * h + e2 + 1) * 128]
        base = lambda h: h * 5 * T
        QD = lambda h: base(h); KD = lambda h: base(h) + T; KRT = lambda h: base(h) + 2 * T; VT = lambda h: base(h) + 3 * T
        def fence(rlist):
            kb.add("dve", lambda e: e.memset(self.SMALL[:, 255:256], 0.0), reads=rlist, writes=rlist)
        sub = r_bA + r_bB + r_rs + r_tmp + self.r_Sm
        fence(sub + self.r_PS + self.Fres(2) + rSin)
        for h in range(4):
            kb.add("act", lambda e, h=h: e.activation(out=Sb(h), in_=Sm(h), func=AF.Copy), reads=[self.r_Sm[h]], writes=rSb(h))
        for m in range(min(16, GL2 - 1)):
            tok = slice(m * 128, (m + 1) * 128)
            for h in range(4):
                kb.add("pe", lambda e, h=h, m=m: e.matmul(I_ps(h), self.bigf(KD(h) + m * 128, 128), self.bigf(QD(h) + m * 128, 128), start=True, stop=True),
                       reads=self.bigres(KD(h) + m * 128, 128) + self.bigres(QD(h) + m * 128, 128), writes=[r_I[h]])
            for h in range(4):
                kb.add("dve", lambda e, h=h: e.tensor_tensor(out=AM(h), in0=I_ps(h), in1=gmask, op=ALU.mult), reads=[r_I[h], self.r_CST], writes=rAM(h))
            for par in range(2):
                n = 2 * m + par
                psl = slice(par * 64, par * 64 + 64)
                for h in range(4):
                    for e2 in range(2):
                        oc = Op_ps(h)[:, e2 * 128 + par * 64:e2 * 128 + (par + 1) * 64]
                        kb.add("pe", lambda e, oc=oc, e2=e2, par=par, h=h, m=m: e.matmul(oc, self.bigf(VT(h) + m * 256 + e2 * 128, 128), AM(h)[:, par * 64:(par + 1) * 64], start=True, stop=False),
                               reads=self.bigres(VT(h) + m * 256, 256) + rAM(h), writes=[r_Op[h]])
                        kb.add("pe", lambda e, oc=oc, e2=e2, par=par, h=h, m=m: e.matmul(oc, Sb(h)[:, e2 * 128:(e2 + 1) * 128], self.bigf(QD(h) + m * 128 + par * 64, 64), start=False, stop=True),
                               reads=rSb(h) + self.bigres(QD(h) + m * 128, 128), writes=[r_Op[h]])
                for h in range(4):
                    kb.add("pe", lambda e, psl=psl, h=h, m=m, par=par: e.matmul(dS_ps(h, par), self.BIGF[psl, KRT(h) + m * 128:KRT(h) + m * 128 + 128], self.BIGF[psl, VT(h) + m * 256:VT(h) + m * 256 + 256], start=True, stop=True),
                           reads=self.bigres(KRT(h) + m * 128, 128) + self.bigres(VT(h) + m * 256, 256), writes=[r_dS(h, par)])
                for h in range(4):
                    kb.add("dve", lambda e, h=h, n=n, par=par: e.scalar_tensor_tensor(out=Sm(h), in0=Sm(h), scalar=dec(h)[:, n:n + 1], in1=dS_ps(h, par), op0=ALU.mult, op1=ALU.add),
                           reads=[r_dS(h, par), self.r_SMALL, self.r_Sm[h]], writes=[self.r_Sm[h]])
                for h in range(4):
                    kb.add("act", lambda e, h=h: e.activation(out=Sb(h), in_=Sm(h), func=AF.Copy), reads=[self.r_Sm[h]], writes=rSb(h))
            for h in range(4):
                kb.add("act", lambda e, h=h: e.activation(out=SQ(h), in_=Op_ps(h), func=AF.Square), reads=[r_Op[h]], writes=rSQ(h))
            for h in range(4):
                for e2 in range(2):
                    kb.add("pe", lambda e, h=h, e2=e2: e.matmul(St_ps(h), self.ONESB[:], SQ(h)[:, e2 * 128:(e2 + 1) * 128], start=(e2 == 0), stop=(e2 == 1)), reads=rSQ(h) + [self.r_cb], writes=[r_St[h]])
            for h in range(4):
                kb.add("dve", lambda e, h=h: e.tensor_scalar(out=rsp(h), in0=St_ps(h), scalar1=1.0 / 256, scalar2=EPS, op0=ALU.mult, op1=ALU.add), reads=[r_St[h]], writes=[r_rs[h]])
            for h in range(4):
                kb.add("act", lambda e, h=h: e.activation(out=rsp(h), in_=rsp(h), func=AF.Sqrt), reads=[r_rs[h]], writes=[r_rs[h]])
            for h in range(4):
                kb.add("dve", lambda e, h=h: e.reciprocal(out=rsp(h), in_=rsp(h)), reads=[r_rs[h]], writes=[r_rs[h]])
            for h in range(4):
                for e2 in range(2):
                    gn = self.GBN[:, 4 * self.depth + l * 2 + e2:4 * self.depth + l * 2 + e2 + 1]
                    kb.add("dve", lambda e, h=h, e2=e2, gn=gn: e.scalar_tensor_tensor(out=tmp(h, e2), in0=Op_ps(h)[:, e2 * 128:(e2 + 1) * 128], scalar=gn, in1=rsp(h), op0=ALU.mult, op1=ALU.mult),
                           reads=[r_Op[h], r_rs[h], self.r_GBN], writes=[r_tmp[h]])
            for h in range(4):
                for e2 in range(2):
                    blk = 8 + 2 * h + e2
                    kb.add("pool", lambda e, h=h, e2=e2, blk=blk, tok=tok: e.tensor_tensor(out=self.HT[:, blk, tok], in0=tmp(h, e2), in1=self.HT[:, blk, tok], op=ALU.mult),
                           reads=[r_tmp[h], self.r_HT[blk][m // 8]], writes=[self.r_HT[blk][m // 8]])
        fence(sub + self.r_PS + self.Fres(2) + rSin)

    def mixer(self, l, x_src, r_xs, x_dst, r_xd, nxt):
        self.phase_inproj(l)
        self.phase_attn(l)
        self.phase_gla_prep(l)
        self.phase_gla_out(l)
        ht = lambda k, h: (self.HT[:, k, :], self.r_HT[k][h])
        self.phase_C(self.w_mo[l], NB, ht, None, True, False, self.OUT, self.r_OUT, 1.0)
        self.phase_E(l, 3, x_src, r_xs, x_dst, r_xd, nxt)


D = 2048
T = 2048
NB = 16
DFF = 5504
NFB = 43
FA = 22
EPS = 1e-6
MIXW = 4624


class Prog(MixerMixin):
    def __init__(self, depth=4, dbg=(), with_ffn=True):
        self.dbg = dbg
        DEPTH = depth
        self.depth = depth
        nc = bass.Bass("TRN2", target_bir_lowering=False)
        self.nc = nc
        self.kb = KB(nc)
        kb = self.kb

        def din(name, shape, dt=F32):
            return nc.dram_tensor(name, list(shape), dt, kind="ExternalInput").ap()

        def dscr(name, shape, dt=F32):
            return nc.dram_tensor(name, list(shape), dt).ap()

        self.xT = din("xT", [NB, 128, T])
        self.yT = nc.dram_tensor("yT", [NB, 128, T], F32, kind="ExternalOutput").ap()
        self.pos = din("pos", [1, T], I32)
        self.gcol_d = din("gcol", [128, DEPTH * 6 * NB])
        if with_ffn:
            self.w_in = din("ffn_w_in", [DEPTH, 2, D, 2 * DFF])
            self.w_out = din("ffn_w_out", [DEPTH, 2, DFF, D])
        self.w_mi = din("w_mix_in", [DEPTH, D, MIXW])
        self.w_mo = din("w_mix_out", [DEPTH, D, D])
        self.sinks_d = din("attn_sinks", [DEPTH, 16])
        self.w2_d = din("gla_gate_w2", [DEPTH, 16, 512])
        self.gb_d = din("gb_col", [128, DEPTH * 4])
        self.gn_d = din("gn_col", [128, DEPTH * 2])
        self.cst_d = din("cst", [128, 512])
        self.ident_d = din("ident", [128, 128])
        self.core_d = din("corec", [128, 520])
        self.XA = dscr("XA", [NB, 128, T])
        self.XB = dscr("XB", [NB, 128, T])
        self.PART = dscr("PART", [NB, 128, T])
        self.OUT = dscr("OUT", [NB, 128, T])
        self.r_XA, self.r_XB, self.r_PART, self.r_OUT = [kb.res(n) for n in ("XA", "XB", "PART", "OUT")]
        self.r_xin = kb.res("xin")
        self.r_y = kb.res("yT")
        self.HT = kb.sbuf("HT", [128, NB, T], BF16)
        self.r_HT = [[kb.res(f"HT{j}_{h}") for h in range(2)] for j in range(NB)]
        self.BIGF = kb.sbuf("BIG", [128, FA * T], BF16)
        self.BIG = self.BIGF[:].rearrange("p (a b) -> p a b", b=T)
        self.r_BIG = [[kb.res(f"BIG{j}_{h}") for h in range(2)] for j in range(FA)]
        self.W = [kb.sbuf(f"W{i}", [128, NB, 128], BF16) for i in range(4)]
        self.r_W = [kb.res(f"W{i}") for i in range(4)]
        self.F = [kb.sbuf(f"F{i}", [128, T], F32) for i in range(4)]
        self.r_F = [[kb.res(f"F{i}_{h}") for h in range(2)] for i in range(4)]
        self.CST = kb.sbuf("CST", [128, 512], F32)
        self.r_CST = kb.res("CST")
        self.GCOL = kb.sbuf("GCOL", [128, DEPTH * 6 * NB], F32)
        self.r_GCOL = kb.res("GCOL")
        self.ONESB = kb.sbuf("ONESB", [128, 128], BF16)
        self.IDB = kb.sbuf("IDB", [128, 128], BF16)
        self.r_cb = kb.res("cb")
        self.PS = kb.psum("PS", [128, 8 * 512], F32)
        self.r_PS = [kb.res(f"PS{i}") for i in range(8)]
        self.wslot = 0
        self.stg = 0
        self.load_consts()
        self.mixer_init()

    C_ONES = 0
    C_PERM = 128
    C_GMASK = 256
    C_INVF = 384
    C_SIGN = 385
    def cst(self, c0, n):
        return self.CST[:, c0:c0 + n]

    def load_consts(self):
        kb = self.kb
        kb.add("sp", lambda e: e.dma_start(out=self.CST[:], in_=self.cst_d), writes=[self.r_CST], dma=True)
        kb.add("sp", lambda e: e.dma_start(out=self.GCOL[:], in_=self.gcol_d), writes=[self.r_GCOL], dma=True)
        kb.add("act", lambda e: e.activation(out=self.ONESB[:], in_=self.cst(self.C_ONES, 128), func=AF.Copy), reads=[self.r_CST], writes=[self.r_cb])
        kb.add("pool", lambda e: e.dma_start(out=self.IDB[:], in_=self.ident_d), writes=[self.r_cb], dma=True, key="identld")

    def gcol(self, l, i, j):
        c = (l * 6 + i) * NB + j
        return self.GCOL[:, c:c + 1]

    def stats_to_rstd(self, fi, c):
        kb = self.kb
        Fb = self.F[fi]
        rF = self.r_F[fi]
        for n in range(4):
            h = n // 2
            kb.add("pe", lambda e, n=n: e.matmul(self.PS[:, n * 512:(n + 1) * 512], self.cst(self.C_ONES, 128), Fb[:, n * 512:(n + 1) * 512], start=True, stop=True),
                   reads=[rF[h], self.r_CST], writes=[self.r_PS[n]])
        for h in range(2):
            sl = slice(h * 1024, (h + 1) * 1024)
            kb.add("dve", lambda e, sl=sl: e.tensor_scalar(out=Fb[:, sl], in0=self.PS[:, sl], scalar1=1.0 / (D * c * c), scalar2=EPS / (c * c), op0=ALU.mult, op1=ALU.add),
                   reads=[self.r_PS[2 * h], self.r_PS[2 * h + 1]], writes=[rF[h]])
            kb.add("act", lambda e, sl=sl: e.activation(out=Fb[:, sl], in_=Fb[:, sl], func=AF.Sqrt), reads=[rF[h]], writes=[rF[h]])
            kb.add("dve", lambda e, sl=sl: e.reciprocal(out=Fb[:, sl], in_=Fb[:, sl]), reads=[rF[h]], writes=[rF[h]])

    def next_stg(self):
        s = self.stg
        self.stg = (s + 1) % 4
        fi, h = 2 + s // 2, s % 2
        return self.F[fi][:, h * 1024:(h + 1) * 1024], self.r_F[fi][h]

    def phase_prenorm_from(self, x_ap, r_x, l, gi):
        kb = self.kb
        acc = self.F[0]
        kb.add("dve", lambda e: e.memset(acc[:], 0.0), writes=self.r_F[0])
        for j in range(NB):
            for h in range(2):
                sl = slice(h * 1024, (h + 1) * 1024)
                st, rst = self.next_stg()
                kb.add("sp", lambda e, st=st, j=j, sl=sl: e.dma_start(out=st, in_=x_ap[j][:, sl]), reads=[r_x], writes=[rst], dma=True)
                kb.add("act", lambda e, st=st, j=j, sl=sl: e.activation(out=self.HT[:, j, sl], in_=st, func=AF.Copy, scale=self.gcol(l, gi, j)),
                       reads=[rst, self.r_GCOL], writes=[self.r_HT[j][h]])
                kb.add("pool", lambda e, st=st: e.tensor_tensor(out=st, in0=st, in1=st, op=ALU.mult), reads=[rst], writes=[rst])
                kb.add("pool", lambda e, st=st, sl=sl: e.tensor_tensor(out=acc[:, sl], in0=acc[:, sl], in1=st, op=ALU.add), reads=[rst, self.r_F[0][h]], writes=[self.r_F[0][h]])
        self.stats_to_rstd(0, 1.0)

    def load_w(self, src_ap, nk):
        kb = self.kb
        s = self.wslot
        self.wslot = (s + 1) % 4
        Wt = self.W[s]
        kb.add("pool", lambda e: e.dma_start(out=Wt[:, 0:nk, :], in_=src_ap.rearrange("(kc p) m -> p kc m", p=128)), writes=[self.r_W[s]], dma=True)
        return Wt, self.r_W[s]

    def phase_ffn_B(self, l, i, f0, nf):
        kb = self.kb
        rstd = self.F[0]
        unit = getattr(self, "_bunit", 0)
        def ldB(b):
            fb = f0 + b
            return (self.load_w(self.w_in[l, i][:, fb * 128:(fb + 1) * 128], NB),
                    self.load_w(self.w_in[l, i][:, DFF + fb * 128:DFF + (fb + 1) * 128], NB))
        nxtw = ldB(0)
        for b in range(nf):
            (Wg, rWg), (Wu, rWu) = nxtw
            if b + 1 < nf:
                nxtw = ldB(b + 1)
            for h in range(2):
                base = (unit % 2) * 4
                unit += 1
                for (Wt, rW, boff) in ((Wg, rWg, 0), (Wu, rWu, 2)):
                    for n in range(2):
                        bank = base + boff + n
                        tsl = slice(h * 1024 + n * 512, h * 1024 + (n + 1) * 512)
                        for k in range(NB):
                            kb.add("pe", lambda e, Wt=Wt, k=k, tsl=tsl, bank=bank: e.matmul(self.PS[:, bank * 512:(bank + 1) * 512], Wt[:, k, :], self.HT[:, k, tsl], start=(k == 0), stop=(k == NB - 1)),
                                   reads=[rW, self.r_HT[k][h]], writes=[self.r_PS[bank]])
                sl = slice(h * 1024, (h + 1) * 1024)
                g_ps = self.PS[:, base * 512:(base + 2) * 512]
                u_ps = self.PS[:, (base + 2) * 512:(base + 4) * 512]
                s1, r1 = self.next_stg()
                s2, r2 = self.next_stg()
                kb.add("dve", lambda e, s1=s1, g_ps=g_ps, sl=sl: e.tensor_tensor(out=s1, in0=g_ps, in1=rstd[:, sl], op=ALU.mult),
                       reads=[self.r_PS[base], self.r_PS[base + 1], self.r_F[0][h]], writes=[r1])
                kb.add("act", lambda e, s1=s1: e.activation(out=s1, in_=s1, func=AF.Silu), reads=[r1], writes=[r1])
                kb.add("dve", lambda e, s2=s2, u_ps=u_ps, sl=sl: e.tensor_tensor(out=s2, in0=u_ps, in1=rstd[:, sl], op=ALU.mult),
                       reads=[self.r_PS[base + 2], self.r_PS[base + 3], self.r_F[0][h]], writes=[r2])
                kb.add("dve", lambda e, s1=s1, s2=s2, b=b, sl=sl: e.tensor_tensor(out=self.BIG[:, b, sl], in0=s1, in1=s2, op=ALU.mult),
                       reads=[r1, r2], writes=[self.r_BIG[b][h]])
        self._bunit = unit

    def phase_C(self, w_ap, nk, xsrc_blocks, r_src, last, part_in, out_ap, r_out, post_c):
        kb = self.kb
        acc = self.F[1]
        if last:
            kb.add("dve", lambda e: e.memset(acc[:], 0.0), writes=self.r_F[1])
        unit = getattr(self, "_cunit", 0)
        def ldC(j):
            a = self.load_w(w_ap[0:min(nk, NB) * 128, j * 128:(j + 1) * 128], min(nk, NB))
            b_ = self.load_w(w_ap[NB * 128:nk * 128, j * 128:(j + 1) * 128], nk - NB) if nk > NB else (None, None)
            return a, b_
        nxtw = ldC(0)
        for j in range(NB):
            (Wa, rWa), (Wb, rWb) = nxtw
            if j + 1 < NB:
                nxtw = ldC(j + 1)
            for h in range(2):
                base = (unit % 4) * 2
                unit += 1
                sl = slice(h * 1024, (h + 1) * 1024)
                st, rst = self.next_stg()
                if last and part_in:
                    kb.add("sp", lambda e, st=st, j=j, sl=sl: e.dma_start(out=st, in_=self.PART[j][:, sl]), reads=[self.r_PART], writes=[rst], dma=True)
                for n in range(2):
                    bank = base + n
                    tsl = slice(h * 1024 + n * 512, h * 1024 + (n + 1) * 512)
                    for k in range(nk):
                        Wt, rW, kk = (Wa, rWa, k) if k < NB else (Wb, rWb, k - NB)
                        src, rs = xsrc_blocks(k, h)
                        kb.add("pe", lambda e, Wt=Wt, kk=kk, src=src, tsl=tsl, bank=bank, k=k: e.matmul(self.PS[:, bank * 512:(bank + 1) * 512], Wt[:, kk, :], src[:, tsl], start=(k == 0), stop=(k == nk - 1)),
                               reads=[rW, rs], writes=[self.r_PS[bank]])
                ps = self.PS[:, base * 512:(base + 2) * 512]
                rps = [self.r_PS[base], self.r_PS[base + 1]]
                if not last:
                    kb.add("act", lambda e, st=st, ps=ps: e.activation(out=st, in_=ps, func=AF.Copy), reads=rps, writes=[rst])
                    kb.add("sp", lambda e, st=st, j=j, sl=sl: e.dma_start(out=self.PART[j][:, sl], in_=st), reads=[rst], writes=[self.r_PART], dma=True, par=True)
                else:
                    if part_in:
                        kb.add("dve", lambda e, st=st, ps=ps: e.tensor_tensor(out=st, in0=ps, in1=st, op=ALU.add), reads=rps + [rst], writes=[rst])
                    else:
                        kb.add("act", lambda e, st=st, ps=ps: e.activation(out=st, in_=ps, func=AF.Copy), reads=rps, writes=[rst])
                    kb.add("sp", lambda e, st=st, j=j, sl=sl: e.dma_start(out=out_ap[j][:, sl], in_=st), reads=[rst], writes=[r_out], dma=True, par=True)
                    s2, r2 = self.next_stg()
                    kb.add("act", lambda e, st=st, s2=s2: e.activation(out=s2, in_=st, func=AF.Square), reads=[rst], writes=[r2])
                    kb.add("dve", lambda e, s2=s2, sl=sl: e.tensor_tensor(out=acc[:, sl], in0=acc[:, sl], in1=s2, op=ALU.add), reads=[r2, self.r_F[1][h]], writes=[self.r_F[1][h]])
        self._cunit = unit
        if last:
            self.stats_to_rstd(1, post_c)

    def phase_E(self, l, gi_post, x_src, r_xs, x_dst, r_xd, nxt):
        kb = self.kb
        rstd = self.F[1]
        acc = self.F[0]
        if nxt is not None:
            kb.add("dve", lambda e: e.memset(acc[:], 0.0), writes=self.r_F[0])
        units = [(j, h) for j in range(NB) for h in range(2)]
        sbuf_ = lambda i: (self.BIGF[:, i * T:(i + 1) * T].bitcast(F32), self.r_BIG[i])
        state = {"n": 0}

        def loads(j, h):
            sl = slice(h * 1024, (h + 1) * 1024)
            i = state["n"]
            state["n"] = (i + 2) % 16
            s1, r1 = sbuf_(i)
            s2, r2 = sbuf_(i + 1)
            kb.add("sp", lambda e, s1=s1, j=j, sl=sl: e.dma_start(out=s1, in_=self.OUT[j][:, sl]), reads=[self.r_OUT], writes=r1, dma=True, key=f"dma_Estg{i}")
            kb.add("sp", lambda e, s2=s2, j=j, sl=sl: e.dma_start(out=s2, in_=x_src[j][:, sl]), reads=[r_xs], writes=r2, dma=True, key=f"dma_Estg{i + 1}")
            return s1, r1, s2, r2
        DEPTH_PF = 6
        pre = [loads(*units[i]) for i in range(DEPTH_PF)]
        for ui, (j, h) in enumerate(units):
            sl = slice(h * 1024, (h + 1) * 1024)
            s1, r1, s2, r2 = pre.pop(0)
            if ui + DEPTH_PF < len(units):
                pre.append(loads(*units[ui + DEPTH_PF]))
            kb.add("dve", lambda e, s1=s1, sl=sl: e.tensor_tensor(out=s1, in0=s1, in1=rstd[:, sl], op=ALU.mult), reads=r1 + [self.r_F[1][h]], writes=r1)
            kb.add("dve", lambda e, s1=s1, s2=s2, j=j: e.scalar_tensor_tensor(out=s2, in0=s1, scalar=self.gcol(l, gi_post, j), in1=s2, op0=ALU.mult, op1=ALU.add),
                   reads=r1 + r2 + [self.r_GCOL], writes=r2)
            kb.add("sp", lambda e, s2=s2, j=j, sl=sl: e.dma_start(out=x_dst[j][:, sl], in_=s2), reads=r2, writes=[r_xd], dma=True, par=True)
            if nxt is not None:
                nl, ng = nxt
                kb.add("act", lambda e, s2=s2, j=j, sl=sl, nl=nl, ng=ng: e.activation(out=self.HT[:, j, sl], in_=s2, func=AF.Copy, scale=self.gcol(nl, ng, j)),
                       reads=r2 + [self.r_GCOL], writes=[self.r_HT[j][h]])
                kb.add("act", lambda e, s1=s1, s2=s2: e.activation(out=s1, in_=s2, func=AF.Square), reads=r2, writes=r1)
                kb.add("pool", lambda e, s1=s1, sl=sl: e.tensor_tensor(out=acc[:, sl], in0=acc[:, sl], in1=s1, op=ALU.add), reads=r1 + [self.r_F[0][h]], writes=[self.r_F[0][h]])
        if nxt is not None:
            self.stats_to_rstd(0, 1.0)

    def ffn(self, l, i, x_src, r_xs, x_dst, r_xd, nxt):
        big = lambda k, h: (self.BIG[:, k, :], self.r_BIG[k][h])
        self.phase_ffn_B(l, i, 0, FA)
        self.phase_C(self.w_out[l, i][0:FA * 128, :], FA, big, None, False, False, None, None, None)
        self.phase_ffn_B(l, i, FA, NFB - FA)
        self.phase_C(self.w_out[l, i][FA * 128:DFF, :], NFB - FA, big, None, True, True, self.OUT, self.r_OUT, 0.5)
        self.phase_E(l, 1 + 4 * i, x_src, r_xs, x_dst, r_xd, nxt)

    def finish(self):
        kb = self.kb
        kb.add("sp", lambda e: e.nop(), reads=[self.r_y])
        kb.emit()
        kb.close()
        return self.nc


def make_consts():
    c = np.zeros((128, 512), np.float32)
    c[:, 0:128] = 1.0
    P = np.zeros((128, 128), np.float32)
    for m in range(128):
        k = (m // 64) * 64 + ((m % 64) + 32) % 64
        P[k, m] = 1.0
    c[:, 128:256] = P
    s = np.arange(128)[:, None]; t = np.arange(128)[None, :]
    c[:, 256:384] = ((s // 64 == t // 64) & (s <= t)).astype(np.float32)
    inv_freq = (10000.0 ** (-np.arange(0, 64, 2, dtype=np.float32) / 64)).astype(np.float32)
    d = np.arange(128) % 64
    c[:, 384] = inv_freq[d % 32]
    c[:, 385] = np.where(d < 32, -1.0, 1.0)
    return c

def make_core_consts(rank):
    c = np.zeros((128, 520), np.float32)
    qi = np.arange(128)[:, None]; kj = np.arange(256)[None, :]
    diff = qi + 128 - kj
    band = (diff >= 0) & (diff < 128)
    c[:, 0:256] = np.where(band, 0.0, -30000.0)
    first = band & (kj >= 128) if rank == 0 else band
    c[:, 256:512] = np.where(first, 0.0, -30000.0)
    if rank > 0:
        c[:, 512 + rank - 1] = 1.0
    for r in range(4):
        if r < rank:
            c[:, 516 + r] = 1.0
    return c


def build_full(depth=4):
    p = Prog(depth=depth)
    p.phase_rope_tables()
    p.phase_prenorm_from(p.xT, p.r_xin, 0, 0)
    locs = [(p.XA, p.r_XA), (p.XB, p.r_XB)]
    src = (p.xT, p.r_xin)
    k = 0
    nsub = 3 * depth
    for l in range(depth):
        for sub in range(3):
            dst = (p.yT, p.r_y) if k == nsub - 1 else locs[k % 2]
            if sub == 0:
                p.ffn(l, 0, src[0], src[1], dst[0], dst[1], (l, 2))
            elif sub == 1:
                p.mixer(l, src[0], src[1], dst[0], dst[1], (l, 4))
            else:
                p.ffn(l, 1, src[0], src[1], dst[0], dst[1], (l + 1, 0) if l + 1 < depth else None)
            src = dst
            k += 1
    return p.finish()


def kernel(x, positions, norm_gains, ffn_w_in, ffn_w_out, w_mix_in, attn_sinks, gla_gate_w2, gla_gate_b, gla_norm_gain, w_mix_out):
    x = np.asarray(x, dtype=np.float32)
    positions = np.asarray(positions, dtype=np.int32)
    depth = int(np.asarray(norm_gains).shape[0])
    B, S, _ = x.shape
    f32 = lambda a: np.ascontiguousarray(np.asarray(a, dtype=np.float32))
    shared = dict(
        gcol=np.ascontiguousarray(f32(norm_gains).reshape(depth * 6 * NB, 128).T),
        ffn_w_in=f32(ffn_w_in), ffn_w_out=f32(ffn_w_out), w_mix_in=f32(w_mix_in), w_mix_out=f32(w_mix_out),
        attn_sinks=f32(attn_sinks), gla_gate_w2=f32(gla_gate_w2),
        gb_col=np.ascontiguousarray(f32(gla_gate_b).reshape(depth * 4, 128).T),
        gn_col=np.ascontiguousarray(f32(gla_norm_gain).reshape(depth * 2, 128).T),
        cst=make_consts(), ident=np.eye(128, dtype=np.float32))
    ins = []
    for core in range(8):
        b, r = core // 4, core % 4
        d = dict(shared)
        d["xT"] = np.ascontiguousarray(x[b, r * T:(r + 1) * T].T).reshape(NB, 128, T)
        d["pos"] = np.ascontiguousarray(positions[b, r * T:(r + 1) * T][None])
        d["corec"] = make_core_consts(r)
        ins.append(d)
    nc = build_full(depth)
    res = run_bass_kernel_spmd(nc, ins, core_ids=list(range(8)))
    y = np.empty((B, S, D), np.float32)
    for core in range(8):
        b, r = core // 4, core % 4
        y[b, r * T:(r + 1) * T] = res.results[core]["yT"].reshape(D, T).T
    return y
```

```python
import numpy as np
from contextlib import ExitStack
import concourse.bass as bass
import concourse.mybir as mybir
from concourse.bass_utils import run_bass_kernel_spmd

F32 = mybir.dt.float32
BF16 = mybir.dt.bfloat16
I32 = mybir.dt.int32
AF = mybir.ActivationFunctionType
ALU = mybir.AluOpType
AX = mybir.AxisListType

ENGS = ("pe", "act", "dve", "pool", "sp")


class Res:
    __slots__ = ("name", "ws", "rs", "prs")

    def __init__(self, name):
        self.name = name
        self.ws = []
        self.rs = []
        self.prs = []


class Op:
    __slots__ = ("eng", "fn", "deps", "marked", "val", "dma", "key", "idx", "inc")

    def __init__(self, eng, fn, dma, key, idx, inc=16):
        self.inc = inc
        self.eng = eng
        self.fn = fn
        self.deps = []
        self.marked = False
        self.val = None
        self.dma = dma
        self.key = key
        self.idx = idx


class KB:
    def __init__(self, nc):
        self.nc = nc
        self.ops = {e: [] for e in ENGS}
        self.stack = ExitStack()
        self.nres = 0

    def sbuf(self, name, shape, dtype):
        return self.stack.enter_context(self.nc.sbuf_tensor(name, list(shape), dtype))

    def psum(self, name, shape, dtype):
        return self.stack.enter_context(self.nc.psum_tensor(name, list(shape), dtype))

    def res(self, name=None):
        self.nres += 1
        return Res(name or f"r{self.nres}")

    def add(self, eng, fn, reads=(), writes=(), dma=False, key=None, inc=16, par=False):
        ops = self.ops[eng]
        if dma and key is None:
            key = "dma_" + writes[0].name
        op = Op(eng, fn, dma, key, len(ops), inc)
        deps = []
        for r in reads:
            for o in r.ws:
                deps.append((o, "raw"))
        for w in writes:
            if w.rs:
                for o in w.rs:
                    deps.append((o, "war"))
            elif par:
                for o in w.prs:
                    deps.append((o, "war"))
            else:
                for o in w.ws:
                    deps.append((o, "waw"))
        for d, kind in deps:
            if d is op:
                continue
            if d.eng == eng and not d.dma and not dma:
                if eng == "pe":
                    continue
                if kind != "raw" or (op.idx - d.idx) > 2:
                    continue
            if d.eng == eng and d.dma and dma and False:
                continue
            d.marked = True
            op.deps.append(d)
        for r in reads:
            r.rs.append(op)
        for w in writes:
            if w.rs:
                w.prs = w.rs
                w.rs = []
                w.ws = [op]
            elif par:
                w.ws.append(op)
            else:
                w.ws = [op]
        ops.append(op)
        return op

    def emit(self):
        nc = self.nc
        keys = {}
        for e in ENGS:
            cnt = 0
            for op in self.ops[e]:
                if op.dma:
                    keys[op.key] = keys.get(op.key, 0) + op.inc
                    op.val = keys[op.key]
                elif op.marked:
                    cnt += 1
                    op.val = cnt
            self.nmarks = getattr(self, "nmarks", {})
            self.nmarks[e] = cnt
        EPOCH = 20000
        nep = {e: self.nmarks[e] // EPOCH + 1 for e in ENGS}
        semnames = [f"{e}{i}" for e in ENGS for i in range(nep[e])] + sorted(keys.keys())
        sems = {}
        for n in semnames:
            sems[n] = self.stack.enter_context(nc.semaphore("s_" + n))
        self.nsems = len(semnames)

        def run(engname, engobj):
            seen = {}
            for op in self.ops[engname]:
                need = {}
                for d in op.deps:
                    k = d.key if d.dma else d.eng
                    if d.val > need.get(k, 0):
                        need[k] = d.val
                for k, v in need.items():
                    if seen.get(k, 0) >= v:
                        continue
                    seen[k] = v
                    if k in ENGS:
                        ep, vv = (v - 1) // EPOCH, (v - 1) % EPOCH + 1
                        engobj.wait_ge(sems[f"{k}{ep}"], vv)
                    else:
                        engobj.wait_ge(sems[k], v)
                ins = op.fn(engobj)
                if op.dma:
                    ins.then_inc(sems[op.key], op.inc)
                elif op.marked:
                    ins.then_inc(sems[f"{engname}{(op.val - 1) // EPOCH}"], 1)

        with nc.Block() as block:
            @block.tensor
            def _(e):
                run("pe", e)

            @block.scalar
            def _(e):
                run("act", e)

            @block.vector
            def _(e):
                run("dve", e)

            @block.gpsimd
            def _(e):
                run("pool", e)

            @block.sync
            def _(e):
                run("sp", e)

    def close(self):
        self.stack.close()


NPROJ = 41
P_QA, P_KA, P_VA, P_QG, P_KG, P_VG, P_RG, P_GLR = 0, 8, 12, 16, 20, 24, 32, 40
O_QR, O_KT, O_V2, O_TMPB, O_GKV, O_PN, O_PT = 0, 16384, 25088, 34816, 36864, 40960, 44032
KTW = 2176
O_TMP2, O_AM, O_SQP, O_SB = 40960, 43008, 43264, 43776
S_DEC, S_DTOT, S_ATT = 0, 128, 136
TWO_PI_HI = 6.28125
TWO_PI_LO = 6.283185307179586 - 6.28125
MAGIC = 12582912.0


def _proj_cols(m):
    if m < 8:
        return [(m * 128, 128)]
    if m < 12:
        g = m - 8
        return [(1024 + g * 64, 64)] * 2
    if m < 16:
        g = m - 12
        return [(1280 + g * 64, 64)] * 2
    if m < 20:
        return [(1536 + (m - 16) * 128, 128)]
    if m < 24:
        return [(2048 + (m - 20) * 128, 128)]
    if m < 32:
        return [(2560 + (m - 24) * 128, 128)]
    if m < 40:
        return [(3584 + (m - 32) * 128, 128)]
    return [(4608, 16)]


class MixerMixin:
    def mixer_init(self):
        kb, nc = self.kb, self.nc
        dscr = lambda name, shape, dt=F32: nc.dram_tensor(name, list(shape), dt).ap()
        self.PROJ = dscr("PROJ", [NPROJ, 128, T])
        self.r_PROJ = kb.res("PROJ")
        self.COSD = dscr("COSD", [128, T]); self.SIND = dscr("SIND", [128, T])
        self.r_ROPE = kb.res("ROPE")
        self.KVsrc = dscr("KVsrc", [128, 1024], BF16); self.KVdst = dscr("KVdst", [512, 1024], BF16)
        self.Ssrc = dscr("Ssrc", [128, 1028]); self.Sdst = dscr("Sdst", [512, 1028])
        self.r_KVsrc, self.r_KVdst, self.r_Ssrc, self.r_Sdst = [kb.res(n) for n in ("KVsrc", "KVdst", "Ssrc", "Sdst")]
        self.CORE = kb.sbuf("CORE", [128, 520], F32); self.r_CORE = kb.res("CORE")
        self.SINK = kb.sbuf("SINK", [128, 16], F32); self.r_SINK = kb.res("SINK")
        self.SINKP = kb.sbuf("SINKP", [128, 16], F32); self.r_SINKP = kb.res("SINKP")
        self.GBN = kb.sbuf("GBN", [128, 6 * self.depth], F32); self.r_GBN = kb.res("GBN")
        self.SMALL = kb.sbuf("SMALL", [128, 256], F32); self.r_SMALL = kb.res("SMALL")
        self.r_att = [kb.res("att0"), kb.res("att1"), kb.res("att2")]
        self.r_att2 = [kb.res("att20"), kb.res("att21"), kb.res("att22")]
        self.r_Sm = [kb.res(f"Sm{h}") for h in range(4)]
        kb.add("sp", lambda e: e.dma_start(out=self.CORE[:], in_=self.core_d), writes=[self.r_CORE], dma=True)
        kb.add("sp", lambda e: e.dma_start(out=self.GBN[:, 0:4 * self.depth], in_=self.gb_d), writes=[self.r_GBN], dma=True, key="gbn")
        kb.add("sp", lambda e: e.dma_start(out=self.GBN[:, 4 * self.depth:6 * self.depth], in_=self.gn_d), writes=[self.r_GBN], dma=True, key="gbn", par=True)

    def bigf(self, lo, n):
        return self.BIGF[:, lo:lo + n]

    def bigres(self, lo, n):
        out = []
        for blk in range(lo // 2048, (lo + n - 1) // 2048 + 1):
            for h in range(2):
                a = blk * 2048 + h * 1024
                if a < lo + n and a + 1024 > lo:
                    out.append(self.r_BIG[blk][h])
        return out

    def Fres(self, i):
        return [self.r_F[i][0], self.r_F[i][1]]

    def fence(self, rlist):
        self.kb.add("dve", lambda e: e.memset(self.SMALL[:, 255:256], 0.0), reads=rlist, writes=rlist)

    def phase_rope_tables(self):
        kb = self.kb
        A, B_, C_, Pi = self.F[0], self.F[1], self.F[2], self.F[3]
        rA, rB, rC, rP = self.Fres(0), self.Fres(1), self.Fres(2), self.Fres(3)
        invf = self.cst(self.C_INVF, 1); sign = self.cst(self.C_SIGN, 1)
        kb.add("sp", lambda e: e.dma_start(out=Pi[:].bitcast(I32), in_=self.pos.partition_broadcast(128)), writes=rP, dma=True, key="posld")
        kb.add("dve", lambda e: e.tensor_copy(out=A[:], in_=Pi[:].bitcast(I32)), reads=rP, writes=rA)
        kb.add("dve", lambda e: e.tensor_scalar(out=A[:], in0=A[:], scalar1=invf, scalar2=0.0, op0=ALU.mult, op1=ALU.add), reads=rA + [self.r_CST], writes=rA)

        def reduce_sin(src, rsrc, dst, rdst, tmp, rtmp):
            kb.add("dve", lambda e: e.tensor_scalar(out=tmp[:], in0=src[:], scalar1=float(1.0 / (2 * np.pi)), scalar2=MAGIC, op0=ALU.mult, op1=ALU.add), reads=rsrc, writes=rtmp)
            kb.add("dve", lambda e: e.tensor_scalar(out=tmp[:], in0=tmp[:], scalar1=-MAGIC, scalar2=1.0, op0=ALU.add, op1=ALU.mult), reads=rtmp, writes=rtmp)
            kb.add("dve", lambda e: e.scalar_tensor_tensor(out=dst[:], in0=tmp[:], scalar=-TWO_PI_HI, in1=src[:], op0=ALU.mult, op1=ALU.add), reads=rtmp + rsrc, writes=rdst)
            kb.add("dve", lambda e: e.scalar_tensor_tensor(out=dst[:], in0=tmp[:], scalar=-TWO_PI_LO, in1=dst[:], op0=ALU.mult, op1=ALU.add), reads=rtmp + rdst, writes=rdst)
            kb.add("dve", lambda e: e.tensor_scalar(out=dst[:], in0=dst[:], scalar1=3.1415925, scalar2=-3.1415925, op0=ALU.min, op1=ALU.max), reads=rdst, writes=rdst)

        reduce_sin(A, rA, B_, rB, C_, rC)
        kb.add("act", lambda e: e.activation(out=C_[:], in_=B_[:], func=AF.Sin), reads=rB, writes=rC)
        kb.add("dve", lambda e: e.tensor_scalar(out=C_[:], in0=C_[:], scalar1=sign, scalar2=0.0, op0=ALU.mult, op1=ALU.add), reads=rC + [self.r_CST], writes=rC)
        kb.add("sp", lambda e: e.dma_start(out=self.SIND, in_=C_[:]), reads=rC, writes=[self.r_ROPE], dma=True, key="rope")
        kb.add("dve", lambda e: e.tensor_scalar(out=A[:], in0=B_[:], scalar1=float(np.pi / 2), scalar2=0.0, op0=ALU.add, op1=ALU.add), reads=rB, writes=rA)
        reduce_sin(A, rA, B_, rB, Pi, rP)
        kb.add("act", lambda e: e.activation(out=A[:], in_=B_[:], func=AF.Sin), reads=rB, writes=rA)
        kb.add("sp", lambda e: e.dma_start(out=self.COSD, in_=A[:]), reads=rA, writes=[self.r_ROPE], dma=True, key="rope", par=True)

    def phase_inproj(self, l):
        kb = self.kb
        rstd = self.F[0]
        unit = 0

        def ld(m):
            runs = _proj_cols(m)
            s = self.wslot
            self.wslot = (s + 1) % 4
            Wt = self.W[s]
            c = 0
            for ri, (c0, n) in enumerate(runs):
                kb.add("pool", lambda e, c=c, c0=c0, n=n: e.dma_start(out=Wt[:, :, c:c + n], in_=self.w_mi[l][:, c0:c0 + n].rearrange("(kc p) m -> p kc m", p=128)),
                       writes=[self.r_W[s]], dma=True, par=(ri > 0))
                c += n
            return Wt, self.r_W[s], c
        import os
        mlist = [int(v) for v in os.environ.get("MLIST", "").split(",") if v] or list(range(NPROJ))
        nxtw = ld(mlist[0])
        for mi, m in enumerate(mlist):
            Wt, rW, ncol = nxtw
            if mi + 1 < len(mlist):
                nxtw = ld(mlist[mi + 1])
            for h in range(2):
                base = (unit % 4) * 2
                unit += 1
                sl = slice(h * 1024, (h + 1) * 1024)
                for n in range(2):
                    bank = base + n
                    tsl = slice(h * 1024 + n * 512, h * 1024 + (n + 1) * 512)
                    for k in range(NB):
                        kb.add("pe", lambda e, Wt=Wt, k=k, tsl=tsl, bank=bank, ncol=ncol: e.matmul(self.PS[0:ncol, bank * 512:(bank + 1) * 512], Wt[:, k, 0:ncol], self.HT[:, k, tsl], start=(k == 0), stop=(k == NB - 1)),
                               reads=[rW, self.r_HT[k][h]], writes=[self.r_PS[bank]])
                st, rst = self.next_stg()
                kb.add("dve", lambda e, st=st, base=base, sl=sl, ncol=ncol: e.tensor_tensor(out=st[0:ncol, :], in0=self.PS[0:ncol, base * 512:(base + 2) * 512], in1=rstd[0:ncol, sl], op=ALU.mult),
                       reads=[self.r_PS[base], self.r_PS[base + 1], self.r_F[0][h]], writes=[rst])
                kb.add("sp", lambda e, st=st, m=m, sl=sl, ncol=ncol: e.dma_start(out=self.PROJ[m][0:ncol, sl], in_=st[0:ncol, :]), reads=[rst], writes=[self.r_PROJ], dma=True, par=True)

    def rope_block(self, m, dst_lo, COS, SIN, rT):
        kb = self.kb
        for h in range(2):
            sl = slice(h * 1024, (h + 1) * 1024)
            X, rX = self.F[2][:, sl], [self.r_F[2][h]]
            Tm, rTm = self.F[3][:, sl], [self.r_F[3][h]]
            kb.add("sp", lambda e, X=X, sl=sl: e.dma_start(out=X, in_=self.PROJ[m][:, sl]), reads=[self.r_PROJ], writes=rX, dma=True)
            pb = 4 + 2 * h
            for n in range(2):
                kb.add("pe", lambda e, n=n, X=X, pb=pb: e.matmul(self.PS[:, (pb + n) * 512:(pb + n + 1) * 512], self.cst(self.C_PERM, 128), X[:, n * 512:(n + 1) * 512], start=True, stop=True),
                       reads=rX + [self.r_CST], writes=[self.r_PS[pb + n]])
            kb.add("dve", lambda e, Tm=Tm, pb=pb, sl=sl: e.tensor_tensor(out=Tm, in0=self.PS[:, pb * 512:(pb + 2) * 512], in1=SIN[:, sl], op=ALU.mult), reads=[self.r_PS[pb], self.r_PS[pb + 1]] + rT, writes=rTm)
            kb.add("pool", lambda e, X=X, sl=sl: e.tensor_tensor(out=X, in0=X, in1=COS[:, sl], op=ALU.mult), reads=rX + rT, writes=rX)
            kb.add("dve", lambda e, X=X, Tm=Tm, h=h: e.tensor_tensor(out=self.bigf(dst_lo + h * 1024, 1024), in0=X, in1=Tm, op=ALU.add), reads=rX + rTm, writes=self.bigres(dst_lo + h * 1024, 1024))

    def phase_attn(self, l):
        kb = self.kb
        COS, SIN = self.F[0], self.F[1]
        rT = self.Fres(0) + self.Fres(1)
        kb.add("sp", lambda e: e.dma_start(out=COS[:], in_=self.COSD), reads=[self.r_ROPE], writes=self.Fres(0), dma=True, key="ldcos")
        kb.add("sp", lambda e: e.dma_start(out=SIN[:], in_=self.SIND), reads=[self.r_ROPE], writes=self.Fres(1), dma=True, key="ldsin")
        kb.add("sp", lambda e: e.dma_start(out=self.SINK[:], in_=self.sinks_d[l:l + 1, :].partition_broadcast(128)), writes=[self.r_SINK], dma=True)
        kb.add("dve", lambda e: e.tensor_copy(out=self.SINKP[:].rearrange("p (g b a) -> p g b a", b=2, a=2), in_=self.SINK[:].rearrange("p (g a b) -> p g b a", a=2, b=2)), reads=[self.r_SINK], writes=[self.r_SINKP])
        for g in range(4):
            self.rope_block(P_KA + g, O_KT + g * KTW + 128, COS, SIN, rT)
        import os
        ATT = int(os.environ.get("ATT", "9"))
        if ATT < 2:
            return
        X = self.F[2]; rX = self.Fres(2)
        for g in range(4):
            kb.add("sp", lambda e, g=g: e.dma_start(out=X[:], in_=self.PROJ[P_VA + g]), reads=[self.r_PROJ], writes=rX, dma=True, key="ropeX")
            kb.add("act", lambda e: e.activation(out=self.bigf(O_TMPB, T), in_=X[:], func=AF.Copy), reads=rX, writes=self.bigres(O_TMPB, T))
            for half in range(2):
                bank = 4 + half
                pst = self.PS[:, bank * 512:(bank + 1) * 512].bitcast(BF16)
                for tt in range(8):
                    t = half * 8 + tt
                    kb.add("pe", lambda e, t=t, tt=tt, pst=pst: e.transpose(out=pst[:, tt * 128:(tt + 1) * 128], in_=self.bigf(O_TMPB + t * 128, 128), identity=self.IDB[:]),
                           reads=self.bigres(O_TMPB + t * 128, 128) + [self.r_cb], writes=[self.r_PS[bank]])
                lo = O_V2 + (1 + half * 8) * 512 + g * 128
                dst = self.BIGF[:, lo:lo + 8 * 512].rearrange("p (t c) -> p t c", c=512)[:, :, 0:128]
                kb.add("act", lambda e, dst=dst, pst=pst: e.activation(out=dst, in_=pst.rearrange("p (t c) -> p t c", c=128), func=AF.Copy),
                       reads=[self.r_PS[bank]], writes=self.bigres(O_V2 + (1 + half * 8) * 512, 8 * 512))
        if ATT < 3:
            return
        for b in range(8):
            self.rope_block(P_QA + b, O_QR + b * T, COS, SIN, rT)
        if ATT < 4:
            return
        for g in range(4):
            kb.add("sp", lambda e, g=g: e.dma_start(out=self.KVsrc[:, g * 128:(g + 1) * 128], in_=self.bigf(O_KT + g * KTW + T, 128)),
                   reads=self.bigres(O_KT + g * KTW + T, 128), writes=[self.r_KVsrc], dma=True, par=(g > 0))
        kb.add("sp", lambda e: e.dma_start(out=self.KVsrc[:, 512:1024], in_=self.bigf(O_V2 + 16 * 512, 512)), reads=self.bigres(O_V2 + 16 * 512, 512), writes=[self.r_KVsrc], dma=True, par=True)
        kb.add("pool", lambda e: e.collective_compute("AllGather", ALU.bypass, replica_groups=[[0, 1, 2, 3], [4, 5, 6, 7]], ins=[self.KVsrc], outs=[self.KVdst]),
               reads=[self.r_KVsrc], writes=[self.r_KVdst], dma=True, key="cc1", inc=1)
        def halo_select():
            G = self.bigf(O_GKV, 4096).rearrange("p (r c) -> p r c", c=1024)
            rG = self.bigres(O_GKV, 4096)
            kb.add("sp", lambda e: e.dma_start(out=G, in_=self.KVdst.rearrange("(r p) c -> p r c", p=128)), reads=[self.r_KVdst], writes=rG, dma=True, key="gkv")
            sel = lambda r: self.CORE[:, 512 + r:513 + r]
            for g in range(4):
                dK = self.bigf(O_KT + g * KTW, 128)
                rK = self.bigres(O_KT + g * KTW, 128)
                for r in range(4):
                    src = G[:, r, g * 128:(g + 1) * 128]
                    if r == 0:
                        kb.add("dve", lambda e, dK=dK, src=src, r=r: e.tensor_scalar(out=dK, in0=src, scalar1=sel(r), scalar2=0.0, op0=ALU.mult, op1=ALU.add), reads=rG + [self.r_CORE], writes=rK)
                    else:
                        kb.add("dve", lambda e, dK=dK, src=src, r=r: e.scalar_tensor_tensor(out=dK, in0=src, scalar=sel(r), in1=dK, op0=ALU.mult, op1=ALU.add), reads=rG + rK + [self.r_CORE], writes=rK)
            dV = self.bigf(O_V2, 512); rV = self.bigres(O_V2, 512)
            for r in range(4):
                src = G[:, r, 512:1024]
                if r == 0:
                    kb.add("dve", lambda e, src=src, r=r: e.tensor_scalar(out=dV, in0=src, scalar1=sel(r), scalar2=0.0, op0=ALU.mult, op1=ALU.add), reads=rG + [self.r_CORE], writes=rV)
                else:
                    kb.add("dve", lambda e, src=src, r=r: e.scalar_tensor_tensor(out=dV, in0=src, scalar=sel(r), in1=dV, op0=ALU.mult, op1=ALU.add), reads=rG + rV + [self.r_CORE], writes=rV)
        kb.add("dve", lambda e: e.tensor_reduce(out=self.SMALL[:, 232:236], in_=self.SINK[:].rearrange("p (g j) -> p g j", j=4), axis=AX.X, op=ALU.max), reads=[self.r_SINK], writes=[self.r_SMALL])
        smax = lambda g: self.SMALL[:, 232 + g:233 + g]
        units = [(g, t) for g in range(4) for t in range(1, 16)] + [(g, 0) for g in range(4)]
        first_halo_unit = 60

        ctx = {}
        T_BANK, O_BANK = 6, 7

        def s0(ui, g, t):
            if ui == first_halo_unit:
                halo_select()
            u = ui % 3
            bS = u * 2
            S = self.PS[:, bS * 512:(bS + 2) * 512]
            rS = [self.r_PS[bS], self.r_PS[bS + 1]]
            for j in range(4):
                blk, hf = 2 * g + j // 2, j % 2
                psl = slice(hf * 64, hf * 64 + 64)
                qlo = O_QR + blk * T + t * 128
                klo = O_KT + g * KTW + t * 128
                pj = (j % 2) * 2 + j // 2
                kb.add("pe", lambda e, pj=pj, psl=psl, qlo=qlo, klo=klo, S=S: e.matmul(S[:, pj * 256:(pj + 1) * 256], self.BIGF[psl, qlo:qlo + 128], self.BIGF[psl, klo:klo + 256], start=True, stop=True),
                       reads=self.bigres(qlo, 128) + self.bigres(klo, 256), writes=[rS[pj // 2]])
            st0 = S_ATT + u * 32
            c = dict(u=u, S=S, rS=rS, sml=(lambda a, st0=st0: self.SMALL[:, st0 + a * 4:st0 + a * 4 + 4]), rsml=self.r_att[u])
            ctx[ui] = c

        def s1(ui, g, t):
            c = ctx[ui]
            sm, rsm = self.next_stg()
            c["sm"], c["rsm"] = sm, rsm
            S, rS, sml, rsml = c["S"], c["rS"], c["sml"], c["rsml"]
            mask = self.CORE[:, 256:512] if t == 0 else self.CORE[:, 0:256]
            sm3 = sm.rearrange("p (j k) -> p j k", k=256)
            c["sm3"] = sm3
            S3 = S.rearrange("p (j k) -> p j k", k=256)
            kb.add("dve", lambda e: e.scalar_tensor_tensor(out=sm3, in0=S3, scalar=0.125, in1=mask.unsqueeze(1).to_broadcast([128, 4, 256]), op0=ALU.mult, op1=ALU.add),
                   reads=rS + [self.r_CORE], writes=[rsm])
            kb.add("dve", lambda e: e.tensor_reduce(out=sml(0)[:, 0:1], in_=sm, axis=AX.X, op=ALU.max), reads=[rsm], writes=[rsml])
            kb.add("dve", lambda e: e.tensor_scalar(out=sml(1)[:, 0:1], in0=sml(0)[:, 0:1], scalar1=smax(g), scalar2=-1.0, op0=ALU.max, op1=ALU.mult), reads=[rsml, self.r_SMALL], writes=[rsml])

        def s2(ui, g, t):
            c = ctx[ui]
            sm, rsm, sml, rsml, u = c["sm"], c["rsm"], c["sml"], c["rsml"], c["u"]
            kb.add("act", lambda e: e.activation(out=sm, in_=sm, func=AF.Exp, bias=sml(1)[:, 0:1]), reads=[rsm, rsml], writes=[rsm])
            kb.add("act", lambda e: e.activation(out=sml(3), in_=self.SINKP[:, 4 * g:4 * g + 4], func=AF.Exp, bias=sml(1)[:, 0:1]), reads=[rsml, self.r_SINKP], writes=[self.r_att2[u]])

        def s3(ui, g, t):
            c = ctx[ui]
            sm3, rsm, sml, rsml, u = c["sm3"], c["rsm"], c["sml"], c["rsml"], c["u"]
            kb.add("dve", lambda e: e.tensor_reduce(out=sml(2), in_=sm3, axis=AX.X, op=ALU.add), reads=[rsm], writes=[rsml])
            kb.add("dve", lambda e: e.tensor_tensor(out=sml(4), in0=sml(2), in1=sml(3), op=ALU.add), reads=[rsml, self.r_att2[u]], writes=[rsml])
            kb.add("dve", lambda e: e.reciprocal(out=sml(4), in_=sml(4)), reads=[rsml], writes=[rsml])
            pn = self.bigf(O_PN + u * 1024, 1024); rpn = self.bigres(O_PN + u * 1024, 1024)
            c["pn"], c["rpn"] = pn, rpn
            kb.add("pool", lambda e: e.tensor_tensor(out=pn.rearrange("p (j k) -> p j k", k=256), in0=sm3, in1=sml(4).unsqueeze(2).to_broadcast([128, 4, 256]), op=ALU.mult),
                   reads=[rsm, rsml], writes=rpn)

        pst = self.PS[:, T_BANK * 512:(T_BANK + 1) * 512].bitcast(BF16)
        pT = self.bigf(O_PT, 1024); rpT = self.bigres(O_PT, 1024)
        O = self.PS[:, O_BANK * 512:(O_BANK + 1) * 512]

        def s4(ui, g, t):
            c = ctx[ui]
            pn, rpn = c["pn"], c["rpn"]
            for jc in range(8):
                kb.add("pe", lambda e, jc=jc: e.transpose(out=pst[:, jc * 128:(jc + 1) * 128], in_=pn[:, jc * 128:(jc + 1) * 128], identity=self.IDB[:]),
                       reads=rpn + [self.r_cb], writes=[self.r_PS[T_BANK]])

        def s5(ui, g, t):
            kb.add("act", lambda e: e.activation(out=pT, in_=pst, func=AF.Copy), reads=[self.r_PS[T_BANK]], writes=rpT)

        def s6(ui, g, t):
            for j in range(4):
                for c_ in range(2):
                    vlo = O_V2 + (t + c_) * 512 + g * 128
                    kb.add("pe", lambda e, j=j, c_=c_, vlo=vlo: e.matmul(O[:, j * 128:(j + 1) * 128], self.bigf(vlo, 128), pT[:, (2 * j + c_) * 128:(2 * j + c_ + 1) * 128], start=(c_ == 0), stop=(c_ == 1)),
                           reads=self.bigres(vlo, 128) + rpT, writes=[self.r_PS[O_BANK]])

        def s7(ui, g, t):
            for hf in range(2):
                psl = slice(hf * 64, hf * 64 + 64)
                kb.add("act", lambda e, hf=hf, psl=psl: e.activation(out=self.HT[psl, 2 * g:2 * g + 2, t * 128:(t + 1) * 128], in_=O[psl, hf * 256:(hf + 1) * 256].rearrange("p (b q) -> p b q", q=128), func=AF.Copy),
                       reads=[self.r_PS[O_BANK]], writes=[self.r_HT[2 * g][t // 8], self.r_HT[2 * g + 1][t // 8]])
            ctx.pop(ui)

        stages = [s0, s1, s2, s3, s4, s5, s6, s7]
        nu = len(units)
        for i in range(nu + 7):
            for s in range(7, -1, -1):
                ui = i - s
                if 0 <= ui < nu:
                    stages[s](ui, *units[ui])

    def phase_gla_prep(self, l):
        kb = self.kb
        Fc, Fe, Fq, Fs = self.F[0], self.F[1], self.F[2], self.F[3]
        rc, re, rq, rs_ = self.Fres(0), self.Fres(1), self.Fres(2), self.Fres(3)
        Wg, rWg = self.W[0], self.r_W[0]
        Ww, rWw = self.W[1], self.r_W[1]
        glr = Wg[:].rearrange("p a b -> p (a b)")
        w2 = Ww[:].rearrange("p a b -> p (a b)")
        kb.add("pool", lambda e: e.dma_start(out=glr[0:16, :], in_=self.PROJ[P_GLR][0:16, :]), reads=[self.r_PROJ], writes=[rWg], dma=True)
        kb.add("pool", lambda e: e.dma_start(out=w2[0:16, 0:512], in_=self.w2_d[l]), writes=[rWw], dma=True)
        kb.add("dve", lambda e: e.memset(Fs[:, 1024:2048], 0.0), writes=[self.r_F[3][1]])
        dec = lambda h: self.SMALL[:, S_DEC + h * 32:S_DEC + (h + 1) * 32]
        for h in range(4):
            base = h * 5 * T
            QD, KD, KRT, VT = base, base + T, base + 2 * T, base + 3 * T
            for n in range(4):
                kb.add("pe", lambda e, n=n, h=h: e.matmul(self.PS[:, n * 512:(n + 1) * 512], w2[0:16, h * 128:(h + 1) * 128], glr[0:16, n * 512:(n + 1) * 512], start=True, stop=True),
                       reads=[rWg, rWw], writes=[self.r_PS[n]])
            gb = self.GBN[:, l * 4 + h:l * 4 + h + 1]
            kb.add("act", lambda e, gb=gb: e.activation(out=Fe[:], in_=self.PS[:, 0:2048], func=AF.Identity, bias=gb, scale=1.0), reads=[self.r_PS[i] for i in range(4)] + [self.r_GBN], writes=re)
            kb.add("act", lambda e: e.activation(out=Fe[:], in_=Fe[:], func=AF.Exp, scale=-1.0), reads=re, writes=re)
            kb.add("act", lambda e: e.activation(out=Fe[:], in_=Fe[:], func=AF.Ln, bias=1.0), reads=re, writes=re)
            for n in range(32):
                kb.add("dve", lambda e, n=n: e.tensor_tensor_scan(out=Fc[:, n * 64:(n + 1) * 64], data0=self.cst(self.C_ONES, 64), data1=Fe[:, n * 64:(n + 1) * 64], initial=0.0, op0=ALU.mult, op1=ALU.add),
                       reads=re + [self.r_CST], writes=[self.r_F[0][n // 16]])
            c3 = Fc[:].rearrange("p (n t) -> p n t", t=64)
            kb.add("act", lambda e, h=h: e.activation(out=dec(h), in_=c3[:, :, 63], func=AF.Exp, scale=-1.0 / 16), reads=rc, writes=[self.r_SMALL])
            kb.add("dve", lambda e, h=h: e.tensor_reduce(out=self.SMALL[:, S_DTOT + h:S_DTOT + h + 1], in_=c3[:, :, 63], axis=AX.X, op=ALU.add), reads=rc, writes=[self.r_SMALL])
            kb.add("act", lambda e, h=h: e.activation(out=self.SMALL[:, S_DTOT + h:S_DTOT + h + 1], in_=self.SMALL[:, S_DTOT + h:S_DTOT + h + 1], func=AF.Exp, scale=-1.0 / 16), reads=[self.r_SMALL], writes=[self.r_SMALL])
            kb.add("sp", lambda e, h=h: e.dma_start(out=Fq[:], in_=self.PROJ[P_QG + h]), reads=[self.r_PROJ], writes=rq, dma=True, key="glaQ")
            kb.add("act", lambda e: e.activation(out=Fe[:], in_=Fc[:], func=AF.Exp, scale=-1.0 / 16), reads=rc, writes=re)
            kb.add("dve", lambda e, QD=QD: e.scalar_tensor_tensor(out=self.bigf(QD, T), in0=Fq[:], scalar=float(128 ** -0.5), in1=Fe[:], op0=ALU.mult, op1=ALU.mult), reads=rq + re, writes=self.bigres(QD, T))
            kb.add("sp", lambda e, h=h: e.dma_start(out=Fq[:], in_=self.PROJ[P_KG + h]), reads=[self.r_PROJ], writes=rq, dma=True, key="glaQ")
            kb.add("act", lambda e: e.activation(out=Fe[:], in_=Fc[:], func=AF.Exp, scale=1.0 / 16), reads=rc, writes=re)
            kb.add("dve", lambda e, KD=KD: e.tensor_tensor(out=self.bigf(KD, T), in0=Fq[:], in1=Fe[:], op=ALU.mult), reads=rq + re, writes=self.bigres(KD, T))
            kb.add("dve", lambda e: e.tensor_tensor(out=c3, in0=c3, in1=c3[:, :, 63:64].to_broadcast([128, 32, 64]), op=ALU.subtract), reads=rc, writes=rc)
            kb.add("act", lambda e: e.activation(out=Fe[:], in_=Fc[:], func=AF.Exp, scale=1.0 / 16), reads=rc, writes=re)
            kb.add("dve", lambda e: e.tensor_tensor(out=self.bigf(O_TMP2, T), in0=Fq[:], in1=Fe[:], op=ALU.mult), reads=rq + re, writes=self.bigres(O_TMP2, T))
            for half in range(2):
                bank = 4 + half
                pst = self.PS[:, bank * 512:(bank + 1) * 512].bitcast(BF16)
                for tt in range(8):
                    t = half * 8 + tt
                    kb.add("pe", lambda e, t=t, tt=tt, pst=pst: e.transpose(out=pst[:, tt * 128:(tt + 1) * 128], in_=self.bigf(O_TMP2 + t * 128, 128), identity=self.IDB[:]),
                           reads=self.bigres(O_TMP2 + t * 128, 128) + [self.r_cb], writes=[self.r_PS[bank]])
                kb.add("act", lambda e, pst=pst, KRT=KRT, half=half: e.activation(out=self.bigf(KRT + half * 1024, 1024), in_=pst, func=AF.Copy), reads=[self.r_PS[bank]], writes=self.bigres(KRT + half * 1024, 1024))
            for e2 in range(2):
                kb.add("sp", lambda e, h=h, e2=e2: e.dma_start(out=Fq[:], in_=self.PROJ[P_VG + 2 * h + e2]), reads=[self.r_PROJ], writes=rq, dma=True, key="glaQ")
                kb.add("act", lambda e: e.activation(out=self.bigf(O_TMP2, T), in_=Fq[:], func=AF.Copy), reads=rq, writes=self.bigres(O_TMP2, T))
                for half in range(2):
                    bank = 6 + half
                    pst = self.PS[:, bank * 512:(bank + 1) * 512].bitcast(BF16)
                    for tt in range(8):
                        t = half * 8 + tt
                        kb.add("pe", lambda e, t=t, tt=tt, pst=pst: e.transpose(out=pst[:, tt * 128:(tt + 1) * 128], in_=self.bigf(O_TMP2 + t * 128, 128), identity=self.IDB[:]),
                               reads=self.bigres(O_TMP2 + t * 128, 128) + [self.r_cb], writes=[self.r_PS[bank]])
                    lo = VT + half * 8 * 256 + e2 * 128
                    dst = self.BIGF[:, lo:lo + 8 * 256].rearrange("p (t c) -> p t c", c=256)[:, :, 0:128]
                    kb.add("act", lambda e, dst=dst, pst=pst: e.activation(out=dst, in_=pst.rearrange("p (t c) -> p t c", c=128), func=AF.Copy),
                           reads=[self.r_PS[bank]], writes=self.bigres(VT + half * 8 * 256, 8 * 256))
        self.fence(self.r_Sm + [self.r_F[3][1]] + self.r_PS)
        for n in range(32):
            tile, par = n // 2, n % 2
            psl = slice(par * 64, par * 64 + 64)
            for h in range(4):
                base = h * 5 * T
                KRT, VT = base + 2 * T, base + 3 * T
                bank = 2 * h + par
                dS = self.PS[:, bank * 512:bank * 512 + 256]
                kb.add("pe", lambda e, tile=tile, psl=psl, dS=dS, KRT=KRT, VT=VT: e.matmul(dS, self.BIGF[psl, KRT + tile * 128:KRT + tile * 128 + 128], self.BIGF[psl, VT + tile * 256:VT + tile * 256 + 256], start=True, stop=True),
                       reads=self.bigres(KRT + tile * 128, 128) + self.bigres(VT + tile * 256, 256), writes=[self.r_PS[bank]])
            for h in range(4):
                bank = 2 * h + par
                dS = self.PS[:, bank * 512:bank * 512 + 256]
                Sm = Fs[:, 1024 + h * 256:1024 + (h + 1) * 256]
                kb.add("dve", lambda e, Sm=Sm, dS=dS, n=n, h=h: e.scalar_tensor_tensor(out=Sm, in0=Sm, scalar=dec(h)[:, n:n + 1], in1=dS, op0=ALU.mult, op1=ALU.add),
                       reads=[self.r_PS[bank], self.r_SMALL, self.r_Sm[h]], writes=[self.r_Sm[h]])
        self.fence(self.r_Sm + [self.r_F[3][1]])
        kb.add("sp", lambda e: e.dma_start(out=self.Ssrc[:, 0:1024], in_=Fs[:, 1024:2048]), reads=[self.r_F[3][1]], writes=[self.r_Ssrc], dma=True)
        kb.add("sp", lambda e: e.dma_start(out=self.Ssrc[:, 1024:1028], in_=self.SMALL[:, S_DTOT:S_DTOT + 4]), reads=[self.r_SMALL], writes=[self.r_Ssrc], dma=True, par=True)
        kb.add("pool", lambda e: e.collective_compute("AllGather", ALU.bypass, replica_groups=[[0, 1, 2, 3], [4, 5, 6, 7]], ins=[self.Ssrc], outs=[self.Sdst]),
               reads=[self.r_Ssrc], writes=[self.r_Sdst], dma=True, key="cc2", inc=1)

    def phase_gla_out(self, l):
        kb = self.kb
        F0, F1, F2, F3 = self.F
        GA = F0[:].rearrange("p (r c) -> p r c", c=1024)
        GB2 = F1[:, 0:1024]
        GD = F1[:, 1024:1036]
        rG = self.Fres(0) + self.Fres(1)
        kb.add("sp", lambda e: e.dma_start(out=GA, in_=self.Sdst[0:256, 0:1024].rearrange("(r p) c -> p r c", p=128)), reads=[self.r_Sdst], writes=self.Fres(0), dma=True, key="gS0")
        kb.add("sp", lambda e: e.dma_start(out=GB2, in_=self.Sdst[256:384, 0:1024]), reads=[self.r_Sdst], writes=[self.r_F[1][0]], dma=True, key="gS1")
        kb.add("sp", lambda e: e.dma_start(out=GD.rearrange("p (r c) -> p r c", c=4), in_=self.Sdst[0:384, 1024:1028].rearrange("(r p) c -> p r c", p=128)), reads=[self.r_Sdst], writes=[self.r_F[1][1]], dma=True, key="gS2")
        Sin = F3[:, 1024:2048]
        rSin = [self.r_F[3][1]]
        Tt = F3[:, 0:1024]
        rTt = [self.r_F[3][0]]
        lt = lambda r: self.CORE[:, 516 + r:517 + r]
        kb.add("dve", lambda e: e.memset(Sin, 0.0), writes=rSin)
        for r in range(3):
            Sr = GA[:, r, :] if r < 2 else GB2
            for h in range(4):
                hs = slice(h * 256, (h + 1) * 256)
                kb.add("dve", lambda e, r=r, h=h, hs=hs, Sr=Sr: e.scalar_tensor_tensor(out=Tt[:, hs], in0=Sin[:, hs], scalar=GD[:, r * 4 + h:r * 4 + h + 1], in1=Sr[:, hs], op0=ALU.mult, op1=ALU.add),
                       reads=rG + rSin, writes=rTt)
            kb.add("dve", lambda e: e.tensor_tensor(out=Tt, in0=Tt, in1=Sin, op=ALU.subtract), reads=rTt + rSin, writes=rTt)
            kb.add("dve", lambda e, r=r: e.scalar_tensor_tensor(out=Sin, in0=Tt, scalar=lt(r), in1=Sin, op0=ALU.mult, op1=ALU.add), reads=rTt + rSin + [self.r_CORE], writes=rSin)
        dec = lambda h: self.SMALL[:, S_DEC + h * 32:S_DEC + (h + 1) * 32]
        gmask = self.cst(self.C_GMASK, 128)
        for bi in range(8):
            for hf in range(2):
                sl = slice(hf * 1024, (hf + 1) * 1024)
                R, rR = self.F[bi % 2][:, sl], [self.r_F[bi % 2][hf]]
                kb.add("sp", lambda e, R=R, bi=bi, sl=sl: e.dma_start(out=R, in_=self.PROJ[P_RG + bi][:, sl]), reads=[self.r_PROJ], writes=rR, dma=True)
                kb.add("act", lambda e, R=R, bi=bi, sl=sl: e.activation(out=self.HT[:, 8 + bi, sl], in_=R, func=AF.Silu), reads=rR, writes=[self.r_HT[8 + bi][hf]])
        import os
        GL2 = int(os.environ.get("GL2", "99"))
        if GL2 < 1:
            return
        O_SB2, O_AM2, O_SQ2 = 40960, 43008, 43520
        rr = lambda n: [kb.res(f"g2{n}{h}") for h in range(4)]
        r_bA, r_bB = rr("bA"), rr("bB")
        r_I, r_dSl, r_St, r_Op, r_dSh = r_bA, r_bA, r_bA, r_bB, r_bB
        r_rs, r_tmp = rr("rs"), rr("tmp")
        bankA = lambda h: self.PS[:, (2 * h) * 512:(2 * h + 1) * 512]
        bankB = lambda h: self.PS[:, (2 * h + 1) * 512:(2 * h + 2) * 512]
        I_ps = lambda h: bankA(h)[:, 0:128]
        dS_ps = lambda h, par: bankA(h)[:, 128:384] if par == 0 else bankB(h)[:, 256:512]
        r_dS = lambda h, par: r_dSl[h] if par == 0 else r_dSh[h]
        St_ps = lambda h: bankA(h)[:, 384:512]
        Op_ps = lambda h: bankB(h)[:, 0:256]
        Sm = lambda h: Sin[:, h * 256:(h + 1) * 256]
        Sb = lambda h: self.bigf(O_SB2 + h * 256, 256)
        rSb = lambda h: self.bigres(O_SB2 + h * 256, 256)
        AM = lambda h: self.bigf(O_AM2 + h * 128, 128)
        rAM = lambda h: self.bigres(O_AM2 + h * 128, 128)
        SQ = lambda h: self.bigf(O_SQ2 + h * 256, 256)
        rSQ = lambda h: self.bigres(O_SQ2 + h * 256, 256)
        rsp = lambda h: F2[:, h * 128:(h + 1) * 128]
        tmp = lambda h, e2: F2[:, 512 + (2 * h + e2) * 128:512 + (2 * h + e2 + 1) * 128]
        base = lambda h: h * 5 * T
        QD = lambda h: base(h); KD = lambda h: base(h) + T; KRT = lambda h: base(h) + 2 * T; VT = lambda h: base(h) + 3 * T
        def fence(rlist):
            kb.add("dve", lambda e: e.memset(self.SMALL[:, 255:256], 0.0), reads=rlist, writes=rlist)
        sub = r_bA + r_bB + r_rs + r_tmp + self.r_Sm
        fence(sub + self.r_PS + self.Fres(2) + rSin)
        for h in range(4):
            kb.add("act", lambda e, h=h: e.activation(out=Sb(h), in_=Sm(h), func=AF.Copy), reads=[self.r_Sm[h]], writes=rSb(h))
        for m in range(min(16, GL2 - 1)):
            tok = slice(m * 128, (m + 1) * 128)
            for h in range(4):
                kb.add("pe", lambda e, h=h, m=m: e.matmul(I_ps(h), self.bigf(KD(h) + m * 128, 128), self.bigf(QD(h) + m * 128, 128), start=True, stop=True),
                       reads=self.bigres(KD(h) + m * 128, 128) + self.bigres(QD(h) + m * 128, 128), writes=[r_I[h]])
            for h in range(4):
                kb.add("dve", lambda e, h=h: e.tensor_tensor(out=AM(h), in0=I_ps(h), in1=gmask, op=ALU.mult), reads=[r_I[h], self.r_CST], writes=rAM(h))
            for par in range(2):
                n = 2 * m + par
                psl = slice(par * 64, par * 64 + 64)
                for h in range(4):
                    for e2 in range(2):
                        oc = Op_ps(h)[:, e2 * 128 + par * 64:e2 * 128 + (par + 1) * 64]
                        kb.add("pe", lambda e, oc=oc, e2=e2, par=par, h=h, m=m: e.matmul(oc, self.bigf(VT(h) + m * 256 + e2 * 128, 128), AM(h)[:, par * 64:(par + 1) * 64], start=True, stop=False),
                               reads=self.bigres(VT(h) + m * 256, 256) + rAM(h), writes=[r_Op[h]])
                        kb.add("pe", lambda e, oc=oc, e2=e2, par=par, h=h, m=m: e.matmul(oc, Sb(h)[:, e2 * 128:(e2 + 1) * 128], self.bigf(QD(h) + m * 128 + par * 64, 64), start=False, stop=True),
                               reads=rSb(h) + self.bigres(QD(h) + m * 128, 128), writes=[r_Op[h]])
                for h in range(4):
                    kb.add("pe", lambda e, psl=psl, h=h, m=m, par=par: e.matmul(dS_ps(h, par), self.BIGF[psl, KRT(h) + m * 128:KRT(h) + m * 128 + 128], self.BIGF[psl, VT(h) + m * 256:VT(h) + m * 256 + 256], start=True, stop=True),
                           reads=self.bigres(KRT(h) + m * 128, 128) + self.bigres(VT(h) + m * 256, 256), writes=[r_dS(h, par)])
                for h in range(4):
                    kb.add("dve", lambda e, h=h, n=n, par=par: e.scalar_tensor_tensor(out=Sm(h), in0=Sm(h), scalar=dec(h)[:, n:n + 1], in1=dS_ps(h, par), op0=ALU.mult, op1=ALU.add),
                           reads=[r_dS(h, par), self.r_SMALL, self.r_Sm[h]], writes=[self.r_Sm[h]])
                for h in range(4):
                    kb.add("act", lambda e, h=h: e.activation(out=Sb(h), in_=Sm(h), func=AF.Copy), reads=[self.r_Sm[h]], writes=rSb(h))
            for h in range(4):
                kb.add("act", lambda e, h=h: e.activation(out=SQ(h), in_=Op_ps(h), func=AF.Square), reads=[r_Op[h]], writes=rSQ(h))
            for h in range(4):
                for e2 in range(2):
                    kb.add("pe", lambda e, h=h, e2=e2: e.matmul(St_ps(h), self.ONESB[:], SQ(h)[:, e2 * 128:(e2 + 1) * 128], start=(e2 == 0), stop=(e2 == 1)), reads=rSQ(h) + [self.r_cb], writes=[r_St[h]])
            for h in range(4):
                kb.add("dve", lambda e, h=h: e.tensor_scalar(out=rsp(h), in0=St_ps(h), scalar1=1.0 / 256, scalar2=EPS, op0=ALU.mult, op1=ALU.add), reads=[r_St[h]], writes=[r_rs[h]])
            for h in range(4):
                kb.add("act", lambda e, h=h: e.activation(out=rsp(h), in_=rsp(h), func=AF.Sqrt), reads=[r_rs[h]], writes=[r_rs[h]])
            for h in range(4):
                kb.add("dve", lambda e, h=h: e.reciprocal(out=rsp(h), in_=rsp(h)), reads=[r_rs[h]], writes=[r_rs[h]])
            for h in range(4):
                for e2 in range(2):
                    gn = self.GBN[:, 4 * self.depth + l * 2 + e2:4 * self.depth + l * 2 + e2 + 1]
                    kb.add("dve", lambda e, h=h, e2=e2, gn=gn: e.scalar_tensor_tensor(out=tmp(h, e2), in0=Op_ps(h)[:, e2 * 128:(e2 + 1) * 128], scalar=gn, in1=rsp(h), op0=ALU.mult, op1=ALU.mult),
                           reads=[r_Op[h], r_rs[h], self.r_GBN], writes=[r_tmp[h]])
            for h in range(4):
                for e2 in range(2):
                    blk = 8 + 2 * h + e2
                    kb.add("pool", lambda e, h=h, e2=e2, blk=blk, tok=tok: e.tensor_tensor(out=self.HT[:, blk, tok], in0=tmp(h, e2), in1=self.HT[:, blk, tok], op=ALU.mult),
                           reads=[r_tmp[h], self.r_HT[blk][m // 8]], writes=[self.r_HT[blk][m // 8]])
        fence(sub + self.r_PS + self.Fres(2) + rSin)

    def mixer(self, l, x_src, r_xs, x_dst, r_xd, nxt):
        self.phase_inproj(l)
        self.phase_attn(l)
        self.phase_gla_prep(l)
        self.phase_gla_out(l)
        ht = lambda k, h: (self.HT[:, k, :], self.r_HT[k][h])
        self.phase_C(self.w_mo[l], NB, ht, None, True, False, self.OUT, self.r_OUT, 1.0)
        self.phase_E(l, 3, x_src, r_xs, x_dst, r_xd, nxt)


D = 2048
T = 2048
NB = 16
DFF = 5504
NFB = 43
FA = 22
EPS = 1e-6
MIXW = 4624


class Prog(MixerMixin):
    def __init__(self, depth=4, dbg=(), with_ffn=True):
        self.dbg = dbg
        DEPTH = depth
        self.depth = depth
        nc = bass.Bass("TRN2", target_bir_lowering=False)
        self.nc = nc
        self.kb = KB(nc)
        kb = self.kb

        def din(name, shape, dt=F32):
            return nc.dram_tensor(name, list(shape), dt, kind="ExternalInput").ap()

        def dscr(name, shape, dt=F32):
            return nc.dram_tensor(name, list(shape), dt).ap()

        self.xT = din("xT", [NB, 128, T])
        self.yT = nc.dram_tensor("yT", [NB, 128, T], F32, kind="ExternalOutput").ap()
        self.pos = din("pos", [1, T], I32)
        self.gcol_d = din("gcol", [128, DEPTH * 6 * NB])
        if with_ffn:
            self.w_in = din("ffn_w_in", [DEPTH, 2, D, 2 * DFF])
            self.w_out = din("ffn_w_out", [DEPTH, 2, DFF, D])
        self.w_mi = din("w_mix_in", [DEPTH, D, MIXW])
        self.w_mo = din("w_mix_out", [DEPTH, D, D])
        self.sinks_d = din("attn_sinks", [DEPTH, 16])
        self.w2_d = din("gla_gate_w2", [DEPTH, 16, 512])
        self.gb_d = din("gb_col", [128, DEPTH * 4])
        self.gn_d = din("gn_col", [128, DEPTH * 2])
        self.cst_d = din("cst", [128, 512])
        self.ident_d = din("ident", [128, 128])
        self.core_d = din("corec", [128, 520])
        self.XA = dscr("XA", [NB, 128, T])
        self.XB = dscr("XB", [NB, 128, T])
        self.PART = dscr("PART", [NB, 128, T])
        self.OUT = dscr("OUT", [NB, 128, T])
        self.r_XA, self.r_XB, self.r_PART, self.r_OUT = [kb.res(n) for n in ("XA", "XB", "PART", "OUT")]
        self.r_xin = kb.res("xin")
        self.r_y = kb.res("yT")
        self.HT = kb.sbuf("HT", [128, NB, T], BF16)
        self.r_HT = [[kb.res(f"HT{j}_{h}") for h in range(2)] for j in range(NB)]
        self.BIGF = kb.sbuf("BIG", [128, FA * T], BF16)
        self.BIG = self.BIGF[:].rearrange("p (a b) -> p a b", b=T)
        self.r_BIG = [[kb.res(f"BIG{j}_{h}") for h in range(2)] for j in range(FA)]
        self.W = [kb.sbuf(f"W{i}", [128, NB, 128], BF16) for i in range(4)]
        self.r_W = [kb.res(f"W{i}") for i in range(4)]
        self.F = [kb.sbuf(f"F{i}", [128, T], F32) for i in range(4)]
        self.r_F = [[kb.res(f"F{i}_{h}") for h in range(2)] for i in range(4)]
        self.CST = kb.sbuf("CST", [128, 512], F32)
        self.r_CST = kb.res("CST")
        self.GCOL = kb.sbuf("GCOL", [128, DEPTH * 6 * NB], F32)
        self.r_GCOL = kb.res("GCOL")
        self.ONESB = kb.sbuf("ONESB", [128, 128], BF16)
        self.IDB = kb.sbuf("IDB", [128, 128], BF16)
        self.r_cb = kb.res("cb")
        self.PS = kb.psum("PS", [128, 8 * 512], F32)
        self.r_PS = [kb.res(f"PS{i}") for i in range(8)]
        self.wslot = 0
        self.stg = 0
        self.load_consts()
        self.mixer_init()

    C_ONES = 0
    C_PERM = 128
    C_GMASK = 256
    C_INVF = 384
    C_SIGN = 385
    def cst(self, c0, n):
        return self.CST[:, c0:c0 + n]

    def load_consts(self):
        kb = self.kb
        kb.add("sp", lambda e: e.dma_start(out=self.CST[:], in_=self.cst_d), writes=[self.r_CST], dma=True)
        kb.add("sp", lambda e: e.dma_start(out=self.GCOL[:], in_=self.gcol_d), writes=[self.r_GCOL], dma=True)
        kb.add("act", lambda e: e.activation(out=self.ONESB[:], in_=self.cst(self.C_ONES, 128), func=AF.Copy), reads=[self.r_CST], writes=[self.r_cb])
        kb.add("pool", lambda e: e.dma_start(out=self.IDB[:], in_=self.ident_d), writes=[self.r_cb], dma=True, key="identld")

    def gcol(self, l, i, j):
        c = (l * 6 + i) * NB + j
        return self.GCOL[:, c:c + 1]

    def stats_to_rstd(self, fi, c, from_psum=False):
        kb = self.kb
        Fb = self.F[fi]
        rF = self.r_F[fi]
        for n in range(0 if from_psum else 4):
            h = n // 2
            kb.add("pe", lambda e, n=n: e.matmul(self.PS[:, n * 512:(n + 1) * 512], self.cst(self.C_ONES, 128), Fb[:, n * 512:(n + 1) * 512], start=True, stop=True),
                   reads=[rF[h], self.r_CST], writes=[self.r_PS[n]])
        for h in range(2):
            sl = slice(h * 1024, (h + 1) * 1024)
            kb.add("dve", lambda e, sl=sl: e.tensor_scalar(out=Fb[:, sl], in0=self.PS[:, sl], scalar1=1.0 / (D * c * c), scalar2=EPS / (c * c), op0=ALU.mult, op1=ALU.add),
                   reads=[self.r_PS[2 * h], self.r_PS[2 * h + 1]], writes=[rF[h]])
            kb.add("act", lambda e, sl=sl: e.activation(out=Fb[:, sl], in_=Fb[:, sl], func=AF.Sqrt), reads=[rF[h]], writes=[rF[h]])
            kb.add("dve", lambda e, sl=sl: e.reciprocal(out=Fb[:, sl], in_=Fb[:, sl]), reads=[rF[h]], writes=[rF[h]])

    def next_stg(self):
        s = self.stg
        self.stg = (s + 1) % 4
        fi, h = 2 + s // 2, s % 2
        return self.F[fi][:, h * 1024:(h + 1) * 1024], self.r_F[fi][h]

    def phase_prenorm_from(self, x_ap, r_x, l, gi):
        kb = self.kb
        acc = self.F[0]
        kb.add("dve", lambda e: e.memset(acc[:], 0.0), writes=self.r_F[0])
        for j in range(NB):
            for h in range(2):
                sl = slice(h * 1024, (h + 1) * 1024)
                st, rst = self.next_stg()
                kb.add("sp", lambda e, st=st, j=j, sl=sl: e.dma_start(out=st, in_=x_ap[j][:, sl]), reads=[r_x], writes=[rst], dma=True)
                kb.add("act", lambda e, st=st, j=j, sl=sl: e.activation(out=self.HT[:, j, sl], in_=st, func=AF.Copy, scale=self.gcol(l, gi, j)),
                       reads=[rst, self.r_GCOL], writes=[self.r_HT[j][h]])
                kb.add("pool", lambda e, st=st: e.tensor_tensor(out=st, in0=st, in1=st, op=ALU.mult), reads=[rst], writes=[rst])
                kb.add("pool", lambda e, st=st, sl=sl: e.tensor_tensor(out=acc[:, sl], in0=acc[:, sl], in1=st, op=ALU.add), reads=[rst, self.r_F[0][h]], writes=[self.r_F[0][h]])
        self.stats_to_rstd(0, 1.0)

    def load_w(self, src_ap, nk):
        kb = self.kb
        s = self.wslot
        self.wslot = (s + 1) % 4
        Wt = self.W[s]
        kb.add("pool", lambda e: e.dma_start(out=Wt[:, 0:nk, :], in_=src_ap.rearrange("(kc p) m -> p kc m", p=128)), writes=[self.r_W[s]], dma=True)
        return Wt, self.r_W[s]

    def phase_ffn_B(self, l, i, f0, nf):
        kb = self.kb
        rstd = self.F[0]
        unit = getattr(self, "_bunit", 0)
        def ldB(b):
            fb = f0 + b
            return (self.load_w(self.w_in[l, i][:, fb * 128:(fb + 1) * 128], NB),
                    self.load_w(self.w_in[l, i][:, DFF + fb * 128:DFF + (fb + 1) * 128], NB))
        nxtw = ldB(0)
        for b in range(nf):
            (Wg, rWg), (Wu, rWu) = nxtw
            if b + 1 < nf:
                nxtw = ldB(b + 1)
            for h in range(2):
                base = (unit % 2) * 4
                unit += 1
                for (Wt, rW, boff) in ((Wg, rWg, 0), (Wu, rWu, 2)):
                    for n in range(2):
                        bank = base + boff + n
                        tsl = slice(h * 1024 + n * 512, h * 1024 + (n + 1) * 512)
                        for k in range(NB):
                            kb.add("pe", lambda e, Wt=Wt, k=k, tsl=tsl, bank=bank: e.matmul(self.PS[:, bank * 512:(bank + 1) * 512], Wt[:, k, :], self.HT[:, k, tsl], start=(k == 0), stop=(k == NB - 1)),
                                   reads=[rW, self.r_HT[k][h]], writes=[self.r_PS[bank]])
                sl = slice(h * 1024, (h + 1) * 1024)
                g_ps = self.PS[:, base * 512:(base + 2) * 512]
                u_ps = self.PS[:, (base + 2) * 512:(base + 4) * 512]
                s1, r1 = self.next_stg()
                s2, r2 = self.next_stg()
                kb.add("dve", lambda e, s1=s1, g_ps=g_ps, sl=sl: e.tensor_tensor(out=s1, in0=g_ps, in1=rstd[:, sl], op=ALU.mult),
                       reads=[self.r_PS[base], self.r_PS[base + 1], self.r_F[0][h]], writes=[r1])
                kb.add("act", lambda e, s1=s1: e.activation(out=s1, in_=s1, func=AF.Silu), reads=[r1], writes=[r1])
                kb.add("dve", lambda e, s2=s2, u_ps=u_ps, sl=sl: e.tensor_tensor(out=s2, in0=u_ps, in1=rstd[:, sl], op=ALU.mult),
                       reads=[self.r_PS[base + 2], self.r_PS[base + 3], self.r_F[0][h]], writes=[r2])
                kb.add("dve", lambda e, s1=s1, s2=s2, b=b, sl=sl: e.tensor_tensor(out=self.BIG[:, b, sl], in0=s1, in1=s2, op=ALU.mult),
                       reads=[r1, r2], writes=[self.r_BIG[b][h]])
        self._bunit = unit

    def phase_C(self, w_ap, nk, xsrc_blocks, r_src, last, part_in, out_ap, r_out, post_c, resident=False):
        kb = self.kb
        acc = self.F[1]
        if last:
            kb.add("dve", lambda e: e.memset(acc[:], 0.0), writes=self.r_F[1])
        unit = getattr(self, "_cunit", 0)
        def ldC(j):
            a = self.load_w(w_ap[0:min(nk, NB) * 128, j * 128:(j + 1) * 128], min(nk, NB))
            b_ = self.load_w(w_ap[NB * 128:nk * 128, j * 128:(j + 1) * 128], nk - NB) if nk > NB else (None, None)
            return a, b_
        nxtw = ldC(0)
        for j in range(NB):
            (Wa, rWa), (Wb, rWb) = nxtw
            if j + 1 < NB:
                nxtw = ldC(j + 1)
            for h in range(2):
                base = (unit % 4) * 2
                unit += 1
                sl = slice(h * 1024, (h + 1) * 1024)
                res_here = bool(last and resident and j < 8)
                if res_here:
                    st, rstl = self.HT[:, 2 * j + h, :].bitcast(F32), list(self.r_HT[2 * j + h])
                else:
                    st, rst = self.next_stg()
                    rstl = [rst]
                if last and part_in:
                    kb.add("sp", lambda e, st=st, j=j, sl=sl: e.dma_start(out=st, in_=self.PART[j][:, sl]), reads=[self.r_PART], writes=rstl, dma=True)
                for n in range(2):
                    bank = base + n
                    tsl = slice(h * 1024 + n * 512, h * 1024 + (n + 1) * 512)
                    for k in range(nk):
                        Wt, rW, kk = (Wa, rWa, k) if k < NB else (Wb, rWb, k - NB)
                        src, rs = xsrc_blocks(k, h)
                        kb.add("pe", lambda e, Wt=Wt, kk=kk, src=src, tsl=tsl, bank=bank, k=k: e.matmul(self.PS[:, bank * 512:(bank + 1) * 512], Wt[:, kk, :], src[:, tsl], start=(k == 0), stop=(k == nk - 1)),
                               reads=[rW, rs], writes=[self.r_PS[bank]])
                ps = self.PS[:, base * 512:(base + 2) * 512]
                rps = [self.r_PS[base], self.r_PS[base + 1]]
                if not last:
                    kb.add("act", lambda e, st=st, ps=ps: e.activation(out=st, in_=ps, func=AF.Copy), reads=rps, writes=rstl)
                    kb.add("sp", lambda e, st=st, j=j, sl=sl: e.dma_start(out=self.PART[j][:, sl], in_=st), reads=rstl, writes=[self.r_PART], dma=True, par=True)
                else:
                    if part_in:
                        kb.add("dve", lambda e, st=st, ps=ps: e.tensor_tensor(out=st, in0=ps, in1=st, op=ALU.add), reads=rps + rstl, writes=rstl)
                    else:
                        kb.add("act", lambda e, st=st, ps=ps: e.activation(out=st, in_=ps, func=AF.Copy), reads=rps, writes=rstl)
                    if not res_here:
                        kb.add("sp", lambda e, st=st, j=j, sl=sl: e.dma_start(out=out_ap[j][:, sl], in_=st), reads=rstl, writes=[r_out], dma=True, par=True)
                    s2, r2 = self.next_stg()
                    kb.add("act", lambda e, st=st, s2=s2: e.activation(out=s2, in_=st, func=AF.Square), reads=rstl, writes=[r2])
                    kb.add("dve", lambda e, s2=s2, sl=sl: e.tensor_tensor(out=acc[:, sl], in0=acc[:, sl], in1=s2, op=ALU.add), reads=[r2, self.r_F[1][h]], writes=[self.r_F[1][h]])
        self._cunit = unit
        if last:
            self.stats_to_rstd(1, post_c)

    def phase_E(self, l, gi_post, x_src, r_xs, x_dst, r_xd, nxt, resident=False):
        kb = self.kb
        rstd = self.F[1]
        acc = self.F[0]
        units = [(j, h) for j in range(NB) for h in range(2)]
        sbuf_ = lambda i: (self.BIGF[:, i * T:(i + 1) * T].bitcast(F32), self.r_BIG[i])
        state = {"n": 0}

        def loads(j, h):
            sl = slice(h * 1024, (h + 1) * 1024)
            i = state["n"]
            state["n"] = (i + 2) % 16
            s1, r1 = sbuf_(i)
            s2, r2 = sbuf_(i + 1)
            sq, rq_ = s1, r1
            if resident and j < 8:
                s1, r1 = self.HT[:, 2 * j + h, :].bitcast(F32), list(self.r_HT[2 * j + h])
            else:
                kb.add("sp", lambda e, s1=s1, j=j, sl=sl: e.dma_start(out=s1, in_=self.OUT[j][:, sl]), reads=[self.r_OUT], writes=r1, dma=True, key=f"dma_Estg{i}")
            kb.add("sp", lambda e, s2=s2, j=j, sl=sl: e.dma_start(out=s2, in_=x_src[j][:, sl]), reads=[r_xs], writes=r2, dma=True, key=f"dma_Estg{i + 1}")
            return s1, r1, s2, r2, sq, rq_
        DEPTH_PF = 6
        pre = [loads(*units[i]) for i in range(DEPTH_PF)]
        for ui, (j, h) in enumerate(units):
            sl = slice(h * 1024, (h + 1) * 1024)
            s1, r1, s2, r2, sq, rq_ = pre.pop(0)
            if ui + DEPTH_PF < len(units):
                pre.append(loads(*units[ui + DEPTH_PF]))
            kb.add("pool", lambda e, s1=s1, sl=sl: e.tensor_tensor(out=s1, in0=s1, in1=rstd[:, sl], op=ALU.mult), reads=r1 + [self.r_F[1][h]], writes=r1)
            kb.add("dve", lambda e, s1=s1, s2=s2, j=j: e.scalar_tensor_tensor(out=s2, in0=s1, scalar=self.gcol(l, gi_post, j), in1=s2, op0=ALU.mult, op1=ALU.add),
                   reads=r1 + r2 + [self.r_GCOL], writes=r2)
            kb.add("sp", lambda e, s2=s2, j=j, sl=sl: e.dma_start(out=x_dst[j][:, sl], in_=s2), reads=r2, writes=[r_xd], dma=True, par=True)
            if nxt is not None:
                nl, ng = nxt
                kb.add("act", lambda e, s2=s2, j=j, sl=sl, nl=nl, ng=ng: e.activation(out=self.HT[:, j, sl], in_=s2, func=AF.Copy, scale=self.gcol(nl, ng, j)),
                       reads=r2 + [self.r_GCOL], writes=[self.r_HT[j][h]])
                kb.add("act", lambda e, sq=sq, s2=s2: e.activation(out=sq, in_=s2, func=AF.Square), reads=r2, writes=rq_)
                for n in range(2):
                    bank = 2 * h + n
                    kb.add("pe", lambda e, sq=sq, n=n, bank=bank, j=j: e.matmul(self.PS[:, bank * 512:(bank + 1) * 512], self.cst(self.C_ONES, 128), sq[:, n * 512:(n + 1) * 512], start=(j == 0), stop=(j == NB - 1)),
                           reads=rq_ + [self.r_CST], writes=[self.r_PS[bank]])
        if nxt is not None:
            self.stats_to_rstd(0, 1.0, from_psum=True)

    def ffn(self, l, i, x_src, r_xs, x_dst, r_xd, nxt):
        big = lambda k, h: (self.BIG[:, k, :], self.r_BIG[k][h])
        self.phase_ffn_B(l, i, 0, FA)
        self.phase_C(self.w_out[l, i][0:FA * 128, :], FA, big, None, False, False, None, None, None)
        self.phase_ffn_B(l, i, FA, NFB - FA)
        self.phase_C(self.w_out[l, i][FA * 128:DFF, :], NFB - FA, big, None, True, True, self.OUT, self.r_OUT, 0.5, resident=True)
        self.phase_E(l, 1 + 4 * i, x_src, r_xs, x_dst, r_xd, nxt, resident=True)

    def finish(self):
        kb = self.kb
        kb.add("sp", lambda e: e.nop(), reads=[self.r_y])
        kb.emit()
        kb.close()
        return self.nc


def make_consts():
    c = np.zeros((128, 512), np.float32)
    c[:, 0:128] = 1.0
    P = np.zeros((128, 128), np.float32)
    for m in range(128):
        k = (m // 64) * 64 + ((m % 64) + 32) % 64
        P[k, m] = 1.0
    c[:, 128:256] = P
    s = np.arange(128)[:, None]; t = np.arange(128)[None, :]
    c[:, 256:384] = ((s // 64 == t // 64) & (s <= t)).astype(np.float32)
    inv_freq = (10000.0 ** (-np.arange(0, 64, 2, dtype=np.float32) / 64)).astype(np.float32)
    d = np.arange(128) % 64
    c[:, 384] = inv_freq[d % 32]
    c[:, 385] = np.where(d < 32, -1.0, 1.0)
    return c

def make_core_consts(rank):
    c = np.zeros((128, 520), np.float32)
    qi = np.arange(128)[:, None]; kj = np.arange(256)[None, :]
    diff = qi + 128 - kj
    band = (diff >= 0) & (diff < 128)
    c[:, 0:256] = np.where(band, 0.0, -30000.0)
    first = band & (kj >= 128) if rank == 0 else band
    c[:, 256:512] = np.where(first, 0.0, -30000.0)
    if rank > 0:
        c[:, 512 + rank - 1] = 1.0
    for r in range(4):
        if r < rank:
            c[:, 516 + r] = 1.0
    return c


def build_full(depth=4):
    p = Prog(depth=depth)
    p.phase_rope_tables()
    p.phase_prenorm_from(p.xT, p.r_xin, 0, 0)
    locs = [(p.XA, p.r_XA), (p.XB, p.r_XB)]
    src = (p.xT, p.r_xin)
    k = 0
    nsub = 3 * depth
    for l in range(depth):
        for sub in range(3):
            dst = (p.yT, p.r_y) if k == nsub - 1 else locs[k % 2]
            if sub == 0:
                p.ffn(l, 0, src[0], src[1], dst[0], dst[1], (l, 2))
            elif sub == 1:
                p.mixer(l, src[0], src[1], dst[0], dst[1], (l, 4))
            else:
                p.ffn(l, 1, src[0], src[1], dst[0], dst[1], (l + 1, 0) if l + 1 < depth else None)
            src = dst
            k += 1
    return p.finish()


def kernel(x, positions, norm_gains, ffn_w_in, ffn_w_out, w_mix_in, attn_sinks, gla_gate_w2, gla_gate_b, gla_norm_gain, w_mix_out):
    x = np.asarray(x, dtype=np.float32)
    positions = np.asarray(positions, dtype=np.int32)
    depth = int(np.asarray(norm_gains).shape[0])
    B, S, _ = x.shape
    f32 = lambda a: np.ascontiguousarray(np.asarray(a, dtype=np.float32))
    shared = dict(
        gcol=np.ascontiguousarray(f32(norm_gains).reshape(depth * 6 * NB, 128).T),
        ffn_w_in=f32(ffn_w_in), ffn_w_out=f32(ffn_w_out), w_mix_in=f32(w_mix_in), w_mix_out=f32(w_mix_out),
        attn_sinks=f32(attn_sinks), gla_gate_w2=f32(gla_gate_w2),
        gb_col=np.ascontiguousarray(f32(gla_gate_b).reshape(depth * 4, 128).T),
        gn_col=np.ascontiguousarray(f32(gla_norm_gain).reshape(depth * 2, 128).T),
        cst=make_consts(), ident=np.eye(128, dtype=np.float32))
    ins = []
    for core in range(8):
        b, r = core // 4, core % 4
        d = dict(shared)
        d["xT"] = np.ascontiguousarray(x[b, r * T:(r + 1) * T].T).reshape(NB, 128, T)
        d["pos"] = np.ascontiguousarray(positions[b, r * T:(r + 1) * T][None])
        d["corec"] = make_core_consts(r)
        ins.append(d)
    nc = build_full(depth)
    res = run_bass_kernel_spmd(nc, ins, core_ids=list(range(8)))
    y = np.empty((B, S, D), np.float32)
    for core in range(8):
        b, r = core // 4, core % 4
        y[b, r * T:(r + 1) * T] = res.results[core]["yT"].reshape(D, T).T
    return y
```

```python
import numpy as np
from contextlib import ExitStack
import concourse.bass as bass
import concourse.mybir as mybir
from concourse.bass_utils import run_bass_kernel_spmd

F32 = mybir.dt.float32
BF16 = mybir.dt.bfloat16
I32 = mybir.dt.int32
AF = mybir.ActivationFunctionType
ALU = mybir.AluOpType
AX = mybir.AxisListType

ENGS = ("pe", "act", "dve", "pool", "sp")


class Res:
    __slots__ = ("name", "ws", "rs", "prs")

    def __init__(self, name):
        self.name = name
        self.ws = []
        self.rs = []
        self.prs = []


class Op:
    __slots__ = ("eng", "fn", "deps", "marked", "val", "dma", "key", "idx", "inc")

    def __init__(self, eng, fn, dma, key, idx, inc=16):
        self.inc = inc
        self.eng = eng
        self.fn = fn
        self.deps = []
        self.marked = False
        self.val = None
        self.dma = dma
        self.key = key
        self.idx = idx


class KB:
    def __init__(self, nc):
        self.nc = nc
        self.ops = {e: [] for e in ENGS}
        self.stack = ExitStack()
        self.nres = 0

    def sbuf(self, name, shape, dtype):
        return self.stack.enter_context(self.nc.sbuf_tensor(name, list(shape), dtype))

    def psum(self, name, shape, dtype):
        return self.stack.enter_context(self.nc.psum_tensor(name, list(shape), dtype))

    def res(self, name=None):
        self.nres += 1
        return Res(name or f"r{self.nres}")

    def add(self, eng, fn, reads=(), writes=(), dma=False, key=None, inc=16, par=False):
        ops = self.ops[eng]
        if dma and key is None:
            key = "dma_" + writes[0].name
        op = Op(eng, fn, dma, key, len(ops), inc)
        deps = []
        for r in reads:
            for o in r.ws:
                deps.append((o, "raw"))
        for w in writes:
            if w.rs:
                for o in w.rs:
                    deps.append((o, "war"))
            elif par:
                for o in w.prs:
                    deps.append((o, "war"))
            else:
                for o in w.ws:
                    deps.append((o, "waw"))
        for d, kind in deps:
            if d is op:
                continue
            if d.eng == eng and not d.dma and not dma:
                if eng == "pe":
                    continue
                if kind != "raw" or (op.idx - d.idx) > 2:
                    continue
            if d.eng == eng and d.dma and dma and False:
                continue
            d.marked = True
            op.deps.append(d)
        for r in reads:
            r.rs.append(op)
        for w in writes:
            if w.rs:
                w.prs = w.rs
                w.rs = []
                w.ws = [op]
            elif par:
                w.ws.append(op)
            else:
                w.ws = [op]
        ops.append(op)
        return op

    def emit(self):
        nc = self.nc
        keys = {}
        for e in ENGS:
            cnt = 0
            for op in self.ops[e]:
                if op.dma:
                    keys[op.key] = keys.get(op.key, 0) + op.inc
                    op.val = keys[op.key]
                elif op.marked:
                    cnt += 1
                    op.val = cnt
            self.nmarks = getattr(self, "nmarks", {})
            self.nmarks[e] = cnt
        EPOCH = 20000
        nep = {e: self.nmarks[e] // EPOCH + 1 for e in ENGS}
        semnames = [f"{e}{i}" for e in ENGS for i in range(nep[e])] + sorted(keys.keys())
        sems = {}
        for n in semnames:
            sems[n] = self.stack.enter_context(nc.semaphore("s_" + n))
        self.nsems = len(semnames)

        def run(engname, engobj):
            seen = {}
            for op in self.ops[engname]:
                need = {}
                for d in op.deps:
                    k = d.key if d.dma else d.eng
                    if d.val > need.get(k, 0):
                        need[k] = d.val
                for k, v in need.items():
                    if seen.get(k, 0) >= v:
                        continue
                    seen[k] = v
                    if k in ENGS:
                        ep, vv = (v - 1) // EPOCH, (v - 1) % EPOCH + 1
                        engobj.wait_ge(sems[f"{k}{ep}"], vv)
                    else:
                        engobj.wait_ge(sems[k], v)
                ins = op.fn(engobj)
                if op.dma:
                    ins.then_inc(sems[op.key], op.inc)
                elif op.marked:
                    ins.then_inc(sems[f"{engname}{(op.val - 1) // EPOCH}"], 1)

        with nc.Block() as block:
            @block.tensor
            def _(e):
                run("pe", e)

            @block.scalar
            def _(e):
                run("act", e)

            @block.vector
            def _(e):
                run("dve", e)

            @block.gpsimd
            def _(e):
                run("pool", e)

            @block.sync
            def _(e):
                run("sp", e)

    def close(self):
        self.stack.close()


NPROJ = 41
P_QA, P_KA, P_VA, P_QG, P_KG, P_VG, P_RG, P_GLR = 0, 8, 12, 16, 20, 24, 32, 40
O_QR, O_KT, O_V2, O_TMPB, O_GKV, O_PN, O_PT = 0, 16384, 25088, 34816, 36864, 40960, 44032
KTW = 2176
O_TMP2, O_AM, O_SQP, O_SB = 40960, 43008, 43264, 43776
S_DEC, S_DTOT, S_ATT = 0, 128, 136
TWO_PI_HI = 6.28125
TWO_PI_LO = 6.283185307179586 - 6.28125
MAGIC = 12582912.0


def _proj_cols(m):
    if m < 8:
        return [(m * 128, 128)]
    if m < 12:
        g = m - 8
        return [(1024 + g * 64, 64)] * 2
    if m < 16:
        g = m - 12
        return [(1280 + g * 64, 64)] * 2
    if m < 20:
        return [(1536 + (m - 16) * 128, 128)]
    if m < 24:
        return [(2048 + (m - 20) * 128, 128)]
    if m < 32:
        return [(2560 + (m - 24) * 128, 128)]
    if m < 40:
        return [(3584 + (m - 32) * 128, 128)]
    return [(4608, 16)]


class MixerMixin:
    def mixer_init(self):
        kb, nc = self.kb, self.nc
        dscr = lambda name, shape, dt=F32: nc.dram_tensor(name, list(shape), dt).ap()
        self.PROJ = dscr("PROJ", [NPROJ, 128, T])
        self.r_PROJ = kb.res("PROJ")
        self.COSD = dscr("COSD", [128, T]); self.SIND = dscr("SIND", [128, T])
        self.r_ROPE = kb.res("ROPE")
        self.KVsrc = dscr("KVsrc", [128, 1024], BF16); self.KVdst = dscr("KVdst", [512, 1024], BF16)
        self.Ssrc = dscr("Ssrc", [128, 1028]); self.Sdst = dscr("Sdst", [512, 1028])
        self.r_KVsrc, self.r_KVdst, self.r_Ssrc, self.r_Sdst = [kb.res(n) for n in ("KVsrc", "KVdst", "Ssrc", "Sdst")]
        self.CORE = kb.sbuf("CORE", [128, 520], F32); self.r_CORE = kb.res("CORE")
        self.SINK = kb.sbuf("SINK", [128, 16], F32); self.r_SINK = kb.res("SINK")
        self.SINKP = kb.sbuf("SINKP", [128, 16], F32); self.r_SINKP = kb.res("SINKP")
        self.GBN = kb.sbuf("GBN", [128, 6 * self.depth], F32); self.r_GBN = kb.res("GBN")
        self.SMALL = kb.sbuf("SMALL", [128, 256], F32); self.r_SMALL = kb.res("SMALL")
        self.r_att = [kb.res("att0"), kb.res("att1"), kb.res("att2")]
        self.r_att2 = [kb.res("att20"), kb.res("att21"), kb.res("att22")]
        self.r_Sm = [kb.res(f"Sm{h}") for h in range(4)]
        kb.add("sp", lambda e: e.dma_start(out=self.CORE[:], in_=self.core_d), writes=[self.r_CORE], dma=True)
        kb.add("sp", lambda e: e.dma_start(out=self.GBN[:, 0:4 * self.depth], in_=self.gb_d), writes=[self.r_GBN], dma=True, key="gbn")
        kb.add("sp", lambda e: e.dma_start(out=self.GBN[:, 4 * self.depth:6 * self.depth], in_=self.gn_d), writes=[self.r_GBN], dma=True, key="gbn", par=True)

    def bigf(self, lo, n):
        return self.BIGF[:, lo:lo + n]

    def bigres(self, lo, n):
        out = []
        for blk in range(lo // 2048, (lo + n - 1) // 2048 + 1):
            for h in range(2):
                a = blk * 2048 + h * 1024
                if a < lo + n and a + 1024 > lo:
                    out.append(self.r_BIG[blk][h])
        return out

    def Fres(self, i):
        return [self.r_F[i][0], self.r_F[i][1]]

    def fence(self, rlist):
        self.kb.add("dve", lambda e: e.memset(self.SMALL[:, 255:256], 0.0), reads=rlist, writes=rlist)

    def phase_rope_tables(self):
        kb = self.kb
        A, B_, C_, Pi = self.F[0], self.F[1], self.F[2], self.F[3]
        rA, rB, rC, rP = self.Fres(0), self.Fres(1), self.Fres(2), self.Fres(3)
        invf = self.cst(self.C_INVF, 1); sign = self.cst(self.C_SIGN, 1)
        kb.add("sp", lambda e: e.dma_start(out=Pi[:].bitcast(I32), in_=self.pos.partition_broadcast(128)), writes=rP, dma=True, key="posld")
        kb.add("dve", lambda e: e.tensor_copy(out=A[:], in_=Pi[:].bitcast(I32)), reads=rP, writes=rA)
        kb.add("dve", lambda e: e.tensor_scalar(out=A[:], in0=A[:], scalar1=invf, scalar2=0.0, op0=ALU.mult, op1=ALU.add), reads=rA + [self.r_CST], writes=rA)

        def reduce_sin(src, rsrc, dst, rdst, tmp, rtmp):
            kb.add("dve", lambda e: e.tensor_scalar(out=tmp[:], in0=src[:], scalar1=float(1.0 / (2 * np.pi)), scalar2=MAGIC, op0=ALU.mult, op1=ALU.add), reads=rsrc, writes=rtmp)
            kb.add("dve", lambda e: e.tensor_scalar(out=tmp[:], in0=tmp[:], scalar1=-MAGIC, scalar2=1.0, op0=ALU.add, op1=ALU.mult), reads=rtmp, writes=rtmp)
            kb.add("dve", lambda e: e.scalar_tensor_tensor(out=dst[:], in0=tmp[:], scalar=-TWO_PI_HI, in1=src[:], op0=ALU.mult, op1=ALU.add), reads=rtmp + rsrc, writes=rdst)
            kb.add("dve", lambda e: e.scalar_tensor_tensor(out=dst[:], in0=tmp[:], scalar=-TWO_PI_LO, in1=dst[:], op0=ALU.mult, op1=ALU.add), reads=rtmp + rdst, writes=rdst)
            kb.add("dve", lambda e: e.tensor_scalar(out=dst[:], in0=dst[:], scalar1=3.1415925, scalar2=-3.1415925, op0=ALU.min, op1=ALU.max), reads=rdst, writes=rdst)

        reduce_sin(A, rA, B_, rB, C_, rC)
        kb.add("act", lambda e: e.activation(out=C_[:], in_=B_[:], func=AF.Sin), reads=rB, writes=rC)
        kb.add("dve", lambda e: e.tensor_scalar(out=C_[:], in0=C_[:], scalar1=sign, scalar2=0.0, op0=ALU.mult, op1=ALU.add), reads=rC + [self.r_CST], writes=rC)
        kb.add("sp", lambda e: e.dma_start(out=self.SIND, in_=C_[:]), reads=rC, writes=[self.r_ROPE], dma=True, key="rope")
        kb.add("dve", lambda e: e.tensor_scalar(out=A[:], in0=B_[:], scalar1=float(np.pi / 2), scalar2=0.0, op0=ALU.add, op1=ALU.add), reads=rB, writes=rA)
        reduce_sin(A, rA, B_, rB, Pi, rP)
        kb.add("act", lambda e: e.activation(out=A[:], in_=B_[:], func=AF.Sin), reads=rB, writes=rA)
        kb.add("sp", lambda e: e.dma_start(out=self.COSD, in_=A[:]), reads=rA, writes=[self.r_ROPE], dma=True, key="rope", par=True)

    def phase_inproj(self, l):
        kb = self.kb
        rstd = self.F[0]
        unit = 0

        def ld(m):
            runs = _proj_cols(m)
            s = self.wslot
            self.wslot = (s + 1) % 4
            Wt = self.W[s]
            c = 0
            for ri, (c0, n) in enumerate(runs):
                kb.add("pool", lambda e, c=c, c0=c0, n=n: e.dma_start(out=Wt[:, :, c:c + n], in_=self.w_mi[l][:, c0:c0 + n].rearrange("(kc p) m -> p kc m", p=128)),
                       writes=[self.r_W[s]], dma=True, par=(ri > 0))
                c += n
            return Wt, self.r_W[s], c
        import os
        mlist = [int(v) for v in os.environ.get("MLIST", "").split(",") if v] or list(range(NPROJ))
        nxtw = ld(mlist[0])
        for mi, m in enumerate(mlist):
            Wt, rW, ncol = nxtw
            if mi + 1 < len(mlist):
                nxtw = ld(mlist[mi + 1])
            for h in range(2):
                base = (unit % 4) * 2
                unit += 1
                sl = slice(h * 1024, (h + 1) * 1024)
                for n in range(2):
                    bank = base + n
                    tsl = slice(h * 1024 + n * 512, h * 1024 + (n + 1) * 512)
                    for k in range(NB):
                        kb.add("pe", lambda e, Wt=Wt, k=k, tsl=tsl, bank=bank, ncol=ncol: e.matmul(self.PS[0:ncol, bank * 512:(bank + 1) * 512], Wt[:, k, 0:ncol], self.HT[:, k, tsl], start=(k == 0), stop=(k == NB - 1)),
                               reads=[rW, self.r_HT[k][h]], writes=[self.r_PS[bank]])
                st, rst = self.next_stg()
                kb.add("dve", lambda e, st=st, base=base, sl=sl, ncol=ncol: e.tensor_tensor(out=st[0:ncol, :], in0=self.PS[0:ncol, base * 512:(base + 2) * 512], in1=rstd[0:ncol, sl], op=ALU.mult),
                       reads=[self.r_PS[base], self.r_PS[base + 1], self.r_F[0][h]], writes=[rst])
                kb.add("sp", lambda e, st=st, m=m, sl=sl, ncol=ncol: e.dma_start(out=self.PROJ[m][0:ncol, sl], in_=st[0:ncol, :]), reads=[rst], writes=[self.r_PROJ], dma=True, par=True)

    def rope_block(self, m, dst_lo, COS, SIN, rT):
        kb = self.kb
        for h in range(2):
            sl = slice(h * 1024, (h + 1) * 1024)
            X, rX = self.F[2][:, sl], [self.r_F[2][h]]
            Tm, rTm = self.F[3][:, sl], [self.r_F[3][h]]
            kb.add("sp", lambda e, X=X, sl=sl: e.dma_start(out=X, in_=self.PROJ[m][:, sl]), reads=[self.r_PROJ], writes=rX, dma=True)
            pb = 4 + 2 * h
            for n in range(2):
                kb.add("pe", lambda e, n=n, X=X, pb=pb: e.matmul(self.PS[:, (pb + n) * 512:(pb + n + 1) * 512], self.cst(self.C_PERM, 128), X[:, n * 512:(n + 1) * 512], start=True, stop=True),
                       reads=rX + [self.r_CST], writes=[self.r_PS[pb + n]])
            kb.add("dve", lambda e, Tm=Tm, pb=pb, sl=sl: e.tensor_tensor(out=Tm, in0=self.PS[:, pb * 512:(pb + 2) * 512], in1=SIN[:, sl], op=ALU.mult), reads=[self.r_PS[pb], self.r_PS[pb + 1]] + rT, writes=rTm)
            kb.add("pool", lambda e, X=X, sl=sl: e.tensor_tensor(out=X, in0=X, in1=COS[:, sl], op=ALU.mult), reads=rX + rT, writes=rX)
            kb.add("dve", lambda e, X=X, Tm=Tm, h=h: e.tensor_tensor(out=self.bigf(dst_lo + h * 1024, 1024), in0=X, in1=Tm, op=ALU.add), reads=rX + rTm, writes=self.bigres(dst_lo + h * 1024, 1024))

    def phase_attn(self, l):
        kb = self.kb
        COS, SIN = self.F[0], self.F[1]
        rT = self.Fres(0) + self.Fres(1)
        kb.add("sp", lambda e: e.dma_start(out=COS[:], in_=self.COSD), reads=[self.r_ROPE], writes=self.Fres(0), dma=True, key="ldcos")
        kb.add("sp", lambda e: e.dma_start(out=SIN[:], in_=self.SIND), reads=[self.r_ROPE], writes=self.Fres(1), dma=True, key="ldsin")
        kb.add("sp", lambda e: e.dma_start(out=self.SINK[:], in_=self.sinks_d[l:l + 1, :].partition_broadcast(128)), writes=[self.r_SINK], dma=True)
        kb.add("dve", lambda e: e.tensor_copy(out=self.SINKP[:].rearrange("p (g b a) -> p g b a", b=2, a=2), in_=self.SINK[:].rearrange("p (g a b) -> p g b a", a=2, b=2)), reads=[self.r_SINK], writes=[self.r_SINKP])
        for g in range(4):
            self.rope_block(P_KA + g, O_KT + g * KTW + 128, COS, SIN, rT)
        import os
        ATT = int(os.environ.get("ATT", "9"))
        if ATT < 2:
            return
        X = self.F[2]; rX = self.Fres(2)
        for g in range(4):
            kb.add("sp", lambda e, g=g: e.dma_start(out=X[:], in_=self.PROJ[P_VA + g]), reads=[self.r_PROJ], writes=rX, dma=True, key="ropeX")
            kb.add("act", lambda e: e.activation(out=self.bigf(O_TMPB, T), in_=X[:], func=AF.Copy), reads=rX, writes=self.bigres(O_TMPB, T))
            for half in range(2):
                bank = 4 + half
                pst = self.PS[:, bank * 512:(bank + 1) * 512].bitcast(BF16)
                for tt in range(8):
                    t = half * 8 + tt
                    kb.add("pe", lambda e, t=t, tt=tt, pst=pst: e.transpose(out=pst[:, tt * 128:(tt + 1) * 128], in_=self.bigf(O_TMPB + t * 128, 128), identity=self.IDB[:]),
                           reads=self.bigres(O_TMPB + t * 128, 128) + [self.r_cb], writes=[self.r_PS[bank]])
                lo = O_V2 + (1 + half * 8) * 512 + g * 128
                dst = self.BIGF[:, lo:lo + 8 * 512].rearrange("p (t c) -> p t c", c=512)[:, :, 0:128]
                kb.add("act", lambda e, dst=dst, pst=pst: e.activation(out=dst, in_=pst.rearrange("p (t c) -> p t c", c=128), func=AF.Copy),
                       reads=[self.r_PS[bank]], writes=self.bigres(O_V2 + (1 + half * 8) * 512, 8 * 512))
        if ATT < 3:
            return
        for b in range(8):
            self.rope_block(P_QA + b, O_QR + b * T, COS, SIN, rT)
        if ATT < 4:
            return
        for g in range(4):
            kb.add("sp", lambda e, g=g: e.dma_start(out=self.KVsrc[:, g * 128:(g + 1) * 128], in_=self.bigf(O_KT + g * KTW + T, 128)),
                   reads=self.bigres(O_KT + g * KTW + T, 128), writes=[self.r_KVsrc], dma=True, par=(g > 0))
        kb.add("sp", lambda e: e.dma_start(out=self.KVsrc[:, 512:1024], in_=self.bigf(O_V2 + 16 * 512, 512)), reads=self.bigres(O_V2 + 16 * 512, 512), writes=[self.r_KVsrc], dma=True, par=True)
        kb.add("pool", lambda e: e.collective_compute("AllGather", ALU.bypass, replica_groups=[[0, 1, 2, 3], [4, 5, 6, 7]], ins=[self.KVsrc], outs=[self.KVdst]),
               reads=[self.r_KVsrc], writes=[self.r_KVdst], dma=True, key="cc1", inc=1)
        def halo_select():
            G = self.bigf(O_GKV, 4096).rearrange("p (r c) -> p r c", c=1024)
            rG = self.bigres(O_GKV, 4096)
            kb.add("sp", lambda e: e.dma_start(out=G, in_=self.KVdst.rearrange("(r p) c -> p r c", p=128)), reads=[self.r_KVdst], writes=rG, dma=True, key="gkv")
            sel = lambda r: self.CORE[:, 512 + r:513 + r]
            for g in range(4):
                dK = self.bigf(O_KT + g * KTW, 128)
                rK = self.bigres(O_KT + g * KTW, 128)
                for r in range(4):
                    src = G[:, r, g * 128:(g + 1) * 128]
                    if r == 0:
                        kb.add("dve", lambda e, dK=dK, src=src, r=r: e.tensor_scalar(out=dK, in0=src, scalar1=sel(r), scalar2=0.0, op0=ALU.mult, op1=ALU.add), reads=rG + [self.r_CORE], writes=rK)
                    else:
                        kb.add("dve", lambda e, dK=dK, src=src, r=r: e.scalar_tensor_tensor(out=dK, in0=src, scalar=sel(r), in1=dK, op0=ALU.mult, op1=ALU.add), reads=rG + rK + [self.r_CORE], writes=rK)
            dV = self.bigf(O_V2, 512); rV = self.bigres(O_V2, 512)
            for r in range(4):
                src = G[:, r, 512:1024]
                if r == 0:
                    kb.add("dve", lambda e, src=src, r=r: e.tensor_scalar(out=dV, in0=src, scalar1=sel(r), scalar2=0.0, op0=ALU.mult, op1=ALU.add), reads=rG + [self.r_CORE], writes=rV)
                else:
                    kb.add("dve", lambda e, src=src, r=r: e.scalar_tensor_tensor(out=dV, in0=src, scalar=sel(r), in1=dV, op0=ALU.mult, op1=ALU.add), reads=rG + rV + [self.r_CORE], writes=rV)
        kb.add("dve", lambda e: e.tensor_reduce(out=self.SMALL[:, 232:236], in_=self.SINK[:].rearrange("p (g j) -> p g j", j=4), axis=AX.X, op=ALU.max), reads=[self.r_SINK], writes=[self.r_SMALL])
        smax = lambda g: self.SMALL[:, 232 + g:233 + g]
        units = [(g, t) for g in range(4) for t in range(1, 16)] + [(g, 0) for g in range(4)]
        first_halo_unit = 60

        ctx = {}
        T_BANK, O_BANK = 6, 7

        def s0(ui, g, t):
            if ui == first_halo_unit:
                halo_select()
            u = ui % 3
            bS = u * 2
            S = self.PS[:, bS * 512:(bS + 2) * 512]
            rS = [self.r_PS[bS], self.r_PS[bS + 1]]
            for j in range(4):
                blk, hf = 2 * g + j // 2, j % 2
                psl = slice(hf * 64, hf * 64 + 64)
                qlo = O_QR + blk * T + t * 128
                klo = O_KT + g * KTW + t * 128
                pj = (j % 2) * 2 + j // 2
                kb.add("pe", lambda e, pj=pj, psl=psl, qlo=qlo, klo=klo, S=S: e.matmul(S[:, pj * 256:(pj + 1) * 256], self.BIGF[psl, qlo:qlo + 128], self.BIGF[psl, klo:klo + 256], start=True, stop=True),
                       reads=self.bigres(qlo, 128) + self.bigres(klo, 256), writes=[rS[pj // 2]])
            st0 = S_ATT + u * 32
            c = dict(u=u, S=S, rS=rS, sml=(lambda a, st0=st0: self.SMALL[:, st0 + a * 4:st0 + a * 4 + 4]), rsml=self.r_att[u])
            ctx[ui] = c

        def s1(ui, g, t):
            c = ctx[ui]
            sm, rsm = self.next_stg()
            c["sm"], c["rsm"] = sm, rsm
            S, rS, sml, rsml = c["S"], c["rS"], c["sml"], c["rsml"]
            mask = self.CORE[:, 256:512] if t == 0 else self.CORE[:, 0:256]
            sm3 = sm.rearrange("p (j k) -> p j k", k=256)
            c["sm3"] = sm3
            S3 = S.rearrange("p (j k) -> p j k", k=256)
            kb.add("dve", lambda e: e.scalar_tensor_tensor(out=sm3, in0=S3, scalar=0.125, in1=mask.unsqueeze(1).to_broadcast([128, 4, 256]), op0=ALU.mult, op1=ALU.add),
                   reads=rS + [self.r_CORE], writes=[rsm])
            kb.add("dve", lambda e: e.tensor_reduce(out=sml(0)[:, 0:1], in_=sm, axis=AX.X, op=ALU.max), reads=[rsm], writes=[rsml])
            kb.add("dve", lambda e: e.tensor_scalar(out=sml(1)[:, 0:1], in0=sml(0)[:, 0:1], scalar1=smax(g), scalar2=-1.0, op0=ALU.max, op1=ALU.mult), reads=[rsml, self.r_SMALL], writes=[rsml])

        def s2(ui, g, t):
            c = ctx[ui]
            sm, rsm, sml, rsml, u = c["sm"], c["rsm"], c["sml"], c["rsml"], c["u"]
            kb.add("act", lambda e: e.activation(out=sm, in_=sm, func=AF.Exp, bias=sml(1)[:, 0:1]), reads=[rsm, rsml], writes=[rsm])
            kb.add("act", lambda e: e.activation(out=sml(3), in_=self.SINKP[:, 4 * g:4 * g + 4], func=AF.Exp, bias=sml(1)[:, 0:1]), reads=[rsml, self.r_SINKP], writes=[self.r_att2[u]])

        def s3(ui, g, t):
            c = ctx[ui]
            sm3, rsm, sml, rsml, u = c["sm3"], c["rsm"], c["sml"], c["rsml"], c["u"]
            kb.add("dve", lambda e: e.tensor_reduce(out=sml(2), in_=sm3, axis=AX.X, op=ALU.add), reads=[rsm], writes=[rsml])
            kb.add("dve", lambda e: e.tensor_tensor(out=sml(4), in0=sml(2), in1=sml(3), op=ALU.add), reads=[rsml, self.r_att2[u]], writes=[rsml])
            kb.add("dve", lambda e: e.reciprocal(out=sml(4), in_=sml(4)), reads=[rsml], writes=[rsml])
            pn = self.bigf(O_PN + u * 1024, 1024); rpn = self.bigres(O_PN + u * 1024, 1024)
            c["pn"], c["rpn"] = pn, rpn
            kb.add("pool", lambda e: e.tensor_tensor(out=pn.rearrange("p (j k) -> p j k", k=256), in0=sm3, in1=sml(4).unsqueeze(2).to_broadcast([128, 4, 256]), op=ALU.mult),
                   reads=[rsm, rsml], writes=rpn)

        pst = self.PS[:, T_BANK * 512:(T_BANK + 1) * 512].bitcast(BF16)
        pT = self.bigf(O_PT, 1024); rpT = self.bigres(O_PT, 1024)
        O = self.PS[:, O_BANK * 512:(O_BANK + 1) * 512]

        def s4(ui, g, t):
            c = ctx[ui]
            pn, rpn = c["pn"], c["rpn"]
            for jc in range(8):
                kb.add("pe", lambda e, jc=jc: e.transpose(out=pst[:, jc * 128:(jc + 1) * 128], in_=pn[:, jc * 128:(jc + 1) * 128], identity=self.IDB[:]),
                       reads=rpn + [self.r_cb], writes=[self.r_PS[T_BANK]])

        def s5(ui, g, t):
            kb.add("act", lambda e: e.activation(out=pT, in_=pst, func=AF.Copy), reads=[self.r_PS[T_BANK]], writes=rpT)

        def s6(ui, g, t):
            for j in range(4):
                for c_ in range(2):
                    vlo = O_V2 + (t + c_) * 512 + g * 128
                    kb.add("pe", lambda e, j=j, c_=c_, vlo=vlo: e.matmul(O[:, j * 128:(j + 1) * 128], self.bigf(vlo, 128), pT[:, (2 * j + c_) * 128:(2 * j + c_ + 1) * 128], start=(c_ == 0), stop=(c_ == 1)),
                           reads=self.bigres(vlo, 128) + rpT, writes=[self.r_PS[O_BANK]])

        def s7(ui, g, t):
            for hf in range(2):
                psl = slice(hf * 64, hf * 64 + 64)
                kb.add("act", lambda e, hf=hf, psl=psl: e.activation(out=self.HT[psl, 2 * g:2 * g + 2, t * 128:(t + 1) * 128], in_=O[psl, hf * 256:(hf + 1) * 256].rearrange("p (b q) -> p b q", q=128), func=AF.Copy),
                       reads=[self.r_PS[O_BANK]], writes=[self.r_HT[2 * g][t // 8], self.r_HT[2 * g + 1][t // 8]])
            ctx.pop(ui)

        stages = [s0, s1, s2, s3, s4, s5, s6, s7]
        nu = len(units)
        for i in range(nu + 7):
            for s in range(7, -1, -1):
                ui = i - s
                if 0 <= ui < nu:
                    stages[s](ui, *units[ui])

    def phase_gla_prep(self, l):
        kb = self.kb
        Fc, Fe, Fq, Fs = self.F[0], self.F[1], self.F[2], self.F[3]
        rc, re, rq, rs_ = self.Fres(0), self.Fres(1), self.Fres(2), self.Fres(3)
        Wg, rWg = self.W[0], self.r_W[0]
        Ww, rWw = self.W[1], self.r_W[1]
        glr = Wg[:].rearrange("p a b -> p (a b)")
        w2 = Ww[:].rearrange("p a b -> p (a b)")
        kb.add("pool", lambda e: e.dma_start(out=glr[0:16, :], in_=self.PROJ[P_GLR][0:16, :]), reads=[self.r_PROJ], writes=[rWg], dma=True)
        kb.add("pool", lambda e: e.dma_start(out=w2[0:16, 0:512], in_=self.w2_d[l]), writes=[rWw], dma=True)
        kb.add("dve", lambda e: e.memset(Fs[:, 1024:2048], 0.0), writes=[self.r_F[3][1]])
        dec = lambda h: self.SMALL[:, S_DEC + h * 32:S_DEC + (h + 1) * 32]
        for h in range(4):
            base = h * 5 * T
            QD, KD, KRT, VT = base, base + T, base + 2 * T, base + 3 * T
            for n in range(4):
                kb.add("pe", lambda e, n=n, h=h: e.matmul(self.PS[:, n * 512:(n + 1) * 512], w2[0:16, h * 128:(h + 1) * 128], glr[0:16, n * 512:(n + 1) * 512], start=True, stop=True),
                       reads=[rWg, rWw], writes=[self.r_PS[n]])
            gb = self.GBN[:, l * 4 + h:l * 4 + h + 1]
            kb.add("act", lambda e, gb=gb: e.activation(out=Fe[:], in_=self.PS[:, 0:2048], func=AF.Identity, bias=gb, scale=1.0), reads=[self.r_PS[i] for i in range(4)] + [self.r_GBN], writes=re)
            kb.add("act", lambda e: e.activation(out=Fe[:], in_=Fe[:], func=AF.Exp, scale=-1.0), reads=re, writes=re)
            kb.add("act", lambda e: e.activation(out=Fe[:], in_=Fe[:], func=AF.Ln, bias=1.0), reads=re, writes=re)
            for n in range(32):
                kb.add("dve", lambda e, n=n: e.tensor_tensor_scan(out=Fc[:, n * 64:(n + 1) * 64], data0=self.cst(self.C_ONES, 64), data1=Fe[:, n * 64:(n + 1) * 64], initial=0.0, op0=ALU.mult, op1=ALU.add),
                       reads=re + [self.r_CST], writes=[self.r_F[0][n // 16]])
            c3 = Fc[:].rearrange("p (n t) -> p n t", t=64)
            kb.add("act", lambda e, h=h: e.activation(out=dec(h), in_=c3[:, :, 63], func=AF.Exp, scale=-1.0 / 16), reads=rc, writes=[self.r_SMALL])
            kb.add("dve", lambda e, h=h: e.tensor_reduce(out=self.SMALL[:, S_DTOT + h:S_DTOT + h + 1], in_=c3[:, :, 63], axis=AX.X, op=ALU.add), reads=rc, writes=[self.r_SMALL])
            kb.add("act", lambda e, h=h: e.activation(out=self.SMALL[:, S_DTOT + h:S_DTOT + h + 1], in_=self.SMALL[:, S_DTOT + h:S_DTOT + h + 1], func=AF.Exp, scale=-1.0 / 16), reads=[self.r_SMALL], writes=[self.r_SMALL])
            kb.add("sp", lambda e, h=h: e.dma_start(out=Fq[:], in_=self.PROJ[P_QG + h]), reads=[self.r_PROJ], writes=rq, dma=True, key="glaQ")
            kb.add("act", lambda e: e.activation(out=Fe[:], in_=Fc[:], func=AF.Exp, scale=-1.0 / 16), reads=rc, writes=re)
            kb.add("dve", lambda e, QD=QD: e.scalar_tensor_tensor(out=self.bigf(QD, T), in0=Fq[:], scalar=float(128 ** -0.5), in1=Fe[:], op0=ALU.mult, op1=ALU.mult), reads=rq + re, writes=self.bigres(QD, T))
            kb.add("sp", lambda e, h=h: e.dma_start(out=Fq[:], in_=self.PROJ[P_KG + h]), reads=[self.r_PROJ], writes=rq, dma=True, key="glaQ")
            kb.add("act", lambda e: e.activation(out=Fe[:], in_=Fc[:], func=AF.Exp, scale=1.0 / 16), reads=rc, writes=re)
            kb.add("dve", lambda e, KD=KD: e.tensor_tensor(out=self.bigf(KD, T), in0=Fq[:], in1=Fe[:], op=ALU.mult), reads=rq + re, writes=self.bigres(KD, T))
            kb.add("dve", lambda e: e.tensor_tensor(out=c3, in0=c3, in1=c3[:, :, 63:64].to_broadcast([128, 32, 64]), op=ALU.subtract), reads=rc, writes=rc)
            kb.add("act", lambda e: e.activation(out=Fe[:], in_=Fc[:], func=AF.Exp, scale=1.0 / 16), reads=rc, writes=re)
            kb.add("dve", lambda e: e.tensor_tensor(out=self.bigf(O_TMP2, T), in0=Fq[:], in1=Fe[:], op=ALU.mult), reads=rq + re, writes=self.bigres(O_TMP2, T))
            for half in range(2):
                bank = 4 + half
                pst = self.PS[:, bank * 512:(bank + 1) * 512].bitcast(BF16)
                for tt in range(8):
                    t = half * 8 + tt
                    kb.add("pe", lambda e, t=t, tt=tt, pst=pst: e.transpose(out=pst[:, tt * 128:(tt + 1) * 128], in_=self.bigf(O_TMP2 + t * 128, 128), identity=self.IDB[:]),
                           reads=self.bigres(O_TMP2 + t * 128, 128) + [self.r_cb], writes=[self.r_PS[bank]])
                kb.add("act", lambda e, pst=pst, KRT=KRT, half=half: e.activation(out=self.bigf(KRT + half * 1024, 1024), in_=pst, func=AF.Copy), reads=[self.r_PS[bank]], writes=self.bigres(KRT + half * 1024, 1024))
            for e2 in range(2):
                kb.add("sp", lambda e, h=h, e2=e2: e.dma_start(out=Fq[:], in_=self.PROJ[P_VG + 2 * h + e2]), reads=[self.r_PROJ], writes=rq, dma=True, key="glaQ")
                kb.add("act", lambda e: e.activation(out=self.bigf(O_TMP2, T), in_=Fq[:], func=AF.Copy), reads=rq, writes=self.bigres(O_TMP2, T))
                for half in range(2):
                    bank = 6 + half
                    pst = self.PS[:, bank * 512:(bank + 1) * 512].bitcast(BF16)
                    for tt in range(8):
                        t = half * 8 + tt
                        kb.add("pe", lambda e, t=t, tt=tt, pst=pst: e.transpose(out=pst[:, tt * 128:(tt + 1) * 128], in_=self.bigf(O_TMP2 + t * 128, 128), identity=self.IDB[:]),
                               reads=self.bigres(O_TMP2 + t * 128, 128) + [self.r_cb], writes=[self.r_PS[bank]])
                    lo = VT + half * 8 * 256 + e2 * 128
                    dst = self.BIGF[:, lo:lo + 8 * 256].rearrange("p (t c) -> p t c", c=256)[:, :, 0:128]
                    kb.add("act", lambda e, dst=dst, pst=pst: e.activation(out=dst, in_=pst.rearrange("p (t c) -> p t c", c=128), func=AF.Copy),
                           reads=[self.r_PS[bank]], writes=self.bigres(VT + half * 8 * 256, 8 * 256))
        self.fence(self.r_Sm + [self.r_F[3][1]] + self.r_PS)
        for n in range(32):
            tile, par = n // 2, n % 2
            psl = slice(par * 64, par * 64 + 64)
            for h in range(4):
                base = h * 5 * T
                KRT, VT = base + 2 * T, base + 3 * T
                bank = 2 * h + par
                dS = self.PS[:, bank * 512:bank * 512 + 256]
                kb.add("pe", lambda e, tile=tile, psl=psl, dS=dS, KRT=KRT, VT=VT: e.matmul(dS, self.BIGF[psl, KRT + tile * 128:KRT + tile * 128 + 128], self.BIGF[psl, VT + tile * 256:VT + tile * 256 + 256], start=True, stop=True),
                       reads=self.bigres(KRT + tile * 128, 128) + self.bigres(VT + tile * 256, 256), writes=[self.r_PS[bank]])
            for h in range(4):
                bank = 2 * h + par
                dS = self.PS[:, bank * 512:bank * 512 + 256]
                Sm = Fs[:, 1024 + h * 256:1024 + (h + 1) * 256]
                kb.add("dve", lambda e, Sm=Sm, dS=dS, n=n, h=h: e.scalar_tensor_tensor(out=Sm, in0=Sm, scalar=dec(h)[:, n:n + 1], in1=dS, op0=ALU.mult, op1=ALU.add),
                       reads=[self.r_PS[bank], self.r_SMALL, self.r_Sm[h]], writes=[self.r_Sm[h]])
        self.fence(self.r_Sm + [self.r_F[3][1]])
        kb.add("sp", lambda e: e.dma_start(out=self.Ssrc[:, 0:1024], in_=Fs[:, 1024:2048]), reads=[self.r_F[3][1]], writes=[self.r_Ssrc], dma=True)
        kb.add("sp", lambda e: e.dma_start(out=self.Ssrc[:, 1024:1028], in_=self.SMALL[:, S_DTOT:S_DTOT + 4]), reads=[self.r_SMALL], writes=[self.r_Ssrc], dma=True, par=True)
        kb.add("pool", lambda e: e.collective_compute("AllGather", ALU.bypass, replica_groups=[[0, 1, 2, 3], [4, 5, 6, 7]], ins=[self.Ssrc], outs=[self.Sdst]),
               reads=[self.r_Ssrc], writes=[self.r_Sdst], dma=True, key="cc2", inc=1)

    def phase_gla_out(self, l):
        kb = self.kb
        F0, F1, F2, F3 = self.F
        GA = F0[:].rearrange("p (r c) -> p r c", c=1024)
        GB2 = F1[:, 0:1024]
        GD = F1[:, 1024:1036]
        rG = self.Fres(0) + self.Fres(1)
        kb.add("sp", lambda e: e.dma_start(out=GA, in_=self.Sdst[0:256, 0:1024].rearrange("(r p) c -> p r c", p=128)), reads=[self.r_Sdst], writes=self.Fres(0), dma=True, key="gS0")
        kb.add("sp", lambda e: e.dma_start(out=GB2, in_=self.Sdst[256:384, 0:1024]), reads=[self.r_Sdst], writes=[self.r_F[1][0]], dma=True, key="gS1")
        kb.add("sp", lambda e: e.dma_start(out=GD.rearrange("p (r c) -> p r c", c=4), in_=self.Sdst[0:384, 1024:1028].rearrange("(r p) c -> p r c", p=128)), reads=[self.r_Sdst], writes=[self.r_F[1][1]], dma=True, key="gS2")
        Sin = F3[:, 1024:2048]
        rSin = [self.r_F[3][1]]
        Tt = F3[:, 0:1024]
        rTt = [self.r_F[3][0]]
        lt = lambda r: self.CORE[:, 516 + r:517 + r]
        kb.add("dve", lambda e: e.memset(Sin, 0.0), writes=rSin)
        for r in range(3):
            Sr = GA[:, r, :] if r < 2 else GB2
            for h in range(4):
                hs = slice(h * 256, (h + 1) * 256)
                kb.add("dve", lambda e, r=r, h=h, hs=hs, Sr=Sr: e.scalar_tensor_tensor(out=Tt[:, hs], in0=Sin[:, hs], scalar=GD[:, r * 4 + h:r * 4 + h + 1], in1=Sr[:, hs], op0=ALU.mult, op1=ALU.add),
                       reads=rG + rSin, writes=rTt)
            kb.add("dve", lambda e: e.tensor_tensor(out=Tt, in0=Tt, in1=Sin, op=ALU.subtract), reads=rTt + rSin, writes=rTt)
            kb.add("dve", lambda e, r=r: e.scalar_tensor_tensor(out=Sin, in0=Tt, scalar=lt(r), in1=Sin, op0=ALU.mult, op1=ALU.add), reads=rTt + rSin + [self.r_CORE], writes=rSin)
        dec = lambda h: self.SMALL[:, S_DEC + h * 32:S_DEC + (h + 1) * 32]
        gmask = self.cst(self.C_GMASK, 128)
        for bi in range(8):
            for hf in range(2):
                sl = slice(hf * 1024, (hf + 1) * 1024)
                R, rR = self.F[bi % 2][:, sl], [self.r_F[bi % 2][hf]]
                kb.add("sp", lambda e, R=R, bi=bi, sl=sl: e.dma_start(out=R, in_=self.PROJ[P_RG + bi][:, sl]), reads=[self.r_PROJ], writes=rR, dma=True)
                kb.add("act", lambda e, R=R, bi=bi, sl=sl: e.activation(out=self.HT[:, 8 + bi, sl], in_=R, func=AF.Silu), reads=rR, writes=[self.r_HT[8 + bi][hf]])
        import os
        GL2 = int(os.environ.get("GL2", "99"))
        if GL2 < 1:
            return
        O_SB2, O_AM2, O_SQ2 = 40960, 43008, 43520
        rr = lambda n: [kb.res(f"g2{n}{h}") for h in range(4)]
        r_bA, r_bB = rr("bA"), rr("bB")
        r_I, r_dSl, r_St, r_Op, r_dSh = r_bA, r_bA, r_bA, r_bB, r_bB
        r_rs, r_tmp = rr("rs"), rr("tmp")
        bankA = lambda h: self.PS[:, (2 * h) * 512:(2 * h + 1) * 512]
        bankB = lambda h: self.PS[:, (2 * h + 1) * 512:(2 * h + 2) * 512]
        I_ps = lambda h: bankA(h)[:, 0:128]
        dS_ps = lambda h, par: bankA(h)[:, 128:384] if par == 0 else bankB(h)[:, 256:512]
        r_dS = lambda h, par: r_dSl[h] if par == 0 else r_dSh[h]
        St_ps = lambda h: bankA(h)[:, 384:512]
        Op_ps = lambda h: bankB(h)[:, 0:256]
        Sm = lambda h: Sin[:, h * 256:(h + 1) * 256]
        Sb = lambda h: self.bigf(O_SB2 + h * 256, 256)
        rSb = lambda h: self.bigres(O_SB2 + h * 256, 256)
        AM = lambda h: self.bigf(O_AM2 + h * 128, 128)
        rAM = lambda h: self.bigres(O_AM2 + h * 128, 128)
        SQ = lambda h: self.bigf(O_SQ2 + h * 256, 256)
        rSQ = lambda h: self.bigres(O_SQ2 + h * 256, 256)
        rsp = lambda h: F2[:, h * 128:(h + 1) * 128]
        tmp = lambda h, e2: F2[:, 512 + (2 * h + e2) * 128:512 + (2 * h + e2 + 1) * 128]
        base = lambda h: h * 5 * T
        QD = lambda h: base(h); KD = lambda h: base(h) + T; KRT = lambda h: base(h) + 2 * T; VT = lambda h: base(h) + 3 * T
        def fence(rlist):
            kb.add("dve", lambda e: e.memset(self.SMALL[:, 255:256], 0.0), reads=rlist, writes=rlist)
        sub = r_bA + r_bB + r_rs + r_tmp + self.r_Sm
        fence(sub + self.r_PS + self.Fres(2) + rSin)
        for h in range(4):
            kb.add("act", lambda e, h=h: e.activation(out=Sb(h), in_=Sm(h), func=AF.Copy), reads=[self.r_Sm[h]], writes=rSb(h))
        for m in range(min(16, GL2 - 1)):
            tok = slice(m * 128, (m + 1) * 128)
            for h in range(4):
                kb.add("pe", lambda e, h=h, m=m: e.matmul(I_ps(h), self.bigf(KD(h) + m * 128, 128), self.bigf(QD(h) + m * 128, 128), start=True, stop=True),
                       reads=self.bigres(KD(h) + m * 128, 128) + self.bigres(QD(h) + m * 128, 128), writes=[r_I[h]])
            for h in range(4):
                kb.add("dve", lambda e, h=h: e.tensor_tensor(out=AM(h), in0=I_ps(h), in1=gmask, op=ALU.mult), reads=[r_I[h], self.r_CST], writes=rAM(h))
            for par in range(2):
                n = 2 * m + par
                psl = slice(par * 64, par * 64 + 64)
                for h in range(4):
                    for e2 in range(2):
                        oc = Op_ps(h)[:, e2 * 128 + par * 64:e2 * 128 + (par + 1) * 64]
                        kb.add("pe", lambda e, oc=oc, e2=e2, par=par, h=h, m=m: e.matmul(oc, self.bigf(VT(h) + m * 256 + e2 * 128, 128), AM(h)[:, par * 64:(par + 1) * 64], start=True, stop=False),
                               reads=self.bigres(VT(h) + m * 256, 256) + rAM(h), writes=[r_Op[h]])
                        kb.add("pe", lambda e, oc=oc, e2=e2, par=par, h=h, m=m: e.matmul(oc, Sb(h)[:, e2 * 128:(e2 + 1) * 128], self.bigf(QD(h) + m * 128 + par * 64, 64), start=False, stop=True),
                               reads=rSb(h) + self.bigres(QD(h) + m * 128, 128), writes=[r_Op[h]])
                for h in range(4):
                    kb.add("pe", lambda e, psl=psl, h=h, m=m, par=par: e.matmul(dS_ps(h, par), self.BIGF[psl, KRT(h) + m * 128:KRT(h) + m * 128 + 128], self.BIGF[psl, VT(h) + m * 256:VT(h) + m * 256 + 256], start=True, stop=True),
                           reads=self.bigres(KRT(h) + m * 128, 128) + self.bigres(VT(h) + m * 256, 256), writes=[r_dS(h, par)])
                for h in range(4):
                    kb.add("dve", lambda e, h=h, n=n, par=par: e.scalar_tensor_tensor(out=Sm(h), in0=Sm(h), scalar=dec(h)[:, n:n + 1], in1=dS_ps(h, par), op0=ALU.mult, op1=ALU.add),
                           reads=[r_dS(h, par), self.r_SMALL, self.r_Sm[h]], writes=[self.r_Sm[h]])
                for h in range(4):
                    kb.add("act", lambda e, h=h: e.activation(out=Sb(h), in_=Sm(h), func=AF.Copy), reads=[self.r_Sm[h]], writes=rSb(h))
            for h in range(4):
                kb.add("act", lambda e, h=h: e.activation(out=SQ(h), in_=Op_ps(h), func=AF.Square), reads=[r_Op[h]], writes=rSQ(h))
            for h in range(4):
                for e2 in range(2):
                    kb.add("pe", lambda e, h=h, e2=e2: e.matmul(St_ps(h), self.ONESB[:], SQ(h)[:, e2 * 128:(e2 + 1) * 128], start=(e2 == 0), stop=(e2 == 1)), reads=rSQ(h) + [self.r_cb], writes=[r_St[h]])
            for h in range(4):
                kb.add("dve", lambda e, h=h: e.tensor_scalar(out=rsp(h), in0=St_ps(h), scalar1=1.0 / 256, scalar2=EPS, op0=ALU.mult, op1=ALU.add), reads=[r_St[h]], writes=[r_rs[h]])
            for h in range(4):
                kb.add("act", lambda e, h=h: e.activation(out=rsp(h), in_=rsp(h), func=AF.Sqrt), reads=[r_rs[h]], writes=[r_rs[h]])
            for h in range(4):
                kb.add("dve", lambda e, h=h: e.reciprocal(out=rsp(h), in_=rsp(h)), reads=[r_rs[h]], writes=[r_rs[h]])
            for h in range(4):
                for e2 in range(2):
                    gn = self.GBN[:, 4 * self.depth + l * 2 + e2:4 * self.depth + l * 2 + e2 + 1]
                    kb.add("dve", lambda e, h=h, e2=e2, gn=gn: e.scalar_tensor_tensor(out=tmp(h, e2), in0=Op_ps(h)[:, e2 * 128:(e2 + 1) * 128], scalar=gn, in1=rsp(h), op0=ALU.mult, op1=ALU.mult),
                           reads=[r_Op[h], r_rs[h], self.r_GBN], writes=[r_tmp[h]])
            for h in range(4):
                for e2 in range(2):
                    blk = 8 + 2 * h + e2
                    kb.add("pool", lambda e, h=h, e2=e2, blk=blk, tok=tok: e.tensor_tensor(out=self.HT[:, blk, tok], in0=tmp(h, e2), in1=self.HT[:, blk, tok], op=ALU.mult),
                           reads=[r_tmp[h], self.r_HT[blk][m // 8]], writes=[self.r_HT[blk][m // 8]])
        fence(sub + self.r_PS + self.Fres(2) + rSin)

    def mixer(self, l, x_src, r_xs, x_dst, r_xd, nxt):
        self.phase_inproj(l)
        self.phase_attn(l)
        self.phase_gla_prep(l)
        self.phase_gla_out(l)
        ht = lambda k, h: (self.HT[:, k, :], self.r_HT[k][h])
        self.phase_C(self.w_mo[l], NB, ht, None, True, False, self.OUT, self.r_OUT, 1.0)
        self.phase_E(l, 3, x_src, r_xs, x_dst, r_xd, nxt)


D = 2048
T = 2048
NB = 16
DFF = 5504
NFB = 43
FA = 22
EPS = 1e-6
MIXW = 4624


class Prog(MixerMixin):
    def __init__(self, depth=4, dbg=(), with_ffn=True):
        self.dbg = dbg
        DEPTH = depth
        self.depth = depth
        nc = bass.Bass("TRN2", target_bir_lowering=False)
        self.nc = nc
        self.kb = KB(nc)
        kb = self.kb

        def din(name, shape, dt=F32):
            return nc.dram_tensor(name, list(shape), dt, kind="ExternalInput").ap()

        def dscr(name, shape, dt=F32):
            return nc.dram_tensor(name, list(shape), dt).ap()

        self.xT = din("xT", [NB, 128, T])
        self.yT = nc.dram_tensor("yT", [NB, 128, T], F32, kind="ExternalOutput").ap()
        self.pos = din("pos", [1, T], I32)
        self.gcol_d = din("gcol", [128, DEPTH * 6 * NB])
        if with_ffn:
            self.w_in = din("ffn_w_in", [DEPTH, 2, D, 2 * DFF])
            self.w_out = din("ffn_w_out", [DEPTH, 2, DFF, D])
        self.w_mi = din("w_mix_in", [DEPTH, D, MIXW])
        self.w_mo = din("w_mix_out", [DEPTH, D, D])
        self.sinks_d = din("attn_sinks", [DEPTH, 16])
        self.w2_d = din("gla_gate_w2", [DEPTH, 16, 512])
        self.gb_d = din("gb_col", [128, DEPTH * 4])
        self.gn_d = din("gn_col", [128, DEPTH * 2])
        self.cst_d = din("cst", [128, 512])
        self.ident_d = din("ident", [128, 128])
        self.core_d = din("corec", [128, 520])
        self.XA = dscr("XA", [NB, 128, T])
        self.XB = dscr("XB", [NB, 128, T])
        self.PART = dscr("PART", [NB, 128, T])
        self.OUT = dscr("OUT", [NB, 128, T])
        self.r_XA, self.r_XB, self.r_PART, self.r_OUT = [kb.res(n) for n in ("XA", "XB", "PART", "OUT")]
        self.r_xin = kb.res("xin")
        self.r_y = kb.res("yT")
        self.HT = kb.sbuf("HT", [128, NB, T], BF16)
        self.r_HT = [[kb.res(f"HT{j}_{h}") for h in range(2)] for j in range(NB)]
        self.BIGF = kb.sbuf("BIG", [128, FA * T], BF16)
        self.BIG = self.BIGF[:].rearrange("p (a b) -> p a b", b=T)
        self.r_BIG = [[kb.res(f"BIG{j}_{h}") for h in range(2)] for j in range(FA)]
        self.W = [kb.sbuf(f"W{i}", [128, NB, 128], BF16) for i in range(4)]
        self.r_W = [kb.res(f"W{i}") for i in range(4)]
        self.F = [kb.sbuf(f"F{i}", [128, T], F32) for i in range(4)]
        self.r_F = [[kb.res(f"F{i}_{h}") for h in range(2)] for i in range(4)]
        self.CST = kb.sbuf("CST", [128, 512], F32)
        self.r_CST = kb.res("CST")
        self.GCOL = kb.sbuf("GCOL", [128, DEPTH * 6 * NB], F32)
        self.r_GCOL = kb.res("GCOL")
        self.ONESB = kb.sbuf("ONESB", [128, 128], BF16)
        self.IDB = kb.sbuf("IDB", [128, 128], BF16)
        self.r_cb = kb.res("cb")
        self.PS = kb.psum("PS", [128, 8 * 512], F32)
        self.r_PS = [kb.res(f"PS{i}") for i in range(8)]
        self.wslot = 0
        self.stg = 0
        self.load_consts()
        self.mixer_init()

    C_ONES = 0
    C_PERM = 128
    C_GMASK = 256
    C_INVF = 384
    C_SIGN = 385
    def cst(self, c0, n):
        return self.CST[:, c0:c0 + n]

    def load_consts(self):
        kb = self.kb
        kb.add("sp", lambda e: e.dma_start(out=self.CST[:], in_=self.cst_d), writes=[self.r_CST], dma=True)
        kb.add("sp", lambda e: e.dma_start(out=self.GCOL[:], in_=self.gcol_d), writes=[self.r_GCOL], dma=True)
        kb.add("act", lambda e: e.activation(out=self.ONESB[:], in_=self.cst(self.C_ONES, 128), func=AF.Copy), reads=[self.r_CST], writes=[self.r_cb])
        kb.add("pool", lambda e: e.dma_start(out=self.IDB[:], in_=self.ident_d), writes=[self.r_cb], dma=True, key="identld")

    def gcol(self, l, i, j):
        c = (l * 6 + i) * NB + j
        return self.GCOL[:, c:c + 1]

    def stats_to_rstd(self, fi, c, from_psum=False):
        kb = self.kb
        Fb = self.F[fi]
        rF = self.r_F[fi]
        for n in range(0 if from_psum else 4):
            h = n // 2
            kb.add("pe", lambda e, n=n: e.matmul(self.PS[:, n * 512:(n + 1) * 512], self.cst(self.C_ONES, 128), Fb[:, n * 512:(n + 1) * 512], start=True, stop=True),
                   reads=[rF[h], self.r_CST], writes=[self.r_PS[n]])
        for h in range(2):
            sl = slice(h * 1024, (h + 1) * 1024)
            kb.add("dve", lambda e, sl=sl: e.tensor_scalar(out=Fb[:, sl], in0=self.PS[:, sl], scalar1=1.0 / (D * c * c), scalar2=EPS / (c * c), op0=ALU.mult, op1=ALU.add),
                   reads=[self.r_PS[2 * h], self.r_PS[2 * h + 1]], writes=[rF[h]])
            kb.add("act", lambda e, sl=sl: e.activation(out=Fb[:, sl], in_=Fb[:, sl], func=AF.Sqrt), reads=[rF[h]], writes=[rF[h]])
            kb.add("dve", lambda e, sl=sl: e.reciprocal(out=Fb[:, sl], in_=Fb[:, sl]), reads=[rF[h]], writes=[rF[h]])

    def next_stg(self):
        s = self.stg
        self.stg = (s + 1) % 4
        fi, h = 2 + s // 2, s % 2
        return self.F[fi][:, h * 1024:(h + 1) * 1024], self.r_F[fi][h]

    def phase_prenorm_from(self, x_ap, r_x, l, gi):
        kb = self.kb
        for j in range(NB):
            for h in range(2):
                sl = slice(h * 1024, (h + 1) * 1024)
                st, rst = self.next_stg()
                kb.add("sp", lambda e, st=st, j=j, sl=sl: e.dma_start(out=st, in_=x_ap[j][:, sl]), reads=[r_x], writes=[rst], dma=True)
                kb.add("act", lambda e, st=st, j=j, sl=sl: e.activation(out=self.HT[:, j, sl], in_=st, func=AF.Copy, scale=self.gcol(l, gi, j)),
                       reads=[rst, self.r_GCOL], writes=[self.r_HT[j][h]])
                kb.add("act", lambda e, st=st: e.activation(out=st, in_=st, func=AF.Square), reads=[rst], writes=[rst])
                for n in range(2):
                    bank = 2 * h + n
                    kb.add("pe", lambda e, st=st, n=n, bank=bank, j=j: e.matmul(self.PS[:, bank * 512:(bank + 1) * 512], self.cst(self.C_ONES, 128), st[:, n * 512:(n + 1) * 512], start=(j == 0), stop=(j == NB - 1)),
                           reads=[rst, self.r_CST], writes=[self.r_PS[bank]])
        self.stats_to_rstd(0, 1.0, from_psum=True)

    def load_w(self, src_ap, nk):
        kb = self.kb
        s = self.wslot
        self.wslot = (s + 1) % 4
        Wt = self.W[s]
        kb.add("pool", lambda e: e.dma_start(out=Wt[:, 0:nk, :], in_=src_ap.rearrange("(kc p) m -> p kc m", p=128)), writes=[self.r_W[s]], dma=True)
        return Wt, self.r_W[s]

    def phase_ffn_B(self, l, i, f0, nf):
        kb = self.kb
        rstd = self.F[0]
        unit = getattr(self, "_bunit", 0)
        def ldB(b):
            fb = f0 + b
            return (self.load_w(self.w_in[l, i][:, fb * 128:(fb + 1) * 128], NB),
                    self.load_w(self.w_in[l, i][:, DFF + fb * 128:DFF + (fb + 1) * 128], NB))
        nxtw = ldB(0)
        for b in range(nf):
            (Wg, rWg), (Wu, rWu) = nxtw
            if b + 1 < nf:
                nxtw = ldB(b + 1)
            for h in range(2):
                base = (unit % 2) * 4
                unit += 1
                for (Wt, rW, boff) in ((Wg, rWg, 0), (Wu, rWu, 2)):
                    for n in range(2):
                        bank = base + boff + n
                        tsl = slice(h * 1024 + n * 512, h * 1024 + (n + 1) * 512)
                        for k in range(NB):
                            kb.add("pe", lambda e, Wt=Wt, k=k, tsl=tsl, bank=bank: e.matmul(self.PS[:, bank * 512:(bank + 1) * 512], Wt[:, k, :], self.HT[:, k, tsl], start=(k == 0), stop=(k == NB - 1)),
                                   reads=[rW, self.r_HT[k][h]], writes=[self.r_PS[bank]])
                sl = slice(h * 1024, (h + 1) * 1024)
                g_ps = self.PS[:, base * 512:(base + 2) * 512]
                u_ps = self.PS[:, (base + 2) * 512:(base + 4) * 512]
                s1, r1 = self.next_stg()
                s2, r2 = self.next_stg()
                kb.add("dve", lambda e, s1=s1, g_ps=g_ps, sl=sl: e.tensor_tensor(out=s1, in0=g_ps, in1=rstd[:, sl], op=ALU.mult),
                       reads=[self.r_PS[base], self.r_PS[base + 1], self.r_F[0][h]], writes=[r1])
                kb.add("act", lambda e, s1=s1: e.activation(out=s1, in_=s1, func=AF.Silu), reads=[r1], writes=[r1])
                kb.add("dve", lambda e, s2=s2, u_ps=u_ps, sl=sl: e.tensor_tensor(out=s2, in0=u_ps, in1=rstd[:, sl], op=ALU.mult),
                       reads=[self.r_PS[base + 2], self.r_PS[base + 3], self.r_F[0][h]], writes=[r2])
                kb.add("dve", lambda e, s1=s1, s2=s2, b=b, sl=sl: e.tensor_tensor(out=self.BIG[:, b, sl], in0=s1, in1=s2, op=ALU.mult),
                       reads=[r1, r2], writes=[self.r_BIG[b][h]])
        self._bunit = unit

    def phase_C(self, w_ap, nk, xsrc_blocks, r_src, last, part_in, out_ap, r_out, post_c, resident=False):
        kb = self.kb
        acc = self.F[1]
        if last:
            kb.add("dve", lambda e: e.memset(acc[:], 0.0), writes=self.r_F[1])
        unit = getattr(self, "_cunit", 0)
        def ldC(j):
            a = self.load_w(w_ap[0:min(nk, NB) * 128, j * 128:(j + 1) * 128], min(nk, NB))
            b_ = self.load_w(w_ap[NB * 128:nk * 128, j * 128:(j + 1) * 128], nk - NB) if nk > NB else (None, None)
            return a, b_
        nxtw = ldC(0)
        for j in range(NB):
            (Wa, rWa), (Wb, rWb) = nxtw
            if j + 1 < NB:
                nxtw = ldC(j + 1)
            for h in range(2):
                base = (unit % 4) * 2
                unit += 1
                sl = slice(h * 1024, (h + 1) * 1024)
                res_here = bool(last and resident and j < 8)
                if res_here:
                    st, rstl = self.HT[:, 2 * j + h, :].bitcast(F32), list(self.r_HT[2 * j + h])
                else:
                    st, rst = self.next_stg()
                    rstl = [rst]
                if last and part_in:
                    kb.add("sp", lambda e, st=st, j=j, sl=sl: e.dma_start(out=st, in_=self.PART[j][:, sl]), reads=[self.r_PART], writes=rstl, dma=True)
                for n in range(2):
                    bank = base + n
                    tsl = slice(h * 1024 + n * 512, h * 1024 + (n + 1) * 512)
                    for k in range(nk):
                        Wt, rW, kk = (Wa, rWa, k) if k < NB else (Wb, rWb, k - NB)
                        src, rs = xsrc_blocks(k, h)
                        kb.add("pe", lambda e, Wt=Wt, kk=kk, src=src, tsl=tsl, bank=bank, k=k: e.matmul(self.PS[:, bank * 512:(bank + 1) * 512], Wt[:, kk, :], src[:, tsl], start=(k == 0), stop=(k == nk - 1)),
                               reads=[rW, rs], writes=[self.r_PS[bank]])
                ps = self.PS[:, base * 512:(base + 2) * 512]
                rps = [self.r_PS[base], self.r_PS[base + 1]]
                if not last:
                    kb.add("act", lambda e, st=st, ps=ps: e.activation(out=st, in_=ps, func=AF.Copy), reads=rps, writes=rstl)
                    kb.add("sp", lambda e, st=st, j=j, sl=sl: e.dma_start(out=self.PART[j][:, sl], in_=st), reads=rstl, writes=[self.r_PART], dma=True, par=True)
                else:
                    if part_in:
                        kb.add("dve", lambda e, st=st, ps=ps: e.tensor_tensor(out=st, in0=ps, in1=st, op=ALU.add), reads=rps + rstl, writes=rstl)
                    else:
                        kb.add("act", lambda e, st=st, ps=ps: e.activation(out=st, in_=ps, func=AF.Copy), reads=rps, writes=rstl)
                    if not res_here:
                        kb.add("sp", lambda e, st=st, j=j, sl=sl: e.dma_start(out=out_ap[j][:, sl], in_=st), reads=rstl, writes=[r_out], dma=True, par=True)
                    s2, r2 = self.next_stg()
                    kb.add("act", lambda e, st=st, s2=s2: e.activation(out=s2, in_=st, func=AF.Square), reads=rstl, writes=[r2])
                    kb.add("dve", lambda e, s2=s2, sl=sl: e.tensor_tensor(out=acc[:, sl], in0=acc[:, sl], in1=s2, op=ALU.add), reads=[r2, self.r_F[1][h]], writes=[self.r_F[1][h]])
        self._cunit = unit
        if last:
            self.stats_to_rstd(1, post_c)

    def phase_E(self, l, gi_post, x_src, r_xs, x_dst, r_xd, nxt, resident=False):
        kb = self.kb
        rstd = self.F[1]
        acc = self.F[0]
        units = [(j, h) for j in range(NB) for h in range(2)]
        sbuf_ = lambda i: (self.BIGF[:, i * T:(i + 1) * T].bitcast(F32), self.r_BIG[i])
        state = {"n": 0}

        def loads(j, h):
            sl = slice(h * 1024, (h + 1) * 1024)
            i = state["n"]
            state["n"] = (i + 2) % 16
            s1, r1 = sbuf_(i)
            s2, r2 = sbuf_(i + 1)
            sq, rq_ = s1, r1
            if resident and j < 8:
                s1, r1 = self.HT[:, 2 * j + h, :].bitcast(F32), list(self.r_HT[2 * j + h])
            else:
                kb.add("sp", lambda e, s1=s1, j=j, sl=sl: e.dma_start(out=s1, in_=self.OUT[j][:, sl]), reads=[self.r_OUT], writes=r1, dma=True, key=f"dma_Estg{i}")
            kb.add("sp", lambda e, s2=s2, j=j, sl=sl: e.dma_start(out=s2, in_=x_src[j][:, sl]), reads=[r_xs], writes=r2, dma=True, key=f"dma_Estg{i + 1}")
            return s1, r1, s2, r2, sq, rq_
        DEPTH_PF = 6
        pre = [loads(*units[i]) for i in range(DEPTH_PF)]
        for ui, (j, h) in enumerate(units):
            sl = slice(h * 1024, (h + 1) * 1024)
            s1, r1, s2, r2, sq, rq_ = pre.pop(0)
            if ui + DEPTH_PF < len(units):
                pre.append(loads(*units[ui + DEPTH_PF]))
            kb.add("pool", lambda e, s1=s1, sl=sl: e.tensor_tensor(out=s1, in0=s1, in1=rstd[:, sl], op=ALU.mult), reads=r1 + [self.r_F[1][h]], writes=r1)
            kb.add("dve", lambda e, s1=s1, s2=s2, j=j: e.scalar_tensor_tensor(out=s2, in0=s1, scalar=self.gcol(l, gi_post, j), in1=s2, op0=ALU.mult, op1=ALU.add),
                   reads=r1 + r2 + [self.r_GCOL], writes=r2)
            kb.add("sp", lambda e, s2=s2, j=j, sl=sl: e.dma_start(out=x_dst[j][:, sl], in_=s2), reads=r2, writes=[r_xd], dma=True, par=True)
            if nxt is not None:
                nl, ng = nxt
                kb.add("act", lambda e, s2=s2, j=j, sl=sl, nl=nl, ng=ng: e.activation(out=self.HT[:, j, sl], in_=s2, func=AF.Copy, scale=self.gcol(nl, ng, j)),
                       reads=r2 + [self.r_GCOL], writes=[self.r_HT[j][h]])
                kb.add("act", lambda e, sq=sq, s2=s2: e.activation(out=sq, in_=s2, func=AF.Square), reads=r2, writes=rq_)
                for n in range(2):
                    bank = 2 * h + n
                    kb.add("pe", lambda e, sq=sq, n=n, bank=bank, j=j: e.matmul(self.PS[:, bank * 512:(bank + 1) * 512], self.cst(self.C_ONES, 128), sq[:, n * 512:(n + 1) * 512], start=(j == 0), stop=(j == NB - 1)),
                           reads=rq_ + [self.r_CST], writes=[self.r_PS[bank]])
        if nxt is not None:
            self.stats_to_rstd(0, 1.0, from_psum=True)

    def ffn(self, l, i, x_src, r_xs, x_dst, r_xd, nxt):
        big = lambda k, h: (self.BIG[:, k, :], self.r_BIG[k][h])
        self.phase_ffn_B(l, i, 0, FA)
        self.phase_C(self.w_out[l, i][0:FA * 128, :], FA, big, None, False, False, None, None, None)
        self.phase_ffn_B(l, i, FA, NFB - FA)
        self.phase_C(self.w_out[l, i][FA * 128:DFF, :], NFB - FA, big, None, True, True, self.OUT, self.r_OUT, 0.5, resident=True)
        self.phase_E(l, 1 + 4 * i, x_src, r_xs, x_dst, r_xd, nxt, resident=True)

    def finish(self):
        kb = self.kb
        kb.add("sp", lambda e: e.nop(), reads=[self.r_y])
        kb.emit()
        kb.close()
        return self.nc


def make_consts():
    c = np.zeros((128, 512), np.float32)
    c[:, 0:128] = 1.0
    P = np.zeros((128, 128), np.float32)
    for m in range(128):
        k = (m // 64) * 64 + ((m % 64) + 32) % 64
        P[k, m] = 1.0
    c[:, 128:256] = P
    s = np.arange(128)[:, None]; t = np.arange(128)[None, :]
    c[:, 256:384] = ((s // 64 == t // 64) & (s <= t)).astype(np.float32)
    inv_freq = (10000.0 ** (-np.arange(0, 64, 2, dtype=np.float32) / 64)).astype(np.float32)
    d = np.arange(128) % 64
    c[:, 384] = inv_freq[d % 32]
    c[:, 385] = np.where(d < 32, -1.0, 1.0)
    return c

def make_core_consts(rank):
    c = np.zeros((128, 520), np.float32)
    qi = np.arange(128)[:, None]; kj = np.arange(256)[None, :]
    diff = qi + 128 - kj
    band = (diff >= 0) & (diff < 128)
    c[:, 0:256] = np.where(band, 0.0, -30000.0)
    first = band & (kj >= 128) if rank == 0 else band
    c[:, 256:512] = np.where(first, 0.0, -30000.0)
    if rank > 0:
        c[:, 512 + rank - 1] = 1.0
    for r in range(4):
        if r < rank:
            c[:, 516 + r] = 1.0
    return c


def build_full(depth=4):
    p = Prog(depth=depth)
    p.phase_rope_tables()
    p.phase_prenorm_from(p.xT, p.r_xin, 0, 0)
    locs = [(p.XA, p.r_XA), (p.XB, p.r_XB)]
    src = (p.xT, p.r_xin)
    k = 0
    nsub = 3 * depth
    for l in range(depth):
        for sub in range(3):
            dst = (p.yT, p.r_y) if k == nsub - 1 else locs[k % 2]
            if sub == 0:
                p.ffn(l, 0, src[0], src[1], dst[0], dst[1], (l, 2))
            elif sub == 1:
                p.mixer(l, src[0], src[1], dst[0], dst[1], (l, 4))
            else:
                p.ffn(l, 1, src[0], src[1], dst[0], dst[1], (l + 1, 0) if l + 1 < depth else None)
            src = dst
            k += 1
    return p.finish()


def kernel(x, positions, norm_gains, ffn_w_in, ffn_w_out, w_mix_in, attn_sinks, gla_gate_w2, gla_gate_b, gla_norm_gain, w_mix_out):
    x = np.asarray(x, dtype=np.float32)
    positions = np.asarray(positions, dtype=np.int32)
    depth = int(np.asarray(norm_gains).shape[0])
    B, S, _ = x.shape
    f32 = lambda a: np.ascontiguousarray(np.asarray(a, dtype=np.float32))
    shared = dict(
        gcol=np.ascontiguousarray(f32(norm_gains).reshape(depth * 6 * NB, 128).T),
        ffn_w_in=f32(ffn_w_in), ffn_w_out=f32(ffn_w_out), w_mix_in=f32(w_mix_in), w_mix_out=f32(w_mix_out),
        attn_sinks=f32(attn_sinks), gla_gate_w2=f32(gla_gate_w2),
        gb_col=np.ascontiguousarray(f32(gla_gate_b).reshape(depth * 4, 128).T),
        gn_col=np.ascontiguousarray(f32(gla_norm_gain).reshape(depth * 2, 128).T),
        cst=make_consts(), ident=np.eye(128, dtype=np.float32))
    ins = []
    for core in range(8):
        b, r = core // 4, core % 4
        d = dict(shared)
        d["xT"] = np.ascontiguousarray(x[b, r * T:(r + 1) * T].T).reshape(NB, 128, T)
        d["pos"] = np.ascontiguousarray(positions[b, r * T:(r + 1) * T][None])
        d["corec"] = make_core_consts(r)
        ins.append(d)
    nc = build_full(depth)
    res = run_bass_kernel_spmd(nc, ins, core_ids=list(range(8)))
    y = np.empty((B, S, D), np.float32)
    for core in range(8):
        b, r = core // 4, core % 4
        y[b, r * T:(r + 1) * T] = res.results[core]["yT"].reshape(D, T).T
    return y
```
